# Optimizing a Trainium2 kernel written in Bass

```python
import math
import jax, jax.numpy as jnp
from jax import lax
import numpy as np

D_MODEL = 1024
BATCH = 8
SEQ = 2048
DEPTH = 2
DEC_BATCH = 128
DEC_SEQ = 8
PAST_LEN = 16384
PAGE_SIZE = 128

MIX_W = D_MODEL // 2
N_BRANCH = 4
RET_HEADS = 4
RET_DK = MIX_W // RET_HEADS
RET_DV = MIX_W // RET_HEADS
RET_CHUNK = 64
ROPE_BASE = 10000.0
SSD_HEADDIM = 64
SSD_HEADS = MIX_W // SSD_HEADDIM
SSD_GROUPS = 2
SSD_HPG = SSD_HEADS // SSD_GROUPS
SSD_STATE = 128
SSD_CONV = 4
SSD_CHUNK = 64
SSD_CONV_DIM = MIX_W + 2 * SSD_GROUPS * SSD_STATE
HG_HEADS = 4
HG_DK = MIX_W // HG_HEADS
HG_DV = MIX_W // HG_HEADS
HG_CHUNK = 64
S5_GROUP = 16
S5_GROUPS = MIX_W // S5_GROUP
S5_STATE = 64
D_FF = 4 * D_MODEL
GATE_COLS = N_BRANCH * D_MODEL
IN_COLS = 4 * MIX_W + (MIX_W + SSD_CONV_DIM + SSD_HEADS) + 4 * MIX_W + MIX_W + GATE_COLS
EPS = 1e-6

kernel_name = 'hybrid_ret_ssd_hgrn2_s5_gated_step'


def _rmsnorm(x, g):
    xf = x.astype(jnp.float32)
    y = xf * lax.rsqrt(jnp.mean(xf * xf, axis=-1, keepdims=True) + EPS)
    return (y * g.astype(jnp.float32)).astype(x.dtype)


def _head_groupnorm(o, g):
    mu = jnp.mean(o, axis=-1, keepdims=True)
    var = jnp.mean(jnp.square(o - mu), axis=-1, keepdims=True)
    return (o - mu) * lax.rsqrt(var + EPS) * g.astype(jnp.float32)


def _chunk_len(L, c):
    return c if L % c == 0 else L


def _to_chunks(a, c):
    B, L = a.shape[:2]
    return jnp.moveaxis(a.reshape((B, L // c, c) + a.shape[2:]), 1, 0)


def _from_chunks(a):
    a = jnp.moveaxis(a, 0, 1)
    return a.reshape((a.shape[0], a.shape[1] * a.shape[2]) + a.shape[3:])


def _masked_exp(mask, seg):
    return jnp.where(mask, jnp.exp(jnp.where(mask, seg, 0.0)), 0.0)


def _rope(x, pos):
    half = x.shape[-1] // 2
    inv = ROPE_BASE ** (-jnp.arange(half, dtype=jnp.float32) / half)
    ang = pos.astype(jnp.float32)[:, None] * inv[None, :]
    cos = jnp.cos(ang)[None, :, None, :]
    sin = jnp.sin(ang)[None, :, None, :]
    x1, x2 = x[..., :half], x[..., half:]
    return jnp.concatenate([x1 * cos - x2 * sin, x1 * sin + x2 * cos], axis=-1)


def _retention(q, k, v, pos, s0):
    L = q.shape[1]
    c = _chunk_len(L, RET_CHUNK)
    q = _rope(q, pos)
    k = _rope(k, pos) * (RET_DK ** -0.5)
    log_g = jnp.log(1.0 - 2.0 ** (-5.0 - jnp.arange(RET_HEADS, dtype=jnp.float32)))
    idx = jnp.arange(c, dtype=jnp.float32)
    diff = idx[:, None] - idx[None, :]
    dmat = jnp.where(diff >= 0, jnp.exp(log_g[:, None, None] * jnp.maximum(diff, 0.0)), 0.0)
    q_dec = jnp.exp(log_g[None, :] * (idx[:, None] + 1.0))
    k_dec = jnp.exp(log_g[None, :] * (c - 1.0 - idx[:, None]))
    chunk_dec = jnp.exp(log_g * c)

    def step(S, blk):
        qc, kc, vc = blk
        att = jnp.einsum('bihd,bjhd->bhij', qc, kc) * dmat
        o = jnp.einsum('bhij,bjhe->bihe', att, vc)
        o = o + jnp.einsum('bihd,bhde->bihe', qc * q_dec[None, :, :, None], S)
        S = S * chunk_dec[None, :, None, None] + jnp.einsum('bjhd,bjhe->bhde', kc * k_dec[None, :, :, None], vc)
        return S, o

    S, o = lax.scan(step, s0, (_to_chunks(q, c), _to_chunks(k, c), _to_chunks(v, c)))
    return _from_chunks(o), S


def _causal_conv(xbc, buf, w, b):
    L = xbc.shape[1]
    full = jnp.concatenate([buf, xbc], axis=1)
    w = w.astype(jnp.float32)
    out = b.astype(jnp.float32) + full[:, 0:L] * w[0]
    for j in range(1, SSD_CONV):
        out = out + full[:, j:j + L] * w[j]
    return jax.nn.silu(out), full[:, -(SSD_CONV - 1):]


def _ssd(x, dt_raw, bm, cm, s0, a_log, dt_bias, d_skip):
    L = x.shape[1]
    c = _chunk_len(L, SSD_CHUNK)
    A = -jnp.exp(a_log.astype(jnp.float32))
    dt = jax.nn.softplus(dt_raw + dt_bias.astype(jnp.float32))
    dA = dt * A
    xdt = x * dt[..., None]
    mask = jnp.tril(jnp.ones((c, c), dtype=bool))

    def step(S, blk):
        xdtc, dAc, bc, cc = blk
        cum = jnp.cumsum(dAc, axis=1)
        cum_t = jnp.moveaxis(cum, 1, -1)
        seg = cum_t[..., :, None] - cum_t[..., None, :]
        lmat = _masked_exp(mask, seg)
        cb = jnp.einsum('bign,bjgn->bgij', cc, bc)
        y = jnp.einsum('bgrij,bjgrp->bigrp', cb[:, :, None] * lmat, xdtc)
        y = y + jnp.einsum('bign,bgrpn->bigrp', cc, S) * jnp.exp(cum)[..., None]
        decay_end = jnp.exp(cum[:, -1:] - cum)
        S = S * jnp.exp(cum[:, -1])[..., None, None] + jnp.einsum('bjgn,bjgrp->bgrpn', bc, xdtc * decay_end[..., None])
        return S, y

    S, y = lax.scan(step, s0, tuple(_to_chunks(a, c) for a in (xdt, dA, bm, cm)))
    y = _from_chunks(y) + x * d_skip.astype(jnp.float32)[..., None]
    return y, S


def _hgrn2(q, f_logit, i, lb, s0):
    L = q.shape[1]
    c = _chunk_len(L, HG_CHUNK)
    q = jax.nn.silu(q)
    f = lb + (1.0 - lb) * jax.nn.sigmoid(f_logit)
    log_f = jnp.log(f)
    k = (1.0 - lb) * jax.nn.sigmoid(-f_logit)
    mask = jnp.tril(jnp.ones((c, c), dtype=bool))[:, :, None, None]

    def step(S, blk):
        qc, kc, ic, lfc = blk
        G = jnp.cumsum(lfc, axis=1)
        seg = G[:, :, None] - G[:, None, :]
        dec = _masked_exp(mask, seg)
        att = jnp.einsum('bijhd,bjhd->bhij', qc[:, :, None] * dec, kc)
        o = jnp.einsum('bhij,bjhe->bihe', att, ic)
        o = o + jnp.einsum('bihd,bhde->bihe', qc * jnp.exp(G), S)
        g_last = G[:, -1]
        S = S * jnp.exp(g_last)[..., None] + jnp.einsum('bjhd,bjhe->bhde', kc * jnp.exp(g_last[:, None] - G), ic)
        return S, o

    S, o = lax.scan(step, s0, tuple(_to_chunks(a, c) for a in (q, k, i, log_f)))
    return _from_chunks(o), S


def _cmul(ar, ai, br, bi):
    return ar * br - ai * bi, ar * bi + ai * br


def _s5_combine(e1, e2):
    a1r, a1i, b1r, b1i = e1
    a2r, a2i, b2r, b2i = e2
    ar, ai = _cmul(a2r, a2i, a1r, a1i)
    br, bi = _cmul(a2r, a2i, b1r, b1i)
    return ar, ai, br + b2r, bi + b2i


def _s5(u, h0_re, h0_im, a_re, a_im, b_re, b_im, c_re, c_im, d, log_dt, w_glu):
    B, L = u.shape[:2]
    a_re, a_im, b_re, b_im, c_re, c_im, d, log_dt = (t.astype(jnp.float32) for t in (a_re, a_im, b_re, b_im, c_re, c_im, d, log_dt))
    dt = jnp.exp(log_dt)[:, None]
    mag = jnp.exp(dt * a_re)
    ab_re, ab_im = mag * jnp.cos(dt * a_im), mag * jnp.sin(dt * a_im)
    den = a_re * a_re + a_im * a_im
    n_re, n_im = ab_re - 1.0, ab_im
    f_re = (n_re * a_re + n_im * a_im) / den
    f_im = (n_im * a_re - n_re * a_im) / den
    bb_re = f_re[..., None] * b_re - f_im[..., None] * b_im
    bb_im = f_re[..., None] * b_im + f_im[..., None] * b_re
    bu_re = jnp.einsum('blgm,gpm->blgp', u, bb_re)
    bu_im = jnp.einsum('blgm,gpm->blgp', u, bb_im)
    i_re, i_im = _cmul(ab_re, ab_im, h0_re, h0_im)
    bu_re = bu_re.at[:, 0].add(i_re)
    bu_im = bu_im.at[:, 0].add(i_im)
    a_re_t = jnp.broadcast_to(ab_re, bu_re.shape)
    a_im_t = jnp.broadcast_to(ab_im, bu_im.shape)
    _, _, h_re, h_im = lax.associative_scan(_s5_combine, (a_re_t, a_im_t, bu_re, bu_im), axis=1)
    y = jnp.einsum('gmp,blgp->blgm', c_re, h_re) - jnp.einsum('gmp,blgp->blgm', c_im, h_im) + d * u
    z = jax.nn.gelu(y.reshape(B, L, MIX_W))
    out = z * jax.nn.sigmoid(jnp.matmul(z, w_glu.astype(jnp.float32)))
    return out, h_re[:, -1], h_im[:, -1]


def _layer(x, st, p, lb, pos):
    B, L, _ = x.shape
    h = _rmsnorm(x, p['g_pre_mix'])
    proj = jnp.matmul(h, p['w_in']).astype(jnp.float32)
    sizes = (MIX_W,) * 4 + (MIX_W, SSD_CONV_DIM, SSD_HEADS) + (MIX_W,) * 4 + (MIX_W, GATE_COLS)
    (rq, rk, rv, rg, sz, sxbc, sdt, hq, hf, hi, hg, su, gl) = jnp.split(proj, np.cumsum(sizes)[:-1].tolist(), axis=-1)

    def heads(a, n):
        return a.reshape(B, L, n, a.shape[-1] // n)

    o_ret, s_ret = _retention(heads(rq, RET_HEADS), heads(rk, RET_HEADS), heads(rv, RET_HEADS), pos, st['ret'])
    o_ret = _head_groupnorm(o_ret, p['ret_gn']).reshape(B, L, MIX_W) * jax.nn.silu(rg)

    xbc, conv_new = _causal_conv(sxbc, st['conv'], p['ssd_conv_w'], p['ssd_conv_b'])
    xs, bm, cm = jnp.split(xbc, [MIX_W, MIX_W + SSD_GROUPS * SSD_STATE], axis=-1)
    grp = (SSD_GROUPS, SSD_HPG)
    y, s_ssd = _ssd(xs.reshape(B, L, SSD_GROUPS, SSD_HPG, SSD_HEADDIM), sdt.reshape(B, L, SSD_GROUPS, SSD_HPG),
                    bm.reshape(B, L, SSD_GROUPS, SSD_STATE), cm.reshape(B, L, SSD_GROUPS, SSD_STATE),
                    st['ssd'].reshape(B, SSD_GROUPS, SSD_HPG, SSD_HEADDIM, SSD_STATE),
                    p['ssd_a_log'].reshape(grp), p['ssd_dt_bias'].reshape(grp), p['ssd_d'].reshape(grp))
    y = y.reshape(B, L, MIX_W) * jax.nn.silu(sz)
    o_ssd = _rmsnorm(y.reshape(B, L, SSD_GROUPS, MIX_W // SSD_GROUPS), p['ssd_norm'].reshape(SSD_GROUPS, MIX_W // SSD_GROUPS)).reshape(B, L, MIX_W)
    s_ssd = s_ssd.reshape(B, SSD_HEADS, SSD_HEADDIM, SSD_STATE)

    o_hg, s_hg = _hgrn2(heads(hq, HG_HEADS), heads(hf, HG_HEADS), heads(hi, HG_HEADS), lb.reshape(HG_HEADS, HG_DK), st['hgrn'])
    o_hg = _rmsnorm(o_hg, p['hg_norm']).reshape(B, L, MIX_W) * jax.nn.sigmoid(hg)

    o_s5, s5_re, s5_im = _s5(su.reshape(B, L, S5_GROUPS, S5_GROUP), st['s5_re'], st['s5_im'],
                             p['s5_a_re'], p['s5_a_im'], p['s5_b_re'], p['s5_b_im'], p['s5_c_re'], p['s5_c_im'],
                             p['s5_d'], p['s5_log_dt'], p['s5_w_glu'])

    branches = jnp.stack([o_ret, o_ssd, o_hg, o_s5], axis=2).astype(x.dtype)
    gates = jax.nn.sigmoid(gl.reshape(B, L, N_BRANCH, D_MODEL)).astype(x.dtype)
    merged = jnp.sum(gates * jnp.einsum('blmw,mwd->blmd', branches, p['w_branch']), axis=2)
    x = x + _rmsnorm(jnp.matmul(merged, p['w_out']), p['g_post_mix']).astype(x.dtype)

    h = _rmsnorm(x, p['g_pre_ffn'])
    u = jnp.square(jax.nn.relu(jnp.matmul(h, p['w_ff1'])))
    x = x + _rmsnorm(jnp.matmul(u, p['w_ff2']), p['g_post_ffn']).astype(x.dtype)
    new = {'ret': s_ret, 'ssd': s_ssd, 'conv': conv_new, 'hgrn': s_hg, 's5_re': s5_re, 's5_im': s5_im}
    return x, new


def _trunk(x, states, params, lb_logits, pos):
    w = jax.nn.softmax(lb_logits.astype(jnp.float32), axis=0)
    lbs = jnp.cumsum(w, axis=0) - w[0]
    names = ('ret', 'ssd', 'conv', 'hgrn', 's5_re', 's5_im')
    collected = {n: [] for n in names}
    for l in range(DEPTH):
        p = {name: arr[l] for name, arr in params.items()}
        st = {n: states[n][l].astype(jnp.float32) for n in names}
        x, new = _layer(x, st, p, lbs[l], pos)
        for n in names:
            collected[n].append(new[n])
    return x, {n: jnp.stack(collected[n]) for n in names}


def setup_inputs(seed: int = 0) -> dict:
    key = jax.random.key(seed)
    k = jax.random.split(key, 35)
    f32 = jnp.float32

    def nrm(i, shape, s=1.0):
        return jax.random.normal(k[i], shape, f32) * s

    def unif(i, shape, lo, hi):
        return jax.random.uniform(k[i], shape, f32, lo, hi)

    dt0 = jnp.exp(unif(17, (DEPTH, SSD_HEADS), math.log(1e-3), math.log(1e-1)))
    a_im0 = math.pi * jnp.arange(S5_STATE, dtype=f32)
    return {
        'x_prompt': nrm(0, (BATCH, SEQ, D_MODEL)),
        'x_sample': nrm(1, (DEC_BATCH, DEC_SEQ, D_MODEL)),
        'state_ret': nrm(2, (DEPTH, DEC_BATCH, RET_HEADS, RET_DK, RET_DV), 0.5),
        'state_ssd': nrm(3, (DEPTH, DEC_BATCH, SSD_HEADS, SSD_HEADDIM, SSD_STATE), 0.5),
        'state_conv': nrm(4, (DEPTH, DEC_BATCH, SSD_CONV - 1, SSD_CONV_DIM)),
        'state_hgrn': nrm(5, (DEPTH, DEC_BATCH, HG_HEADS, HG_DK, HG_DV), 0.5),
        'state_s5_re': nrm(6, (DEPTH, DEC_BATCH, S5_GROUPS, S5_STATE), 0.5),
        'state_s5_im': nrm(7, (DEPTH, DEC_BATCH, S5_GROUPS, S5_STATE), 0.5),
        'g_pre_mix': 1.0 + nrm(8, (DEPTH, D_MODEL), 0.05),
        'g_post_mix': 1.0 + nrm(9, (DEPTH, D_MODEL), 0.05),
        'g_pre_ffn': 1.0 + nrm(10, (DEPTH, D_MODEL), 0.05),
        'g_post_ffn': 1.0 + nrm(11, (DEPTH, D_MODEL), 0.05),
        'w_in': nrm(12, (DEPTH, D_MODEL, IN_COLS), D_MODEL ** -0.5),
        'ret_gn': 1.0 + nrm(13, (DEPTH, RET_HEADS, RET_DV), 0.05),
        'ssd_conv_w': nrm(14, (DEPTH, SSD_CONV, SSD_CONV_DIM), SSD_CONV ** -0.5),
        'ssd_conv_b': nrm(15, (DEPTH, SSD_CONV_DIM), 0.01),
        'ssd_a_log': jnp.log(unif(16, (DEPTH, SSD_HEADS), 1.0, 16.0)),
        'ssd_dt_bias': dt0 + jnp.log(-jnp.expm1(-dt0)),
        'ssd_d': 1.0 + nrm(18, (DEPTH, SSD_HEADS), 0.1),
        'ssd_norm': 1.0 + nrm(19, (DEPTH, MIX_W), 0.05),
        'hg_lb_logits': nrm(20, (DEPTH, HG_HEADS * HG_DK), 0.1),
        'hg_norm': 1.0 + nrm(21, (DEPTH, HG_DV), 0.05),
        's5_a_re': -0.5 + nrm(22, (DEPTH, S5_GROUPS, S5_STATE), 0.01),
        's5_a_im': a_im0 + nrm(23, (DEPTH, S5_GROUPS, S5_STATE), 0.01),
        's5_b_re': nrm(24, (DEPTH, S5_GROUPS, S5_STATE, S5_GROUP), (2 * S5_GROUP) ** -0.5),
        's5_b_im': nrm(25, (DEPTH, S5_GROUPS, S5_STATE, S5_GROUP), (2 * S5_GROUP) ** -0.5),
        's5_c_re': nrm(26, (DEPTH, S5_GROUPS, S5_GROUP, S5_STATE), S5_STATE ** -0.5),
        's5_c_im': nrm(27, (DEPTH, S5_GROUPS, S5_GROUP, S5_STATE), S5_STATE ** -0.5),
        's5_d': nrm(28, (DEPTH, S5_GROUPS, S5_GROUP)),
        's5_log_dt': unif(29, (DEPTH, S5_GROUPS), math.log(1e-3), math.log(1e-1)),
        's5_w_glu': nrm(30, (DEPTH, MIX_W, MIX_W), MIX_W ** -0.5),
        'w_branch': nrm(31, (DEPTH, N_BRANCH, MIX_W, D_MODEL), MIX_W ** -0.5),
        'w_out': nrm(32, (DEPTH, D_MODEL, D_MODEL), D_MODEL ** -0.5),
        'w_ff1': nrm(33, (DEPTH, D_MODEL, D_FF), D_MODEL ** -0.5),
        'w_ff2': nrm(34, (DEPTH, D_FF, D_MODEL), D_FF ** -0.5),
    }


def reference(x_prompt, x_sample, state_ret, state_ssd, state_conv, state_hgrn, state_s5_re, state_s5_im,
              g_pre_mix, g_post_mix, g_pre_ffn, g_post_ffn, w_in, ret_gn, ssd_conv_w, ssd_conv_b,
              ssd_a_log, ssd_dt_bias, ssd_d, ssd_norm, hg_lb_logits, hg_norm, s5_a_re, s5_a_im,
              s5_b_re, s5_b_im, s5_c_re, s5_c_im, s5_d, s5_log_dt, s5_w_glu, w_branch, w_out, w_ff1, w_ff2):
    params = {
        'g_pre_mix': g_pre_mix, 'g_post_mix': g_post_mix, 'g_pre_ffn': g_pre_ffn, 'g_post_ffn': g_post_ffn,
        'w_in': w_in, 'ret_gn': ret_gn, 'ssd_conv_w': ssd_conv_w, 'ssd_conv_b': ssd_conv_b,
        'ssd_a_log': ssd_a_log, 'ssd_dt_bias': ssd_dt_bias, 'ssd_d': ssd_d, 'ssd_norm': ssd_norm,
        'hg_norm': hg_norm, 's5_a_re': s5_a_re, 's5_a_im': s5_a_im, 's5_b_re': s5_b_re, 's5_b_im': s5_b_im,
        's5_c_re': s5_c_re, 's5_c_im': s5_c_im, 's5_d': s5_d, 's5_log_dt': s5_log_dt, 's5_w_glu': s5_w_glu,
        'w_branch': w_branch, 'w_out': w_out, 'w_ff1': w_ff1, 'w_ff2': w_ff2,
    }
    bp = x_prompt.shape[0]
    f32 = jnp.float32
    zero_states = {
        'ret': jnp.zeros((DEPTH, bp, RET_HEADS, RET_DK, RET_DV), f32),
        'ssd': jnp.zeros((DEPTH, bp, SSD_HEADS, SSD_HEADDIM, SSD_STATE), f32),
        'conv': jnp.zeros((DEPTH, bp, SSD_CONV - 1, SSD_CONV_DIM), f32),
        'hgrn': jnp.zeros((DEPTH, bp, HG_HEADS, HG_DK, HG_DV), f32),
        's5_re': jnp.zeros((DEPTH, bp, S5_GROUPS, S5_STATE), f32),
        's5_im': jnp.zeros((DEPTH, bp, S5_GROUPS, S5_STATE), f32),
    }
    sample_states = {'ret': state_ret, 'ssd': state_ssd, 'conv': state_conv, 'hgrn': state_hgrn,
                     's5_re': state_s5_re, 's5_im': state_s5_im}
    pos_p = jnp.arange(x_prompt.shape[1], dtype=jnp.int32)
    pos_s = PAST_LEN + jnp.arange(x_sample.shape[1], dtype=jnp.int32)
    y_prompt, new_p = _trunk(x_prompt, zero_states, params, hg_lb_logits, pos_p)
    y_sample, new_s = _trunk(x_sample, sample_states, params, hg_lb_logits, pos_s)
    return (y_prompt, y_sample,
            new_p['ret'], new_s['ret'], new_p['ssd'], new_s['ssd'], new_p['conv'], new_s['conv'],
            new_p['hgrn'], new_s['hgrn'], new_p['s5_re'], new_s['s5_re'], new_p['s5_im'], new_s['s5_im'])
```

```python
import math
import numpy as np
import ml_dtypes
import concourse.bass as bass
import concourse.mybir as mybir
from concourse.bass_utils import run_bass_kernel_spmd

COMPUTE = ("pe", "act", "dve", "pool")
EPOCH = 2000
RING = {"sp": 16, "act": 8, "pool": 16}


def _norm(k):
    return k if isinstance(k, tuple) else (k, None)


class Sched:
    def __init__(self):
        self.ins = []

    def add(self, eng, fn, r=(), w=(), dma=False, out_dma=False):
        self.ins.append(dict(eng=eng, fn=fn, r=[_norm(k) for k in r], w=[_norm(k) for k in w],
                             dma=dma, out_dma=out_dma, deps=set(), sig=False))
        return len(self.ins) - 1

    def analyze(self):
        lastw = {}
        readers = {}
        for i, ins in enumerate(self.ins):
            deps = set()

            def conf_slots(d, slot):
                if slot is None:
                    return list(d.keys())
                return [s for s in (slot, None) if s in d]

            for (n, s) in ins["r"]:
                d = lastw.get(n, {})
                for ss in conf_slots(d, s):
                    deps.add(d[ss])
            for (n, s) in ins["w"]:
                d = lastw.get(n, {})
                for ss in conf_slots(d, s):
                    deps.add(d[ss])
                rd = readers.get(n, {})
                for ss in conf_slots(rd, s):
                    deps.update(rd[ss].values())
            deps.discard(i)
            keep = set()
            for dpi in deps:
                p = self.ins[dpi]
                if (not p["dma"]) and (not ins["dma"]) and p["eng"] == ins["eng"]:
                    if ins["eng"] == "pe":
                        continue
                    pass
                keep.add(dpi)
            ins["deps"] = keep
            for dpi in keep:
                self.ins[dpi]["sig"] = True
            for (n, s) in ins["w"]:
                d = lastw.setdefault(n, {})
                rd = readers.setdefault(n, {})
                if s is None:
                    d.clear()
                    rd.clear()
                d[s] = i
                rd[s] = {}
            for (n, s) in ins["r"]:
                rd = readers.setdefault(n, {})
                e = rd.setdefault(s, {})
                key = ("dma", i) if ins["dma"] else ins["eng"]
                e[key] = i

    def emit(self, nc):
        self.analyze()
        cnt = {e: 0 for e in COMPUTE}
        dcnt = {q: 0 for q in RING}
        for ins in self.ins:
            if ins["dma"]:
                q = ins["eng"]
                ins["dq"] = dcnt[q]
                dcnt[q] += 1
            elif ins["sig"]:
                cnt[ins["eng"]] += 1
                ins["cnt"] = cnt[ins["eng"]]
        sems = {}
        for e in COMPUTE:
            nep = (cnt[e] + EPOCH - 1) // EPOCH + 1
            sems[e] = [nc.alloc_semaphore(f"s_{e}_{k}") for k in range(nep)]
        rings = {q: [nc.alloc_semaphore(f"r_{q}_{k}") for k in range(RING[q])] for q in RING if dcnt[q] > 0}

        def sig_of(p):
            if p["dma"]:
                q = p["eng"]
                k = p["dq"]
                return rings[q][k % RING[q]], 16 * (k // RING[q] + 1), ("d", q, k % RING[q])
            c = p["cnt"] - 1
            return sems[p["eng"]][c // EPOCH], (c % EPOCH) + 1, ("c", p["eng"], c // EPOCH)

        per_eng = {}
        for i, ins in enumerate(self.ins):
            per_eng.setdefault(ins["eng"], []).append(i)
        out_dmas = [i for i, ins in enumerate(self.ins) if ins["out_dma"]]

        def run(engname, e):
            waited = {}
            maxep = {}

            def do_wait(p):
                sem, val, ok = sig_of(p)
                if ok[0] == "c":
                    me = maxep.get(ok[1], -1)
                    if ok[2] < me:
                        return
                    if ok[2] > me:
                        maxep[ok[1]] = ok[2]
                if waited.get(ok, 0) >= val:
                    return
                waited[ok] = val
                e.wait_ge(sem, val)

            for i in per_eng.get(engname, []):
                ins = self.ins[i]
                for dpi in sorted(ins["deps"]):
                    do_wait(self.ins[dpi])
                if ins["dma"]:
                    q = ins["eng"]
                    k = ins["dq"]
                    if k >= RING[q]:
                        sem = rings[q][k % RING[q]]
                        val = 16 * (k // RING[q])
                        ok = ("d", q, k % RING[q])
                        if waited.get(ok, 0) < val:
                            waited[ok] = val
                            e.wait_ge(sem, val)
                    sem, val, _ = sig_of(ins)
                    ins["fn"](e).then_inc(sem, 16)
                else:
                    bi = ins["fn"](e)
                    if ins["sig"]:
                        sem, val, _ = sig_of(ins)
                        bi.then_inc(sem, 1)
            if engname == "sp":
                for i in out_dmas:
                    do_wait(self.ins[i])

        with nc.Block() as block:
            @block.tensor
            def _(e):
                run("pe", e)

            @block.scalar
            def _(e):
                run("act", e)

            @block.vector
            def _(e):
                run("dve", e)

            @block.gpsimd
            def _(e):
                run("pool", e)

            @block.sync
            def _(e):
                run("sp", e)
        return {e: len(per_eng.get(e, [])) for e in ("pe", "act", "dve", "pool", "sp")}


F32, BF16 = mybir.dt.float32, mybir.dt.bfloat16
AF = mybir.ActivationFunctionType
ALU = mybir.AluOpType
AX = mybir.AxisListType

D = 1024
MIXW = 512
DFF = 4096
IN_COLS = 10248
PAST_LEN = 16384
EPS = 1e-6
NSEQ = 16
DSEQ = 8
C_RQ, C_RK, C_RV, C_RG = 0, 512, 1024, 1536
C_SZ, C_SXBC, C_SDT = 2048, 2560, 3584
C_HQ, C_HF, C_HI, C_HG = 3592, 4104, 4616, 5128
C_SU = 5640
C_GL = 6152
SB_LO, SB_HI = 16512, 229344
RET_DEC = [float(np.exp(np.log(1.0 - 2.0 ** (-5.0 - h)) * 128.0)) for h in range(4)]


class V:
    def __init__(self, ap, *keys):
        self.ap = ap
        self.keys = list(keys)


def host_consts(ntp):
    nt = ntp + 1
    c = {}
    c["ident_bf"] = np.eye(128, dtype=np.float32).astype(ml_dtypes.bfloat16)
    c["ident_f"] = np.eye(128, dtype=np.float32)
    p = np.arange(128)
    pos = np.zeros((128, nt), np.float32)
    for t in range(ntp):
        pos[:, t] = t * 128 + p
    pos[:, ntp] = PAST_LEN + (p % DSEQ)
    half = 64
    inv = (10000.0 ** (-np.arange(half, dtype=np.float32) / half)).astype(np.float32)
    ang = (pos[:, :, None] * inv[None, None, :]).astype(np.float32)
    rope = np.zeros((128, nt, 3, 64), np.float32)
    rope[:, :, 0] = np.cos(ang)
    rope[:, :, 1] = np.sin(ang)
    rope[:, :, 2] = -np.sin(ang)
    c["rope"] = rope
    jj, ii = np.meshgrid(p, p, indexing="ij")
    mp = (jj <= ii).astype(np.float32)
    ms = ((jj <= ii) & (jj // DSEQ == ii // DSEQ)).astype(np.float32)
    c["maskT"] = np.stack([mp, ms], 1).astype(ml_dtypes.bfloat16)
    logg = np.log(1.0 - 2.0 ** (-5.0 - np.arange(4, dtype=np.float64)))
    rdec = np.zeros((128, 2, 3, 4), np.float32)
    for kind, (C, idx) in enumerate(((128, p), (DSEQ, p % DSEQ))):
        for h in range(4):
            rdec[:, kind, 0, h] = np.exp(logg[h] * (idx + 1.0 - C))
            rdec[:, kind, 1, h] = np.exp(logg[h] * (C - 1.0 - idx)) * 128 ** -0.5
            rdec[:, kind, 2, h] = np.exp(logg[h] * C)
    c["rdec"] = rdec
    c["seqsel"] = (p[:, None] // DSEQ == np.arange(NSEQ)[None, :]).astype(np.float32)
    tric = np.zeros((128, 2, 128), np.float32)
    tric[:, 0, :] = (jj <= ii).astype(np.float32) - (jj <= 63).astype(np.float32)
    tric[:, 1, :] = ms
    c["tric"] = tric
    sel3 = np.zeros((128, 3), np.float32)
    sel3[:, 0] = (p <= 63); sel3[:, 1] = 1.0; sel3[:, 2] = (p > 63)
    c["sel3"] = sel3
    trif = np.stack([mp, ms], 1).astype(np.float32)
    c["trif"] = trif
    c["ntrif"] = -trif
    c["negm"] = (-30000.0 * (1.0 - trif)).astype(np.float32)
    blk = np.ones((128, 2, 128), np.float32)
    blk[:, 1, :] = (jj // DSEQ == ii // DSEQ)
    c["blk"] = blk
    c["nhalf"] = np.full((128, 16), -0.5, np.float32)
    c["t1"] = np.tile((p + 1.0)[None, :], (128, 1)).astype(np.float32)
    zm = np.ones((128, 2, 128), np.float32)
    zm[:, 0, 0] = 0.0
    zm[:, 1, ::DSEQ] = 0.0
    c["zmask"] = zm
    return c


class KB:
    def __init__(self, ntp, groups, depth=2, branches=("ret",), ffn=True, plan=None):
        self._plan_arg = plan
        self.ntp, self.nt, self.groups, self.depth = ntp, ntp + 1, groups, depth
        self.branches, self.ffn = branches, ffn
        self.nc = bass.Bass("TRN2", target_bir_lowering=False)
        self.s = Sched()
        self.smax = max(len(g) for g in groups)
        self.consts = host_consts(ntp)
        self.dram = {}
        self._slot_ctr = 0
        self.top = SB_LO
        self.alias = {}
        self.uid = 0
        self.peak = SB_LO
        self.hi = SB_HI
        self.lo_top = SB_HI
        self.top_watermark = 0
        self.s5p = None
        self.plan = self._plan_arg
        self.rec = []
        self.w_next = 0
        self.w_done = 0

    def din(self, name, shape, dt=F32):
        self.dram[name] = self.nc.dram_tensor(name, list(shape), dt, kind="ExternalInput").ap()
        return self.dram[name]

    def dout(self, name, shape, dt=F32):
        self.dram[name] = self.nc.dram_tensor(name, list(shape), dt, kind="ExternalOutput").ap()
        return self.dram[name]

    def sb(self, name, shape, dt=F32, scratch=False):
        n = 1
        for x in shape[1:]:
            n *= x
        nbytes = n * (2 if dt == BF16 else 4)
        off = (self.top + 63) // 64 * 64
        assert off + nbytes <= self.hi, f"SBUF overflow allocating {name}: {off + nbytes - self.hi} bytes over"
        self.uid += 1
        h = self.nc.alloc_sbuf_tensor_at(f"{name}_{self.uid}", list(shape), dt, offset=off)
        self.top = off + nbytes
        self.peak = max(self.peak, self.top)
        if scratch:
            g0, g1 = off // 512, (off + nbytes + 511) // 512
            self.alias[name] = [("A", g) for g in range(g0, g1)]
        return h

    def sb_top(self, name, shape, dt=F32):
        n = 1
        for x in shape[1:]:
            n *= x
        nbytes = n * (2 if dt == BF16 else 4)
        off = (self.hi - nbytes) // 64 * 64
        assert off >= self.top, f"top allocation {name} collides with scratch ({off} < {self.top})"
        self.uid += 1
        h = self.nc.alloc_sbuf_tensor_at(f"{name}_{self.uid}", list(shape), dt, offset=off)
        self.hi = off
        self.lo_top = min(self.lo_top, off)
        g0, g1 = off // 512, (off + nbytes + 511) // 512
        self.alias[name] = [("A", g) for g in range(g0, g1)]
        return h

    def split_alias(self, name, n=2):
        g = self.alias[name]
        m = len(g) // n
        for i in range(n):
            self.alias[name + str(i)] = g[i * m:(i + 1) * m]

    def mark(self):
        return self.top

    def release(self, m):
        self.top = m

    def K(self, keys):
        out = []
        for k in keys:
            n = k[0] if isinstance(k, tuple) else k
            if n in self.alias:
                out += self.alias[n]
            else:
                out.append(k)
        return out

    @staticmethod
    def ap(h, off, dims):
        fs = 1
        for x in h.shape[1:]:
            fs *= x
        return bass.AP(h, off, [[fs, h.shape[0]]] + [list(d) for d in dims])

    def add(self, eng, fn, r, w, **kw):
        self.s.add(eng, fn, r=self.K(r), w=self.K(w), **kw)

    def mm(self, out, lhsT, rhs, start=True, stop=True):
        self.add("pe", lambda e, o=out.ap, l=lhsT.ap, r=rhs.ap: e.matmul(o, lhsT=l, rhs=r, start=start, stop=stop),
                 r=lhsT.keys + rhs.keys + ([] if start else out.keys), w=out.keys)

    def tr(self, out, in_, ident):
        self.add("pe", lambda e, o=out.ap, i=in_.ap, d=ident.ap: e.transpose(o, i, d),
                 r=in_.keys + ident.keys, w=out.keys)

    def tt(self, eng, out, in0, in1, op):
        self.add(eng, lambda e, o=out.ap, a=in0.ap, b=in1.ap: e.tensor_tensor(o, a, b, op),
                 r=in0.keys + in1.keys, w=out.keys)

    def ts(self, eng, out, in0, s1, op0, s2=None, op1=None):
        def fn(e, o=out.ap, a=in0.ap):
            sc1 = s1.ap if isinstance(s1, V) else s1
            sc2 = s2.ap if isinstance(s2, V) else s2
            if op1 is None:
                return e.tensor_scalar(o, a, sc1, None, op0)
            return e.tensor_scalar(o, a, sc1, sc2, op0, op1)
        rk = list(in0.keys)
        for sx in (s1, s2):
            if isinstance(sx, V):
                rk += sx.keys
        self.add(eng, fn, r=rk, w=out.keys)

    def stt(self, out, in0, sc, in1, op0, op1):
        def fn(e, o=out.ap, a=in0.ap, b=in1.ap):
            return e.scalar_tensor_tensor(o, a, sc.ap if isinstance(sc, V) else sc, b, op0, op1)
        rk = in0.keys + in1.keys + (sc.keys if isinstance(sc, V) else [])
        self.add("dve", fn, r=rk, w=out.keys)

    def act(self, out, in_, func, bias=0.0, scale=1.0, accum=None):
        def fn(e, o=out.ap, i=in_.ap):
            kw = {}
            if accum is not None:
                kw["accum_out"] = accum.ap
            return e.activation(o, i, func, bias=(bias.ap if isinstance(bias, V) else bias),
                                scale=(scale.ap if isinstance(scale, V) else scale), **kw)
        rk = list(in_.keys)
        for sx in (bias, scale):
            if isinstance(sx, V):
                rk += sx.keys
        wk = out.keys + (accum.keys if accum is not None else [])
        self.add("act", fn, r=rk, w=wk)

    def cp(self, eng, out, in_):
        if eng == "act":
            self.add("act", lambda e, o=out.ap, i=in_.ap: e.copy(o, i), r=in_.keys, w=out.keys)
        else:
            self.add(eng, lambda e, o=out.ap, i=in_.ap: e.tensor_copy(o, i), r=in_.keys, w=out.keys)

    def red(self, out, in_, op=ALU.add):
        self.add("dve", lambda e, o=out.ap, i=in_.ap: e.tensor_reduce(o, i, AX.X, op), r=in_.keys, w=out.keys)

    def recip(self, out, in_):
        self.add("dve", lambda e, o=out.ap, i=in_.ap: e.reciprocal(o, i), r=in_.keys, w=out.keys)

    def memset(self, eng, out, val):
        self.add(eng, lambda e, o=out.ap: e.memset(o, val), r=[], w=out.keys)

    def dma(self, q, out, in_, out_dma=False):
        self.add(q, lambda e, o=out.ap, i=in_.ap: e.dma_start(out=o, in_=i), r=in_.keys, w=out.keys,
                 dma=True, out_dma=out_dma)


class Prog(KB):
    NSLOT = 8

    def setup(self):
        nc, nt, S = self.nc, self.nt, self.smax
        L = self.depth
        self.din("xin", [nt * 128, D])
        self.din("st_ret", [L, NSEQ, 4, 128, 128])
        self.din("st_hg", [L, NSEQ, 4, 128, 128])
        self.din("st_ssd", [L, NSEQ, 8, 64, 128])
        self.din("st_conv", [L, NSEQ, 3, 1024])
        self.din("st_s5re", [L, NSEQ, 2048])
        self.din("st_s5im", [L, NSEQ, 2048])
        for n, shp in (("g_pre_mix", [L, D]), ("g_post_mix", [L, D]), ("g_pre_ffn", [L, D]), ("g_post_ffn", [L, D]),
                       ("w_in", [L, D, IN_COLS]), ("ret_gn", [L, 512]), ("ssd_conv_w", [L, 4, 1024]),
                       ("ssd_conv_b", [L, 1024]), ("ssd_a_log", [L, 8]), ("ssd_dt_bias", [L, 8]), ("ssd_d", [L, 8]),
                       ("ssd_norm", [L, 512]), ("hg_lb_logits", [L, 512]), ("hg_norm", [L, 128]),
                       ("s5_a_re", [L, 2048]), ("s5_a_im", [L, 2048]), ("s5_b_re", [L, 2048, 16]),
                       ("s5_b_im", [L, 2048, 16]), ("s5_c_re", [L, 32, 16, 64]), ("s5_c_im", [L, 32, 16, 64]),
                       ("s5_d", [L, 512]), ("s5_log_dt", [L, 32]), ("s5_w_glu", [L, 512, 512]),
                       ("w_branch", [L, 4, 512, D]), ("w_out", [L, D, D]), ("w_ff1", [L, D, DFF]), ("w_ff2", [L, DFF, D])):
            self.din(n, shp)
        self.dout("y", [nt * 128, D])
        self.dout("ret_p", [L, 4, 128, 128]); self.dout("ret_s", [L, NSEQ, 4, 128, 128])
        self.dout("ssd_p", [L, 8, 64, 128]); self.dout("ssd_s", [L, NSEQ, 8, 64, 128])
        self.dout("conv_p", [L, 3, 1024]); self.dout("conv_s", [L, NSEQ, 3, 1024])
        self.dout("hg_p", [L, 4, 128, 128]); self.dout("hg_s", [L, NSEQ, 4, 128, 128])
        self.dout("s5re_p", [L, 2048]); self.dout("s5re_s", [L, NSEQ, 2048])
        self.dout("s5im_p", [L, 2048]); self.dout("s5im_s", [L, NSEQ, 2048])
        self.c = {}
        for name, arr in self.consts.items():
            dt = BF16 if arr.dtype == ml_dtypes.bfloat16 else F32
            d = self.din("c_" + name, arr.shape, dt)
            if name == "rope":
                continue
            h = self.sb("k_" + name, arr.shape, dt)
            self.c[name] = h
            nd = len(arr.shape)
            self.dma("sp", V(h[(slice(None),) * nd], "k_" + name), V(d[(slice(None),) * nd]))
        self.c["rope"] = self.sb("k_rope", [128, S, 3, 64])
        self.x = self.sb("x", [128, S, D])
        self.hT = self.sb("hT", [128, 8, S * 128], BF16)
        self.mrg = self.sb("mrg", [128, 8, S * 128])
        self.oT = self.sb("oT", [128, 4, S * 128], BF16)
        self.wr = self.sb("wr", [128, self.NSLOT, 4096], BF16)
        self.ps = nc.alloc_psum_tensor("ps", [128, 8, 512], F32)
        self.gtab = self.sb("gtab", [128, D])
        self.junk = self.sb("junk", [128, D], BF16)
        self.st = self.sb("st", [128, 8, 16])
        self.stc = 0
        self.hb = self.sb("hb", [128, 2, D], BF16)
        self.hbc = 0
        self.c_eps = EPS
        self.S_ret = self.sb("S_ret", [128, L, 512])
        self.Sdbf_ret = self.sb("Sdbf_ret", [128, L, 512], BF16)
        self.S_hg = self.sb("S_hg", [128, L, 512])
        self.ST_ssd = self.sb("ST_ssd", [128, L, 512])
        self.STbf_ssd = self.sb("STbf_ssd", [128, L, 512], BF16)
        self.xcarry = self.sb("xcarry", [128, L, 8, 3])
        self.h_s5 = self.sb("h_s5", [128, L, 2, 16])

    def psb(self, b):
        return V(self.ps[:, b, :], ("ps", b))

    def stat(self):
        i = self.stc % 8
        self.stc += 1
        return i

    def wsrc(self, d):
        kind = d[0]
        if kind == "win":
            _, l, c0, ncol = d
            return self.dram["w_in"][l, :, c0:c0 + ncol].rearrange("(k p) c -> p k c", p=128), 8, ncol
        if kind == "wb":
            return self.dram["w_branch"][d[1], d[2]].rearrange("(k p) c -> p k c", p=128), 4, 1024
        if kind == "wout":
            return self.dram["w_out"][d[1], :, d[2] * 512:(d[2] + 1) * 512].rearrange("(k p) c -> p k c", p=128), 8, 512
        if kind == "ff1":
            return self.dram["w_ff1"][d[1], :, d[2]:d[2] + 512].rearrange("(k p) c -> p k c", p=128), 8, 512
        if kind == "ff2":
            _, l, q, i = d
            return self.dram["w_ff2"][l, q * 1024:(q + 1) * 1024, i * 512:(i + 1) * 512].rearrange("(k p) c -> p k c", p=128), 8, 512
        if kind == "glu":
            return self.dram["s5_w_glu"][d[1]].rearrange("(k p) c -> p k c", p=128), 4, 512
        raise ValueError(d)

    def _issue(self, i):
        src3, a, b = self.wsrc(self.plan[i])
        slot = i % self.NSLOT
        dst = self.ap(self.wr, slot * 4096, [[b, a], [1, b]])
        self.dma("pool", V(dst, ("wr", slot)), V(src3))

    def _pump(self):
        if self.plan is None:
            return
        while self.w_next < len(self.plan) and (self.w_next < self.NSLOT or self.w_done > self.w_next - self.NSLOT):
            self._issue(self.w_next)
            self.w_next += 1

    def phase_end(self):
        self.w_done = self._slot_ctr
        self._pump()

    def wload(self, d):
        i = self._slot_ctr
        self._slot_ctr += 1
        if self.plan is None:
            self.rec.append(d)
            src3, a, b = self.wsrc(d)
            slot = i % self.NSLOT
            dst = self.ap(self.wr, slot * 4096, [[b, a], [1, b]])
            self.dma("pool", V(dst, ("wr", slot)), V(src3))
            return slot
        assert self.plan[i] == d, (self.plan[i], d)
        while self.w_next <= i:
            self._issue(self.w_next)
            self.w_next += 1
        return i % self.NSLOT

    def wv(self, slot, a, b, i, j0=0, jn=None):
        jn = b if jn is None else jn
        return V(self.ap(self.wr, slot * 4096 + i * b + j0, [[1, jn]]), ("wr", slot))

    def win_chunk(self, l, c0, ncol=512):
        return self.wload(("win", l, c0, ncol))

    def load_gain(self, name, l):
        src = self.dram[name][l:l + 1, :].partition_broadcast(128)
        self.dma("sp", V(self.gtab[:, :], "gtab"), V(src))
        return V(self.gtab[:, :], "gtab")

    def bcast_load(self, dst, key, src_row):
        self.dma("sp", V(dst, key), V(src_row.partition_broadcast(128)))

    def rms_rstd(self, src, n):
        si = self.stat()
        ssq = V(self.st[:, si, 0:1], ("st", si))
        jv = self.junk[:, 0:n]
        if len(src.ap.shape) == 3:
            jv = jv.rearrange("p (a b) -> p a b", a=src.ap.shape[1])
        self.act(V(jv, "junk"), src, AF.Square, accum=ssq)
        rms = V(self.st[:, si, 1:2], ("st", si))
        self.ts("dve", rms, ssq, 1.0 / n, ALU.mult, EPS, ALU.add)
        rstd = V(self.st[:, si, 2:3], ("st", si))
        self.tt("pool", rstd, rms, V(self.c["nhalf"][:, 0:1], "k_nhalf"), ALU.pow)
        return rstd

    def norm_to_hT(self, li, g):
        xs = V(self.x[:, li, :], ("x", li))
        rstd = self.rms_rstd(xs, D)
        hi = self.hbc % 2
        self.hbc += 1
        hb = V(self.hb[:, hi, :], ("hb", hi))
        self.stt(hb, xs, rstd, g, ALU.mult, ALU.mult)
        bT = 4
        pT = self.ps[:, bT, :].bitcast(BF16)
        for kt in range(8):
            self.tr(V(pT[:, kt * 128:(kt + 1) * 128], ("ps", bT)), V(self.hb[:, hi, kt * 128:(kt + 1) * 128], ("hb", hi)),
                    V(self.c["ident_bf"][:, :], "k_ident_bf"))
        dst = self.ap(self.hT, li * 128, [[self.smax * 128, 8], [1, 128]])
        self.cp("act", V(dst, ("hT", li)), V(pT.rearrange("p (k t) -> p k t", k=8), ("ps", bT)))

    def proj_tok(self, li, slots, banks):
        for slot, b in zip(slots, banks):
            for kt in range(8):
                self.mm(self.psb(b), V(self.hT[:, kt, li * 128:(li + 1) * 128], ("hT", li)),
                        self.wv(slot, 8, 512, kt), start=(kt == 0), stop=(kt == 7))

    def transpose_to(self, srcs, bank, dsts, eng="act"):
        pT = self.ps[:, bank, :].bitcast(BF16)
        for i, (src, nm) in enumerate(srcs):
            for h in range(4):
                self.tr(V(pT[:, (i * 4 + h) * 128:(i * 4 + h + 1) * 128], ("ps", bank)),
                        V(src[:, h * 128:(h + 1) * 128], nm), V(self.c["ident_bf"][:, :], "k_ident_bf"))
        for i, (dst, nm) in enumerate(dsts):
            self.cp(eng if i % 2 == 0 else "dve", V(dst[:, :], nm), V(pT[:, i * 512:(i + 1) * 512], ("ps", bank)))

    def phase_ret(self, l, tiles):
        c = self.c
        m0 = self.mark()
        sb = lambda n, shp, dt=F32: self.sb(n, shp, dt, scratch=True)
        qr, kr, vbf, qT, kT, attm = [sb(n, [128, 512], BF16) for n in ("qr", "kr", "vbf", "qT", "kT", "attm")]
        ropeA, ropeB, sil, gn = [sb(n, [128, 512]) for n in ("ropeA", "ropeB", "sil", "gn_ret")]
        self.norm_scratch()
        slots = [self.win_chunk(l, c0) for c0 in (C_RQ, C_RK, C_RV, C_RG)]
        self.bcast_load(gn[:, :], "gn_ret", self.dram["ret_gn"][l:l + 1, :])
        Sl = V(self.S_ret[:, l, :], ("S_ret", l))
        Sdb = self.Sdbf_ret
        self.proj_tok(0, slots, [0, 1, 2, 3])
        for li, t in enumerate(tiles):
            samp = (t == self.ntp)
            kind = 1 if samp else 0
            cosb = V(self.ap(c["rope"], li * 192, [[0, 4], [0, 2], [1, 64]]), "k_rope")
            sinb = V(self.ap(c["rope"], li * 192 + 64, [[0, 4], [1, 64]]), "k_rope")
            nsinb = V(self.ap(c["rope"], li * 192 + 128, [[0, 4], [1, 64]]), "k_rope")
            for b, dst, nm, di in ((0, qr, "qr", 0), (1, kr, "kr", 1)):
                pv = self.ps[:, b, :].rearrange("p (h two e) -> p h two e", h=4, two=2)
                A = V(ropeA[:, :].rearrange("p (h two e) -> p h two e", h=4, two=2), "ropeA")
                Bv = ropeB[:, :].rearrange("p (h two e) -> p h two e", h=4, two=2)
                self.tt("dve", A, V(pv, ("ps", b)), cosb, ALU.mult)
                self.tt("dve", V(Bv[:, :, 0, :], "ropeB"), V(pv[:, :, 1, :], ("ps", b)), nsinb, ALU.mult)
                self.tt("dve", V(Bv[:, :, 1, :], "ropeB"), V(pv[:, :, 0, :], ("ps", b)), sinb, ALU.mult)
                self.tt("dve", V(ropeA[:, :], "ropeA"), V(ropeA[:, :], "ropeA"), V(ropeB[:, :], "ropeB"), ALU.add)
                dcb = V(self.ap(c["rdec"], kind * 12 + di * 4, [[1, 4], [0, 128]]), "k_rdec")
                self.tt("dve", V(dst[:, :].rearrange("p (h e) -> p h e", h=4), nm),
                        V(ropeA[:, :].rearrange("p (h e) -> p h e", h=4), "ropeA"), dcb, ALU.mult)
            self.cp("act", V(vbf[:, :], "vbf"), self.psb(2))
            self.act(V(sil[:, :], "sil"), self.psb(3), AF.Silu)
            self.tt("dve", V(sil[:, :], "sil"), V(sil[:, :], "sil"), V(gn[:, :], "gn_ret"), ALU.mult)
            self.transpose_to([(qr, "qr"), (kr, "kr")], 4, [(qT, "qT"), (kT, "kT")])
            for h in range(4):
                hs = slice(h * 128, (h + 1) * 128)
                self.mm(V(self.ps[:, 5, hs], ("ps", 5)), V(kT[:, hs], "kT"), V(qT[:, hs], "qT"))
            mk = V(self.ap(c["maskT"], kind * 128, [[0, 4], [1, 128]]), "k_maskT")
            self.tt("dve", V(attm[:, :].rearrange("p (h i) -> p h i", h=4), "attm"),
                    V(self.ps[:, 5, :].rearrange("p (h i) -> p h i", h=4), ("ps", 5)), mk, ALU.mult)
            if not samp:
                first = (t == 0)
                for h in range(4):
                    hs = slice(h * 128, (h + 1) * 128)
                    self.mm(V(self.ps[:, 6, hs], ("ps", 6)), V(attm[:, hs], "attm"), V(vbf[:, hs], "vbf"),
                            start=True, stop=first)
                    if not first:
                        self.mm(V(self.ps[:, 6, hs], ("ps", 6)), V(qT[:, hs], "qT"), V(Sdb[:, l, hs], ("Sdbf_ret", l)),
                                start=False, stop=True)
                for h in range(4):
                    hs = slice(h * 128, (h + 1) * 128)
                    self.mm(V(self.ps[:, 7, hs], ("ps", 7)), V(kr[:, hs], "kr"), V(vbf[:, hs], "vbf"))
                if first:
                    self.cp("dve", Sl, self.psb(7))
                else:
                    for h in range(4):
                        hs = slice(h * 128, (h + 1) * 128)
                        self.stt(V(self.S_ret[:, l, hs], ("S_ret", l)), V(self.S_ret[:, l, hs], ("S_ret", l)), RET_DEC[h],
                                 V(self.ps[:, 7, hs], ("ps", 7)), ALU.mult, ALU.add)
                if t == self.ntp - 1:
                    self.dma("sp", V(self.dram["ret_p"][l].rearrange("h d e -> d h e")),
                             V(self.S_ret[:, l, :].rearrange("p (h e) -> p h e", h=4), ("S_ret", l)), out_dma=True)
                else:
                    decb = V(self.ap(c["rdec"], 8, [[1, 4], [0, 128]]), "k_rdec")
                    self.tt("dve", V(Sdb[:, l, :].rearrange("p (h e) -> p h e", h=4), ("Sdbf_ret", l)),
                            V(self.S_ret[:, l, :].rearrange("p (h e) -> p h e", h=4), ("S_ret", l)), decb, ALU.mult)
            else:
                self.sample_states(l, "st_ret", "ret_s", qT, kr, vbf, attm, 6,
                                   lambda h, dst, src: self.ts("dve", dst, src, V(c["rdec"][:, 1, 2, h:h + 1], "k_rdec"), ALU.mult))
            if li + 1 < len(tiles):
                self.proj_tok(li + 1, slots, [0, 1, 2, 3])
            ov = V(self.ps[:, 6, :].rearrange("p (h e) -> p h e", h=4), ("ps", 6))
            self.headnorm(ov, V(sil[:, :], "sil"), li, center=True)
        self.release(m0)

    def sample_states(self, l, st_in, st_out, qT, kk, vv, attm, obank, decay_fn, post_fn=None, nm=("qT", "kr", "vbf", "attm")):
        SBN = 4
        m0 = self.mark()
        sb = lambda n, shp, dt=F32: self.sb(n, shp, dt, scratch=True)
        qTblk = sb("qTblk", [128, NSEQ, 128], BF16)
        vblk = sb("vblk", [128, NSEQ, 128], BF16)
        sst, sso = [sb(n, [128, 2, SBN * 128]) for n in ("sst", "sso")]
        ssd = sb("ssd_", [128, 2, SBN * 128]) if decay_fn is not None else sst
        if decay_fn is None:
            self.alias["ssd_"] = self.alias["sst"]
        ssb = sb("ssb", [128, 2, SBN * 128], BF16)
        for nm_ in ("sst", "sso", "ssb") + (("ssd_",) if decay_fn is not None else ()):
            g = self.alias[nm_]
            hlf = len(g) // 2
            self.alias[nm_ + "0"] = g[:hlf]
            self.alias[nm_ + "1"] = g[hlf:]
        if decay_fn is None:
            self.alias["ssd_0"], self.alias["ssd_1"] = self.alias["sst0"], self.alias["sst1"]
        self.memset("pool", V(qTblk[:, :, :], "qTblk"), 0.0)
        c = self.c
        it = 0
        steps = [(h, sg) for h in range(4) for sg in range(0, NSEQ, SBN)]

        def load(step):
            hh, sgg = steps[step]
            self.dma("sp", V(sst[:, step % 2, :].rearrange("p (s e) -> p s e", s=SBN), "sst%d" % (step % 2)),
                     V(self.dram[st_in][l, sgg:sgg + SBN, hh].rearrange("s d e -> d s e")))
        load(0)
        for h in range(4):
            hs = slice(h * 128, (h + 1) * 128)
            dst = self.ap(qTblk, 0, [[128 + DSEQ, NSEQ], [1, DSEQ]])
            src = self.ap(qT, h * 128, [[DSEQ, NSEQ], [1, DSEQ]])
            self.cp("dve", V(dst, "qTblk"), V(src, nm[0]))
            self.tt("dve", V(vblk[:, :, :], "vblk"), V(self.ap(vv, h * 128, [[0, NSEQ], [1, 128]]), nm[2]),
                    V(self.ap(c["seqsel"], 0, [[1, NSEQ], [0, 128]]), "k_seqsel"), ALU.mult)
            self.mm(V(self.ps[:, obank, hs], ("ps", obank)), V(attm[:, hs], nm[3]), V(vv[:, hs], nm[2]), start=True, stop=False)
            for sg in range(0, NSEQ, SBN):
                i2 = it % 2
                it += 1
                if it < len(steps):
                    load(it)
                if decay_fn is not None:
                    sd = V(ssd[:, i2, :], "ssd_%d" % i2)
                    decay_fn(h, sd, V(sst[:, i2, :], "sst%d" % i2))
                else:
                    sd = V(sst[:, i2, :], "sst%d" % i2)
                self.cp("act", V(ssb[:, i2, :], "ssb%d" % i2), sd)
                for s in range(SBN):
                    self.mm(V(self.ps[:, obank, hs], ("ps", obank)), V(qTblk[:, sg + s, :], "qTblk"),
                            V(ssb[:, i2, s * 128:(s + 1) * 128], "ssb%d" % i2), start=False, stop=(sg + s == NSEQ - 1))
                pb = i2
                for s in range(SBN):
                    self.mm(V(self.ps[:, pb, s * 128:(s + 1) * 128], ("ps", pb)), V(kk[:, hs], nm[1]), V(vblk[:, sg + s, :], "vblk"))
                so = V(sso[:, i2, :], "sso%d" % i2)
                self.tt("dve", so, self.psb(pb), sd, ALU.add)
                if post_fn is not None:
                    post_fn(h, sg, SBN, V(sso[:, i2, :].rearrange("p (s e) -> p s e", s=SBN), "sso%d" % i2))
                self.dma("sp", V(self.dram[st_out][l, sg:sg + SBN, h].rearrange("s d e -> d s e")),
                         V(sso[:, i2, :].rearrange("p (s e) -> p s e", s=SBN), "sso%d" % i2), out_dma=True)
        self.release(m0)

    def norm_scratch(self, reuse=None):
        sb = lambda n, shp, dt=F32: self.sb(n, shp, dt, scratch=True)
        if reuse is not None:
            self.sqb = reuse[0]
            self.alias["sqb"] = self.alias[reuse[1]]
        else:
            self.sqb = sb("sqb", [128, 512])
        self.onb = sb("onb", [128, 512])
        self.obf = sb("obf", [128, 512], BF16)

    def headnorm(self, ov, mulv, li, center, nh=4, hd=128):
        si = self.stat()
        sk = ("st", si)
        sq = V(self.sqb[:, :].rearrange("p (h e) -> p h e", h=nh), "sqb")
        self.act(sq, ov, AF.Square)
        ssq = V(self.st[:, si, 4:4 + nh], sk)
        self.red(ssq, sq)
        var = V(self.st[:, si, 8:8 + nh], sk)
        if center:
            sm = V(self.st[:, si, 0:nh], sk)
            self.red(sm, ov)
            self.ts("dve", sm, sm, 1.0 / hd, ALU.mult)
            msq = V(self.st[:, si, 12:12 + nh], sk)
            self.tt("dve", msq, sm, sm, ALU.mult)
            self.ts("dve", msq, msq, -EPS, ALU.add)
            self.stt(var, ssq, 1.0 / hd, msq, ALU.mult, ALU.subtract)
        else:
            self.ts("dve", var, ssq, 1.0 / hd, ALU.mult, EPS, ALU.add)
        self.tt("pool", var, var, V(self.c["nhalf"][:, 0:nh], "k_nhalf"), ALU.pow)
        on = self.onb
        self.split_alias("onb", nh)
        if center:
            nb = V(self.st[:, si, 12:12 + nh], sk)
            self.stt(nb, V(self.st[:, si, 0:nh], sk), -1.0, var, ALU.mult, ALU.mult)
        for h in range(nh):
            o_h = V(ov.ap[:, h, :], *ov.keys)
            dst = V(on[:, h * hd:(h + 1) * hd], "onb%d" % h)
            if center:
                self.act(dst, o_h, AF.Identity, bias=V(self.st[:, si, 12 + h:13 + h], sk), scale=V(self.st[:, si, 8 + h:9 + h], sk))
            else:
                self.act(dst, o_h, AF.Copy, scale=V(self.st[:, si, 8 + h:9 + h], sk))
        self.tt("dve", V(self.obf[:, :], "obf"), V(on[:, :], "onb"), mulv, ALU.mult)
        self.o_to_oT(li)

    def o_to_oT(self, li):
        pT = self.ps[:, 4, :].bitcast(BF16)
        for k in range(4):
            self.tr(V(pT[:, k * 128:(k + 1) * 128], ("ps", 4)), V(self.obf[:, k * 128:(k + 1) * 128], "obf"),
                    V(self.c["ident_bf"][:, :], "k_ident_bf"))
        dst = self.ap(self.oT, li * 128, [[self.smax * 128, 4], [1, 128]])
        self.cp("act", V(dst, ("oT", li)), V(pT[:, 0:512].rearrange("p (k t) -> p k t", k=4), ("ps", 4)))


    def phase_hg(self, l, tiles):
        c = self.c
        L = self.depth
        m0 = self.mark()
        sb = lambda n, shp, dt=F32: self.sb(n, shp, dt, scratch=True)
        qh, kh, ivb, qT, kT, attm = [sb(n, [128, 512], BF16) for n in ("hqh", "hkh", "hiv", "hqT", "hkT", "hattm")]
        lb, oml, sg, uu, kf, logf, eG, mulv = [sb(n, [128, 512]) for n in ("hlb", "homl", "hsg", "huu", "hkf", "hlogf", "heG", "hmulv")]
        hgn = sb("hgn", [128, 128])
        scal = sb("hscal", [128, 4, NSEQ])
        smid = sb("hsmid", [128, 512], BF16) if any(t != self.ntp for t in tiles) else None
        ptmp = sb("hptmp", [128, 512]) if any(t != self.ntp for t in tiles) else None
        self.norm_scratch(reuse=(sg, "hsg"))
        if l == 0:
            self.memset("pool", V(lb[:, :], "hlb"), 0.0)
        else:
            mlb = self.mark()
            lg = sb("hlg", [128, L, 512])
            mx = sb("hmx", [128, 512])
            den = sb("hden", [128, 512])
            self.dma("sp", V(lg[:, :, :], "hlg"), V(self.dram["hg_lb_logits"][:, :].partition_broadcast(128)))
            self.tt("dve", V(mx[:, :], "hmx"), V(lg[:, 0, :], "hlg"), V(lg[:, 1, :], "hlg"), ALU.max)
            for i in range(2, L):
                self.tt("dve", V(mx[:, :], "hmx"), V(mx[:, :], "hmx"), V(lg[:, i, :], "hlg"), ALU.max)
            for i in range(L):
                self.tt("dve", V(lg[:, i, :], "hlg"), V(lg[:, i, :], "hlg"), V(mx[:, :], "hmx"), ALU.subtract)
            self.act(V(lg[:, :, :], "hlg"), V(lg[:, :, :], "hlg"), AF.Exp)
            self.tt("dve", V(den[:, :], "hden"), V(lg[:, 0, :], "hlg"), V(lg[:, 1, :], "hlg"), ALU.add)
            for i in range(2, L):
                self.tt("dve", V(den[:, :], "hden"), V(den[:, :], "hden"), V(lg[:, i, :], "hlg"), ALU.add)
            self.recip(V(den[:, :], "hden"), V(den[:, :], "hden"))
            self.cp("dve", V(lb[:, :], "hlb"), V(lg[:, 1, :], "hlg"))
            for i in range(2, l + 1):
                self.tt("dve", V(lb[:, :], "hlb"), V(lb[:, :], "hlb"), V(lg[:, i, :], "hlg"), ALU.add)
            self.tt("dve", V(lb[:, :], "hlb"), V(lb[:, :], "hlb"), V(den[:, :], "hden"), ALU.mult)
            self.release(mlb)
        self.ts("dve", V(oml[:, :], "homl"), V(lb[:, :], "hlb"), -1.0, ALU.mult, 1.0, ALU.add)
        self.bcast_load(hgn[:, :], "hgn", self.dram["hg_norm"][l:l + 1, :])
        self.memset("pool", V(attm[:, :], "hattm"), 0.0)
        slots = [self.win_chunk(l, c0) for c0 in (C_HQ, C_HF, C_HI, C_HG)]
        hoist_s5 = ("s5" in self.branches) and any(t != self.ntp for t in tiles)
        if hoist_s5:
            self.s5_prep_dma(l)
        Sl = self.S_hg
        self.proj_tok(0, slots, [0, 1, 2, 3])
        for li, t in enumerate(tiles):
            samp = (t == self.ntp)
            kind = 1 if samp else 0
            first = (t == 0)
            self.act(V(sg[:, :], "hsg"), self.psb(1), AF.Sigmoid)
            self.act(V(mulv[:, :], "hmulv"), self.psb(3), AF.Sigmoid)
            self.tt("dve", V(mulv[:, :].rearrange("p (h e) -> p h e", h=4), "hmulv"),
                    V(mulv[:, :].rearrange("p (h e) -> p h e", h=4), "hmulv"),
                    V(self.ap(hgn, 0, [[0, 4], [1, 128]]), "hgn"), ALU.mult)
            self.tt("dve", V(uu[:, :], "huu"), V(sg[:, :], "hsg"), V(oml[:, :], "homl"), ALU.mult)
            self.tt("dve", V(kf[:, :], "hkf"), V(oml[:, :], "homl"), V(uu[:, :], "huu"), ALU.subtract)
            self.tt("dve", V(uu[:, :], "huu"), V(uu[:, :], "huu"), V(lb[:, :], "hlb"), ALU.add)
            self.act(V(logf[:, :], "hlogf"), V(uu[:, :], "huu"), AF.Ln)
            self.mm(self.psb(5), V(c["tric"][:, kind, :], "k_tric"), V(logf[:, :], "hlogf"))
            ncol = NSEQ if samp else 3
            selv = V(c["seqsel"][:, :], "k_seqsel") if samp else V(c["sel3"][:, :], "k_sel3")
            for h in range(4):
                self.mm(V(self.ps[:, 7, h * NSEQ:h * NSEQ + ncol], ("ps", 7)), V(logf[:, h * 128:(h + 1) * 128], "hlogf"), selv)
            sc_ps = self.ps[:, 7, 0:4 * NSEQ].rearrange("p (h n) -> p h n", h=4)
            self.act(V(scal[:, :, 0:ncol], "hscal"), V(sc_ps[:, :, 0:ncol], ("ps", 7)), AF.Exp)
            self.act(V(eG[:, :], "heG"), self.psb(5), AF.Exp)
            self.act(V(uu[:, :], "huu"), self.psb(5), AF.Exp, scale=-1.0)
            self.act(V(sg[:, :], "hsg"), self.psb(0), AF.Silu)
            self.tt("dve", V(qh[:, :], "hqh"), V(sg[:, :], "hsg"), V(eG[:, :], "heG"), ALU.mult)
            self.tt("dve", V(kh[:, :], "hkh"), V(kf[:, :], "hkf"), V(uu[:, :], "huu"), ALU.mult)
            self.cp("act", V(ivb[:, :], "hiv"), self.psb(2))
            self.transpose_to([(qh, "hqh"), (kh, "hkh")], 4, [(qT, "hqT"), (kT, "hkT")])
            for h in range(4):
                hs = slice(h * 128, (h + 1) * 128)
                self.mm(V(self.ps[:, 5, hs], ("ps", 5)), V(kT[:, hs], "hkT"), V(qT[:, hs], "hqT"))
            if samp:
                mk = V(self.ap(c["maskT"], 128, [[0, 4], [1, 128]]), "k_maskT")
                self.tt("dve", V(attm[:, :].rearrange("p (h i) -> p h i", h=4), "hattm"),
                        V(self.ps[:, 5, :].rearrange("p (h i) -> p h i", h=4), ("ps", 5)), mk, ALU.mult)
            else:
                a3 = attm[:, :].rearrange("p (h i) -> p h i", h=4)
                p3 = self.ps[:, 5, :].rearrange("p (h i) -> p h i", h=4)
                m3 = c["maskT"][:, 0, :]
                self.tt("dve", V(a3[0:64, :, :], "hattm"), V(p3[0:64, :, :], ("ps", 5)),
                        V(bass.AP(c["maskT"], 0, [[256, 64], [0, 4], [1, 128]]), "k_maskT"), ALU.mult)
                self.tt("dve", V(a3[64:128, :, 64:128], "hattm"), V(p3[64:128, :, 64:128], ("ps", 5)),
                        V(bass.AP(c["maskT"], 64 * 256 + 64, [[256, 64], [0, 4], [1, 64]]), "k_maskT"), ALU.mult)
            if not samp:
                if not first:
                    for h in range(4):
                        hs = slice(h * 128, (h + 1) * 128)
                        self.ts("dve", V(smid[:, hs], "hsmid"), V(Sl[:, l, hs], ("S_hg", l)), V(scal[:, h, 0:1], "hscal"), ALU.mult)
                for h in range(4):
                    hs = slice(h * 128, (h + 1) * 128)
                    self.mm(V(self.ps[:, 6, hs], ("ps", 6)), V(attm[:, hs], "hattm"), V(ivb[:, hs], "hiv"), start=True, stop=first)
                    if not first:
                        self.mm(V(self.ps[:, 6, hs], ("ps", 6)), V(qT[:, hs], "hqT"), V(smid[:, hs], "hsmid"), start=False, stop=True)
                for h in range(4):
                    hs = slice(h * 128, (h + 1) * 128)
                    self.mm(V(self.ps[:, 7, hs], ("ps", 7)), V(kh[:, hs], "hkh"), V(ivb[:, hs], "hiv"))
                for h in range(4):
                    hs = slice(h * 128, (h + 1) * 128)
                    if first:
                        self.ts("dve", V(Sl[:, l, hs], ("S_hg", l)), V(self.ps[:, 7, hs], ("ps", 7)), V(scal[:, h, 2:3], "hscal"), ALU.mult)
                    else:
                        self.ts("dve", V(ptmp[:, hs], "hptmp"), V(self.ps[:, 7, hs], ("ps", 7)), V(scal[:, h, 2:3], "hscal"), ALU.mult)
                        self.stt(V(Sl[:, l, hs], ("S_hg", l)), V(Sl[:, l, hs], ("S_hg", l)), V(scal[:, h, 1:2], "hscal"),
                                 V(ptmp[:, hs], "hptmp"), ALU.mult, ALU.add)
                if t == self.ntp - 1:
                    self.dma("sp", V(self.dram["hg_p"][l].rearrange("h d e -> d h e")),
                             V(Sl[:, l, :].rearrange("p (h e) -> p h e", h=4), ("S_hg", l)), out_dma=True)
            else:
                def post(h, sg, nb, so3):
                    self.tt("dve", so3, so3, V(self.ap(scal, h * NSEQ + sg, [[1, nb], [0, 128]]), "hscal"), ALU.mult)
                self.sample_states(l, "st_hg", "hg_s", qT, kh, ivb, attm, 6, None, post_fn=post,
                                   nm=("hqT", "hkh", "hiv", "hattm"))
            if li + 1 < len(tiles):
                self.proj_tok(li + 1, slots, [0, 1, 2, 3])
            ov = V(self.ps[:, 6, :].rearrange("p (h e) -> p h e", h=4), ("ps", 6))
            self.headnorm(ov, V(mulv[:, :], "hmulv"), li, center=False)
        if hoist_s5:
            self.s5_prep_compute()
        self.release(m0)


    def phase_ssd(self, l, tiles):
        c = self.c
        m0 = self.mark()
        sb = lambda n, shp, dt=F32: self.sb(n, shp, dt, scratch=True)
        idf = V(c["ident_f"][:, :], "k_ident_f")
        idb = V(c["ident_bf"][:, :], "k_ident_bf")
        cw = sb("s_cw", [128, 8, 5])
        abd = sb("s_abd", [128, 3, 8])
        gno = sb("s_gno", [128, 512])
        self.norm_scratch()
        mw = self.mark()
        wrow = sb("s_wrow", [5, 1024])
        self.dma("sp", V(wrow[0:4, :], "s_wrow"), V(self.dram["ssd_conv_w"][l]))
        self.dma("sp", V(wrow[4:5, :], "s_wrow"), V(self.dram["ssd_conv_b"][l:l + 1, :]))
        for cb in range(8):
            self.tr(V(self.ps[:, 1, cb * 5:cb * 5 + 5], ("ps", 1)), V(wrow[0:5, cb * 128:(cb + 1) * 128], "s_wrow"),
                    V(c["ident_f"][0:5, 0:5], "k_ident_f"))
        self.cp("dve", V(cw[:, :, :], "s_cw"), V(self.ps[:, 1, 0:40].rearrange("p (a b) -> p a b", a=8), ("ps", 1)))
        self.release(mw)
        for i, nm in enumerate(("ssd_a_log", "ssd_dt_bias", "ssd_d")):
            self.bcast_load(abd[:, i, :], "s_abd", self.dram[nm][l:l + 1, :])
        self.act(V(abd[:, 0, :], "s_abd"), V(abd[:, 0, :], "s_abd"), AF.Exp)
        self.ts("dve", V(abd[:, 0, :], "s_abd"), V(abd[:, 0, :], "s_abd"), -1.0, ALU.mult)
        self.bcast_load(gno[:, :], "s_gno", self.dram["ssd_norm"][l:l + 1, :])
        sz_slot = self.win_chunk(l, C_SZ)
        xb_slots = [self.win_chunk(l, C_SXBC + i * 512) for i in range(2)]
        dt_slot = self.wload(("win", l, C_SDT, 8))
        ST, STb, XC = self.ST_ssd, self.STbf_ssd, self.xcarry
        for li, t in enumerate(tiles):
            samp = (t == self.ntp)
            kind = 1 if samp else 0
            first = (t == 0)
            nseq, T = (NSEQ, DSEQ) if samp else (1, 128)
            W = 3 + T
            m1 = self.mark()
            xpre = sb("s_xpre", [128, 8, nseq * W])
            xact = sb("s_xact", [128, 8, 128])
            tailc = sb("s_tail", [128, 8, nseq * 3])
            bcb = sb("s_bcb", [128, 4, 128], BF16)
            bmT = sb("s_bmT", [128, 256], BF16)
            sm = sb("s_sm", [128, 64])
            wm = sb("s_wm", [128, 1024], BF16)
            xdt = sb("s_xdt", [128, 512], BF16)
            xdw = sb("s_xdw", [128, 512], BF16)
            xsD = sb("s_xsD", [128, 512])
            ysl = sb("s_ysl", [128, 512])
            yy = sb("s_yy", [128, 512])
            m2 = self.mark()
            dAtri = sb("s_dAtri", [128, 1024])
            Lm = dAtri
            self.alias["s_L"] = self.alias["s_dAtri"]
            self.release(m2)
            hk = ("hT", li)
            self.proj_tok(li, [sz_slot], [0])
            for kt in range(8):
                self.mm(V(self.ps[:, 1, 0:8], ("ps", 1)), V(self.hT[:, kt, li * 128:(li + 1) * 128], hk),
                        self.wv(dt_slot, 8, 8, kt), start=(kt == 0), stop=(kt == 7))
            for cb in range(8):
                bk = 2 + cb // 4
                for kt in range(8):
                    self.mm(V(self.ps[:, bk, (cb % 4) * 128:(cb % 4 + 1) * 128], ("ps", bk)),
                            self.wv(xb_slots[cb // 4], 8, 512, kt, (cb % 4) * 128, 128),
                            V(self.hT[:, kt, li * 128:(li + 1) * 128], hk), start=(kt == 0), stop=(kt == 7))
            x4 = xpre[:, :, :].rearrange("p c (s w) -> p c s w", s=nseq)
            src = self.ps[:, 2:4, :].rearrange("p a (b s t) -> p (a b) s t", b=4, s=nseq)
            self.cp("act", V(x4[:, :, :, 3:W], "s_xpre"), V(src, ("ps", 2), ("ps", 3)))
            if samp:
                mh = self.mark()
                hist = sb("s_hist", [NSEQ * 3, 1024])
                self.dma("sp", V(hist[:, :], "s_hist"), V(self.dram["st_conv"][l].rearrange("s j c -> (s j) c")))
                for cb in range(8):
                    self.tr(V(self.ps[:, 5, cb * 48:(cb + 1) * 48], ("ps", 5)), V(hist[:, cb * 128:(cb + 1) * 128], "s_hist"),
                            V(c["ident_f"][0:48, 0:48], "k_ident_f"))
                self.cp("dve", V(x4[:, :, :, 0:3], "s_xpre"),
                        V(self.ps[:, 5, 0:384].rearrange("p (c s j) -> p c s j", c=8, s=NSEQ), ("ps", 5)))
                self.release(mh)
            elif first:
                self.memset("pool", V(x4[:, :, :, 0:3], "s_xpre"), 0.0)
            else:
                self.cp("dve", V(x4[:, :, 0, 0:3], "s_xpre"), V(XC[:, l, :, :], ("xcarry", l)))
            t4 = tailc[:, :, :].rearrange("p c (s j) -> p c s j", s=nseq)
            self.cp("dve", V(t4, "s_tail"), V(x4[:, :, :, T:W], "s_xpre"))
            if not samp:
                self.cp("dve", V(XC[:, l, :, :], ("xcarry", l)), V(tailc[:, :, :], "s_tail"))
            if samp or t == self.ntp - 1:
                nr = nseq * 3
                mh = self.mark()
                tout = sb("s_tout", [nr, 1024])
                for cb in range(8):
                    bk = 6 + cb // 4
                    self.tr(V(self.ps[0:nr, bk, (cb % 4) * 128:(cb % 4 + 1) * 128], ("ps", bk)), V(tailc[:, cb, :], "s_tail"), idf)
                self.cp("act", V(tout[:, :].rearrange("p (a b) -> p a b", a=2), "s_tout"), V(self.ps[0:nr, 6:8, :], ("ps", 6), ("ps", 7)))
                dst = self.dram["conv_s"][l].rearrange("s j c -> (s j) c") if samp else self.dram["conv_p"][l]
                self.dma("sp", V(dst), V(tout[:, :], "s_tout"), out_dma=True)
                self.release(mh)
            self.split_alias("s_xact", 8)
            for cb in range(8):
                a4 = xact[:, cb, :].rearrange("p (s t) -> p s t", s=nseq)
                xin = x4[:, cb, :, :]
                self.act(V(a4, "s_xact%d" % cb), V(xin[:, :, 0:T], "s_xpre"), AF.Identity,
                         bias=V(cw[:, cb, 4:5], "s_cw"), scale=V(cw[:, cb, 0:1], "s_cw"))
            for cb in range(8):
                a4 = xact[:, cb, :].rearrange("p (s t) -> p s t", s=nseq)
                xin = x4[:, cb, :, :]
                for j in range(1, 4):
                    self.stt(V(a4, "s_xact%d" % cb), V(xin[:, :, j:j + T], "s_xpre"), V(cw[:, cb, j:j + 1], "s_cw"),
                             V(a4, "s_xact%d" % cb), ALU.mult, ALU.add)
            self.act(V(xact[:, :, :], "s_xact"), V(xact[:, :, :], "s_xact"), AF.Silu)
            self.act(V(ysl[:, :], "s_ysl"), self.psb(0), AF.Silu)
            self.cp("dve", V(bcb[:, :, :], "s_bcb"), V(xact[:, 4:8, :], "s_xact"))
            for cb in range(4):
                self.tr(V(self.ps[:, 4, cb * 128:(cb + 1) * 128], ("ps", 4)), V(xact[:, cb, :], "s_xact"), idf)
            pT5 = self.ps[:, 5, 256:384].bitcast(BF16)
            for g in range(2):
                self.tr(V(pT5[:, g * 128:(g + 1) * 128], ("ps", 5)), V(bcb[:, g, :], "s_bcb"), idb)
            self.cp("act", V(bmT[:, :], "s_bmT"), V(pT5, ("ps", 5)))
            dt = V(sm[:, 0:8], "s_sm"); dA = V(sm[:, 8:16], "s_sm"); E = V(sm[:, 16:24], "s_sm")
            dend = V(sm[:, 24:32], "s_sm"); etot = V(sm[:, 32:40], "s_sm"); cum = V(sm[:, 40:48], "s_sm")
            self.tt("dve", dt, V(self.ps[:, 1, 0:8], ("ps", 1)), V(abd[:, 1, :], "s_abd"), ALU.add)
            self.act(dt, dt, AF.Exp)
            self.act(dt, dt, AF.Ln, bias=1.0)
            self.tt("dve", dA, dt, V(abd[:, 0, :], "s_abd"), ALU.mult)
            self.mm(V(self.ps[:, 1, 8:16], ("ps", 1)), V(c["trif"][:, kind, :], "k_trif"), dA)
            self.mm(V(self.ps[:, 1, 16:24], ("ps", 1)), V(c["blk"][:, kind, :], "k_blk"), dA)
            self.cp("dve", cum, V(self.ps[:, 1, 8:16], ("ps", 1)))
            self.act(E, cum, AF.Exp)
            self.tt("dve", dend, V(self.ps[:, 1, 16:24], ("ps", 1)), cum, ALU.subtract)
            self.act(dend, dend, AF.Exp)
            self.act(etot, V(self.ps[:, 1, 16:24], ("ps", 1)), AF.Exp)
            self.tt("dve", V(dAtri[:, :].rearrange("p (h i) -> p h i", h=8), "s_dAtri"),
                    V(self.ap(c["trif"], kind * 128, [[0, 8], [1, 128]]), "k_trif"),
                    V(self.ap(sm, 8, [[1, 8], [0, 128]]), "s_sm"), ALU.mult)
            for hf in range(2):
                bk = 2 + hf
                self.mm(self.psb(bk), V(c["blk"][:, 0, :], "k_blk"), V(dAtri[:, hf * 512:(hf + 1) * 512], "s_dAtri"),
                        start=True, stop=False)
                self.mm(self.psb(bk), V(c["ntrif"][:, kind, :], "k_ntrif"),
                        V(self.ap(sm, 8 + hf * 4, [[1, 4], [0, 128]]), "s_sm"), start=False, stop=False)
                self.mm(self.psb(bk), idf, V(self.ap(c["negm"], kind * 128, [[0, 4], [1, 128]]), "k_negm"),
                        start=False, stop=True)
            self.act(V(Lm[:, :].rearrange("p (a b) -> p a b", a=2), "s_L"), V(self.ps[:, 2:4, :], ("ps", 2), ("ps", 3)), AF.Exp)
            for g in range(2):
                self.mm(V(self.ps[:, 5, g * 128:(g + 1) * 128], ("ps", 5)), V(bcb[:, g, :], "s_bcb"), V(bcb[:, 2 + g, :], "s_bcb"))
            self.tt("dve", V(wm[:, :].rearrange("p (g r i) -> p g r i", g=2, r=4), "s_wm"),
                    V(Lm[:, :].rearrange("p (g r i) -> p g r i", g=2, r=4), "s_L"),
                    V(self.ap(self.ps, 5 * 512, [[128, 2], [0, 4], [1, 128]]), ("ps", 5)), ALU.mult)
            xs3 = V(self.ps[:, 4, :].rearrange("p (h q) -> p h q", h=8), ("ps", 4))
            bc = lambda o: V(self.ap(sm, o, [[1, 8], [0, 64]]), "s_sm")
            self.tt("dve", V(xdt[:, :].rearrange("p (h q) -> p h q", h=8), "s_xdt"), xs3, bc(0), ALU.mult)
            self.tt("dve", V(xdw[:, :].rearrange("p (h q) -> p h q", h=8), "s_xdw"),
                    V(xdt[:, :].rearrange("p (h q) -> p h q", h=8), "s_xdt"), bc(24), ALU.mult)
            self.tt("dve", V(xsD[:, :].rearrange("p (h q) -> p h q", h=8), "s_xsD"), xs3,
                    V(self.ap(abd, 16, [[1, 8], [0, 64]]), "s_abd"), ALU.mult)
            for h in range(8):
                self.mm(V(self.ps[:, 6, h * 64:(h + 1) * 64], ("ps", 6)), V(wm[:, h * 128:(h + 1) * 128], "s_wm"),
                        V(xdt[:, h * 64:(h + 1) * 64], "s_xdt"))
            et3 = V(self.ap(sm, 32, [[1, 8], [0, 64]]), "s_sm")
            have_inter = samp or not first
            if not samp:
                if not first:
                    for g in range(2):
                        self.mm(V(self.ps[:, 7, g * 256:(g + 1) * 256], ("ps", 7)), V(bcb[:, 2 + g, :], "s_bcb"),
                                V(STb[:, l, g * 256:(g + 1) * 256], ("STbf_ssd", l)))
                for g in range(2):
                    self.mm(V(self.ps[:, 2, g * 256:(g + 1) * 256], ("ps", 2)), V(bmT[:, g * 128:(g + 1) * 128], "s_bmT"),
                            V(xdw[:, g * 256:(g + 1) * 256], "s_xdw"))
                Sl = V(ST[:, l, :], ("ST_ssd", l))
                if first:
                    self.cp("dve", Sl, self.psb(2))
                else:
                    self.tt("dve", V(ST[:, l, :].rearrange("p (h q) -> p h q", h=8), ("ST_ssd", l)),
                            V(ST[:, l, :].rearrange("p (h q) -> p h q", h=8), ("ST_ssd", l)), et3, ALU.mult)
                    self.tt("dve", Sl, self.psb(2), Sl, ALU.add)
                inter_v = [V(self.ps[:, 7, :].rearrange("p (h q) -> p h q", h=8), ("ps", 7))]
            else:
                self.top = max(self.top, m2 + 4096 + 64)
                inter_v = self.ssd_sample(l, bcb, bmT, xdw, sm)
            y3 = V(yy[:, :].rearrange("p (h q) -> p h q", h=8), "s_yy")
            if have_inter:
                if len(inter_v) == 1:
                    self.tt("dve", y3, inter_v[0], bc(16), ALU.mult)
                else:
                    for g in range(2):
                        self.tt("dve", V(yy[:, g * 256:(g + 1) * 256].rearrange("p (h q) -> p h q", h=4), "s_yy"), inter_v[g],
                                V(self.ap(sm, 16 + g * 4, [[1, 4], [0, 64]]), "s_sm"), ALU.mult)
                self.tt("dve", V(yy[:, :], "s_yy"), self.psb(6), V(yy[:, :], "s_yy"), ALU.add)
                self.tt("dve", V(yy[:, :], "s_yy"), V(yy[:, :], "s_yy"), V(xsD[:, :], "s_xsD"), ALU.add)
            else:
                self.tt("dve", V(yy[:, :], "s_yy"), self.psb(6), V(xsD[:, :], "s_xsD"), ALU.add)
            self.tt("dve", V(yy[:, :], "s_yy"), V(yy[:, :], "s_yy"), V(ysl[:, :], "s_ysl"), ALU.mult)
            if (not samp) and t != self.ntp - 1:
                self.cp("act", V(STb[:, l, :], ("STbf_ssd", l)), V(ST[:, l, :], ("ST_ssd", l)))
            if (not samp) and t == self.ntp - 1:
                sto = sb("s_sto", [128, 4, 128])
                for k in range(4):
                    self.tr(V(self.ps[:, 3, k * 128:(k + 1) * 128], ("ps", 3)), V(ST[:, l, k * 128:(k + 1) * 128], ("ST_ssd", l)), idf)
                self.cp("act", V(sto[:, :, :], "s_sto"), V(self.ps[:, 3, :].rearrange("p (k n) -> p k n", k=4), ("ps", 3)))
                self.dma("sp", V(self.dram["ssd_p"][l].rearrange("h p n -> (h p) n").rearrange("(k q) n -> q k n", q=128)),
                         V(sto[:, :, :], "s_sto"), out_dma=True)
            self.headnorm(V(yy[:, :].rearrange("p (g e) -> p g e", g=2), "s_yy"), V(gno[:, :], "s_gno"), li, center=False, nh=2, hd=256)
            self.release(m1)
        self.release(m0)

    def ssd_sample(self, l, bcb, bmT, xdw, sm):
        c = self.c
        sb = lambda n, shp, dt=F32: self.sb(n, shp, dt, scratch=True)
        idf = V(c["ident_f"][:, :], "k_ident_f")
        dAs = sb("s_dAs", [128, NSEQ, 8])
        etS = sb("s_etS", [128, NSEQ, 8])
        sin2 = sb("s_sin", [128, 2, 512])
        g_ = self.alias["s_sin"]
        self.alias["s_sin0"], self.alias["s_sin1"] = g_[:len(g_) // 2], g_[len(g_) // 2:]

        def load_state(s_):
            self.dma("sp", V(sin2[:, s_ % 2, :].rearrange("q (k n) -> q k n", k=4), "s_sin%d" % (s_ % 2)),
                     V(self.dram["st_ssd"][l, s_].rearrange("h p n -> (h p) n").rearrange("(k q) n -> q k n", q=128)))
        load_state(0)
        sbf = sb("s_sbf", [128, 512], BF16)
        sdd = sb("s_sdd", [128, 512])
        sout = sb("s_sout", [128, 512])
        xdws = sb("s_xdws", [128, 512], BF16)
        yacc = sb("s_yacc", [128, 512])
        self.tt("dve", V(dAs[:, :, :], "s_dAs"), V(self.ap(c["seqsel"], 0, [[1, NSEQ], [0, 8]]), "k_seqsel"),
                V(self.ap(sm, 8, [[0, NSEQ], [1, 8]]), "s_sm"), ALU.mult)
        self.mm(V(self.ps[:, 1, 128:256], ("ps", 1)), V(c["blk"][:, 0, :], "k_blk"), V(dAs[:, :, :].rearrange("p s h -> p (s h)"), "s_dAs"))
        self.act(V(etS[:, :, :].rearrange("p s h -> p (s h)"), "s_etS"), V(self.ps[:, 1, 128:256], ("ps", 1)), AF.Exp)
        for s in range(NSEQ):
            sel = V(c["seqsel"][:, s:s + 1], "k_seqsel")
            if s + 1 < NSEQ:
                load_state(s + 1)
            for k in range(4):
                self.tr(V(self.ps[:, 3, k * 128:(k + 1) * 128], ("ps", 3)),
                        V(sin2[:, s % 2, k * 128:(k + 1) * 128], "s_sin%d" % (s % 2)), idf)
            self.cp("act", V(sbf[:, :], "s_sbf"), self.psb(3))
            self.tt("dve", V(sdd[:, :].rearrange("p (h q) -> p h q", h=8), "s_sdd"),
                    V(self.ps[:, 3, :].rearrange("p (h q) -> p h q", h=8), ("ps", 3)),
                    V(self.ap(etS, s * 8, [[1, 8], [0, 64]]), "s_etS"), ALU.mult)
            for g in range(2):
                self.mm(V(self.ps[:, 7, g * 256:(g + 1) * 256], ("ps", 7)), V(bcb[:, 2 + g, :], "s_bcb"),
                        V(sbf[:, g * 256:(g + 1) * 256], "s_sbf"))
            if s == 0:
                self.ts("dve", V(yacc[:, :], "s_yacc"), self.psb(7), sel, ALU.mult)
            else:
                self.stt(V(yacc[:, :], "s_yacc"), self.psb(7), sel, V(yacc[:, :], "s_yacc"), ALU.mult, ALU.add)
            self.ts("dve", V(xdws[:, :], "s_xdws"), V(xdw[:, :], "s_xdw"), sel, ALU.mult)
            for g in range(2):
                self.mm(V(self.ps[:, 2, g * 256:(g + 1) * 256], ("ps", 2)), V(bmT[:, g * 128:(g + 1) * 128], "s_bmT"),
                        V(xdws[:, g * 256:(g + 1) * 256], "s_xdws"))
            self.tt("dve", V(sdd[:, :], "s_sdd"), self.psb(2), V(sdd[:, :], "s_sdd"), ALU.add)
            for k in range(4):
                self.tr(V(self.ps[:, 4, k * 128:(k + 1) * 128], ("ps", 4)), V(sdd[:, k * 128:(k + 1) * 128], "s_sdd"), idf)
            self.cp("act", V(sout[:, :], "s_sout"), self.psb(4))
            self.dma("sp", V(self.dram["ssd_s"][l, s].rearrange("h p n -> (h p) n").rearrange("(k q) n -> q k n", q=128)),
                     V(sout[:, :].rearrange("q (k n) -> q k n", k=4), "s_sout"), out_dma=True)
        return [V(yacc[:, :].rearrange("p (h q) -> p h q", h=8), "s_yacc")]


    def sincos_tab(self, dst, ang, tmp, tmpi, shift):
        inv2pi = 1.0 / (2.0 * math.pi)
        self.ts("dve", tmp, ang, inv2pi, ALU.mult, shift, ALU.add)
        self.cp("dve", tmpi, tmp)
        self.cp("dve", dst, tmpi)
        self.tt("dve", tmp, tmp, dst, ALU.subtract)
        self.act(dst, tmp, AF.Sin, scale=2.0 * math.pi)

    def s5_prep_dma(self, l):
        keep = lambda nm, shp, dt=F32: self.sb_top(nm, shp, dt)
        pv = keep("z_pv", [128, 20, 16])
        BW = keep("z_BW", [128, 2, 16, 128], BF16)
        Cch = keep("z_Cch", [128, 2, 16, 32], BF16)
        self.hi_keep = self.hi
        sb = lambda nm, shp, dt=F32: self.sb_top(nm, shp, dt)
        prow = sb("z_prow", [16, 3, 128])
        ldt16 = sb("z_ldt16", [16, 2])
        pvi = sb("z_pvi", [128, 16], mybir.dt.int32)
        braw = sb("z_braw", [128, 2, 16, 16])
        bbt = sb("z_bbt", [128, 3, 16, 16])
        pad = sb("z_pad", [128, 16, 96], BF16)
        crow = sb("z_crow", [128, 4, 128])
        self.dma("sp", V(prow[:, 0, :], "z_prow"), V(self.dram["s5_a_re"][l].rearrange("(ct q) -> ct q", q=128)))
        self.dma("sp", V(prow[:, 1, :], "z_prow"), V(self.dram["s5_a_im"][l].rearrange("(ct q) -> ct q", q=128)))
        self.dma("sp", V(ldt16[:, :], "z_ldt16"), V(self.dram["s5_log_dt"][l].rearrange("(ct g) -> ct g", g=2)))
        self.dma("sp", V(braw[:, 0], "z_braw"), V(self.dram["s5_b_re"][l].rearrange("(ct q) m -> q ct m", q=128)))
        self.dma("sp", V(braw[:, 1], "z_braw"), V(self.dram["s5_b_im"][l].rearrange("(ct q) m -> q ct m", q=128)))
        for part, nm in enumerate(("s5_c_re", "s5_c_im")):
            for blk in range(2):
                for j in range(8):
                    g0 = (blk * 8 + j) * 2
                    self.dma("sp",
                             V(crow[j * 16:(j + 1) * 16, part * 2 + blk, :].rearrange("m (gl p) -> m gl p", gl=2), "z_crow"),
                             V(self.dram[nm][l, g0:g0 + 2].rearrange("gl m p -> m gl p")))
        self.s5p = dict(l=l, pv=pv, BW=BW, Cch=Cch, prow=prow, ldt16=ldt16, pvi=pvi, braw=braw, bbt=bbt, pad=pad, crow=crow, done=False)

    def s5_prep_compute(self):
        c = self.c
        sp = self.s5p
        pv, BW, Cch = sp["pv"], sp["BW"], sp["Cch"]
        prow, ldt16, pvi, braw, bbt, pad, crow = (sp[n] for n in ("prow", "ldt16", "pvi", "braw", "bbt", "pad", "crow"))
        idf = V(c["ident_f"][:, :], "k_ident_f")
        idb = V(c["ident_bf"][:, :], "k_ident_bf")
        P = lambda i: V(pv[:, i, :], "z_pv")
        A_RE, A_IM, LDT, DT, DRE, DIM, MAG, COS, SIN, ABR, ABI, DEN, NRE, FRE, FIM, T0, T1, T2 = range(18)
        self.cp("dve", V(prow[:, 2, :].rearrange("c (g p) -> c g p", g=2), "z_prow"),
                V(bass.AP(ldt16, 0, [[2, 16], [1, 2], [0, 64]]), "z_ldt16"))
        for i in range(3):
            self.tr(V(self.ps[:, 0, i * 16:(i + 1) * 16], ("ps", 0)), V(prow[:, i, :], "z_prow"), V(c["ident_f"][0:16, 0:16], "k_ident_f"))
        self.cp("dve", V(pv[:, 0:3, :], "z_pv"), V(self.ps[:, 0, 0:48].rearrange("p (a b) -> p a b", a=3), ("ps", 0)))
        self.act(P(DT), P(LDT), AF.Exp)
        self.tt("dve", P(DRE), P(DT), P(A_RE), ALU.mult)
        self.tt("dve", P(DIM), P(DT), P(A_IM), ALU.mult)
        self.act(P(MAG), P(DRE), AF.Exp)
        self.sincos_tab(P(SIN), P(DIM), P(T0), V(pvi[:, :], "z_pvi"), 0.0)
        self.sincos_tab(P(COS), P(DIM), P(T0), V(pvi[:, :], "z_pvi"), 0.25)
        self.tt("dve", P(ABR), P(MAG), P(COS), ALU.mult)
        self.tt("dve", P(ABI), P(MAG), P(SIN), ALU.mult)
        self.tt("dve", P(DEN), P(A_RE), P(A_RE), ALU.mult)
        self.tt("dve", P(T0), P(A_IM), P(A_IM), ALU.mult)
        self.tt("dve", P(DEN), P(DEN), P(T0), ALU.add)
        self.recip(P(DEN), P(DEN))
        self.ts("dve", P(NRE), P(ABR), -1.0, ALU.add)
        self.tt("dve", P(T0), P(NRE), P(A_RE), ALU.mult)
        self.tt("dve", P(T1), P(ABI), P(A_IM), ALU.mult)
        self.tt("dve", P(T0), P(T0), P(T1), ALU.add)
        self.tt("dve", P(FRE), P(T0), P(DEN), ALU.mult)
        self.tt("dve", P(T0), P(ABI), P(A_RE), ALU.mult)
        self.tt("dve", P(T1), P(NRE), P(A_IM), ALU.mult)
        self.tt("dve", P(T0), P(T0), P(T1), ALU.subtract)
        self.tt("dve", P(FIM), P(T0), P(DEN), ALU.mult)
        fb = lambda i: V(self.ap(pv, i * 16, [[1, 16], [0, 16]]), "z_pv")
        for part in range(2):
            x0, x1 = (0, 1) if part == 0 else (1, 0)
            self.tt("dve", V(bbt[:, 0], "z_bbt"), V(braw[:, x0], "z_braw"), fb(FRE), ALU.mult)
            self.tt("dve", V(bbt[:, 1], "z_bbt"), V(braw[:, x1], "z_braw"), fb(FIM), ALU.mult)
            self.tt("dve", V(bbt[:, 2], "z_bbt"), V(bbt[:, 0], "z_bbt"), V(bbt[:, 1], "z_bbt"),
                    ALU.subtract if part == 0 else ALU.add)
            self.memset("pool", V(pad[:, :, :], "z_pad"), 0.0)
            for r in range(3):
                for gl in range(2):
                    dst = bass.AP(pad, gl * 64 * (16 * 96) + r * 96 + r * 32 + gl * 16, [[16 * 96, 64], [288, 5], [1, 16]])
                    src = bass.AP(bbt, gl * 64 * (3 * 256) + 2 * 256 + r * 16, [[3 * 256, 64], [48, 5], [1, 16]])
                    self.cp("dve", V(dst, "z_pad"), V(src, "z_bbt"))
            for gl in range(2):
                self.cp("dve", V(pad[gl * 64:(gl + 1) * 64, 15, 64 + gl * 16:64 + gl * 16 + 16], "z_pad"),
                        V(bbt[gl * 64:(gl + 1) * 64, 2, 15, :], "z_bbt"))
            for half in range(2):
                pT = self.ps[:, 1 + half, :].bitcast(BF16)
                for kk in range(8):
                    ct = half * 8 + kk
                    self.tr(V(pT[0:96, kk * 128:(kk + 1) * 128], ("ps", 1 + half)), V(pad[:, ct, :], "z_pad"), idb)
                self.cp("act", V(BW[0:96, part, half * 8:(half + 1) * 8, :].rearrange("p a b -> p (a b)"), "z_BW"),
                        V(pT[0:96, :], ("ps", 1 + half)))
        self.memset("pool", V(Cch[:, :, :, :], "z_Cch"), 0.0)
        for part in range(2):
            for blk in range(2):
                self.tr(V(self.ps[:, 2, blk * 128:(blk + 1) * 128], ("ps", 2)), V(crow[:, part * 2 + blk, :], "z_crow"), idf)
            for gl in range(2):
                self.act(V(Cch[gl * 64:(gl + 1) * 64, part, :, gl * 16:(gl + 1) * 16], "z_Cch"),
                         V(self.ps[gl * 64:(gl + 1) * 64, 2, 0:256].rearrange("p (a b) -> p a b", a=16), ("ps", 2)),
                         AF.Copy, scale=(1.0 if part == 0 else -1.0))
        self.hi = self.hi_keep
        sp["done"] = True

    def phase_s5(self, l, tiles):
        c = self.c
        n = len(tiles)
        ntok = n * 128
        has_samp = self.ntp in tiles
        m0 = self.mark()
        sb = lambda nm, shp, dt=F32: self.sb(nm, shp, dt, scratch=True)
        idf = V(c["ident_f"][:, :], "k_ident_f")
        idb = V(c["ident_bf"][:, :], "k_ident_bf")
        I32 = mybir.dt.int32
        if getattr(self, "s5p", None) is None or self.s5p["l"] != l:
            self.s5_prep_dma(l)
        if not self.s5p["done"]:
            self.s5_prep_compute()
        pv, BW, Cch = self.s5p["pv"], self.s5p["BW"], self.s5p["Cch"]
        A_RE, A_IM, LDT, DT, DRE, DIM, MAG, COS, SIN, ABR, ABI, DEN, NRE, FRE, FIM, T0, T1, T2 = range(18)
        dtab = sb("z_dtab", [128, 512])
        self.bcast_load(dtab[:, :], "z_dtab", self.dram["s5_d"][l:l + 1, :])
        self.norm_scratch()
        su_slot = self.win_chunk(l, C_SU)
        glu_slot = self.wload(("glu", l))
        uT8 = sb("z_uT8", [128, 6, ntok], BF16)
        self.memset("pool", V(uT8[:, :, :], "z_uT8"), 0.0)
        for li in range(n):
            for hb_ in range(2):
                bk = 2 + hb_
                for pl in range(3):
                    w = hb_ * 3 + pl
                    ncol = 96
                    c0w = min(w * 96, 512 - 96)
                    for kt in range(8):
                        lhs = V(self.ap(self.wr, su_slot * 4096 + kt * 512 + c0w, [[1, ncol]]), ("wr", su_slot))
                        self.mm(V(self.ps[0:ncol, bk, pl * 128:(pl + 1) * 128], ("ps", bk)), lhs,
                                V(self.hT[:, kt, li * 128:(li + 1) * 128], ("hT", li)), start=(kt == 0), stop=(kt == 7))
                if hb_ == 0:
                    self.cp("act", V(uT8[0:96, 0:3, li * 128:(li + 1) * 128], "z_uT8"),
                            V(self.ps[0:96, bk, 0:384].rearrange("p (a b) -> p a b", a=3), ("ps", bk)))
                else:
                    self.cp("dve", V(uT8[0:96, 3:6, li * 128:(li + 1) * 128], "z_uT8"),
                            V(self.ps[0:96, bk, 0:384].rearrange("p (a b) -> p a b", a=3), ("ps", bk)))
        ytok = sb("z_ytok", [128, n, 512])
        if has_samp:
            hin = sb("z_hin", [128, 2, 16, NSEQ])
            hfin = sb("z_hfin", [128, 2, 16, NSEQ])
            mp = self.mark()
            srow = sb("z_srow", [NSEQ, 2048])
            for part, nm in enumerate(("st_s5re", "st_s5im")):
                self.dma("sp", V(srow[:, :], "z_srow"), V(self.dram[nm][l]))
                for ct in range(16):
                    self.tr(V(self.ps[:, 4, ct * 16:(ct + 1) * 16], ("ps", 4)), V(srow[:, ct * 128:(ct + 1) * 128], "z_srow"),
                            V(c["ident_f"][0:16, 0:16], "k_ident_f"))
                self.cp("dve", V(hin[:, part].rearrange("p a b -> p (a b)"), "z_hin"), V(self.ps[:, 4, 0:256], ("ps", 4)))
            self.release(mp)
        hst = self.h_s5
        tabs = sb("z_tabs", [128, 3, 512])
        rhos = sb("z_rhos", [128, 512]) if has_samp else None
        cin = sb("z_cin", [128, 4 * NSEQ])
        mscan = self.mark()
        ang = sb("z_ang", [128, 512])
        tmpf = sb("z_tmpf", [128, 512])
        tmpi = sb("z_tmpi", [128, 512], I32)
        self.release(mscan)
        bpre = sb("z_bpre", [128, 2, 512])
        gsc = sb("z_gsc", [128, 2, 512])
        hh = sb("z_hh", [128, 2, 512])
        hbf = sb("z_hbf", [128, 2, 512], BF16)
        t4 = sb("z_t4", [128, 2, 512])
        t5 = sb("z_t5", [128, 2, 512])
        for nm_ in ("z_t4", "z_t5", "z_bpre", "z_gsc"):
            self.split_alias(nm_)
        for part in range(2):
            g0 = self.alias["z_hh"]
            hlf = len(g0) // 2
            self.alias["z_hh%d" % part] = g0[part * hlf:(part + 1) * hlf]
            g1 = self.alias["z_hbf"]
            hl2 = len(g1) // 2
            self.alias["z_hbf%d" % part] = g1[part * hl2:(part + 1) * hl2]
        for quad in range(4):
            for ctl in range(4):
                ct = quad * 4 + ctl
                self.ts("dve", V(ang[:, ctl * 128:(ctl + 1) * 128], "z_ang"), V(c["t1"][:, :], "k_t1"), V(pv[:, DIM, ct:ct + 1], "z_pv"), ALU.mult)
                self.ts("dve", V(tabs[:, 2, ctl * 128:(ctl + 1) * 128], "z_tabs"), V(c["zmask"][:, 0, :], "k_zmask"),
                        V(pv[:, MAG, ct:ct + 1], "z_pv"), ALU.mult)
                if has_samp:
                    self.ts("dve", V(rhos[:, ctl * 128:(ctl + 1) * 128], "z_rhos"), V(c["zmask"][:, 1, :], "k_zmask"),
                            V(pv[:, MAG, ct:ct + 1], "z_pv"), ALU.mult)
            self.sincos_tab(V(tabs[:, 1, :], "z_tabs"), V(ang[:, :], "z_ang"), V(tmpf[:, :], "z_tmpf"), V(tmpi[:, :], "z_tmpi"), 0.0)
            self.sincos_tab(V(tabs[:, 0, :], "z_tabs"), V(ang[:, :], "z_ang"), V(tmpf[:, :], "z_tmpf"), V(tmpi[:, :], "z_tmpi"), 0.25)
            for li, t in enumerate(tiles):
                samp = (t == self.ntp)
                first = (t == 0)
                for part in range(2):
                    bk = 5 + part
                    for ctl in range(4):
                        ct = quad * 4 + ctl
                        w = min(ct // 3, 5)
                        self.mm(V(self.ps[:, bk, ctl * 128:(ctl + 1) * 128], ("ps", bk)),
                                V(BW[0:96, part, ct, :], "z_BW"), V(uT8[0:96, w, li * 128:(li + 1) * 128], "z_uT8"))
                if samp:
                    Ct = V(self.ap(tabs, 0, [[128, 4], [0, NSEQ], [1, DSEQ]]), "z_tabs")
                    St = V(self.ap(tabs, 512, [[128, 4], [0, NSEQ], [1, DSEQ]]), "z_tabs")
                    rho = V(rhos[:, :], "z_rhos")
                    shp = lambda ap: ap.rearrange("p (a s t) -> p a s t", a=4, s=NSEQ)
                else:
                    Ct = V(tabs[:, 0, :], "z_tabs")
                    St = V(tabs[:, 1, :], "z_tabs")
                    rho = V(tabs[:, 2, :], "z_tabs")
                    shp = lambda ap: ap
                Bre = V(shp(self.ps[:, 5, :]), ("ps", 5))
                Bim = V(shp(self.ps[:, 6, :]), ("ps", 6))
                T0v, T1v = V(shp(t4[:, 0, :]), "z_t40"), V(shp(t4[:, 1, :]), "z_t41")
                self.tt("dve", T0v, Bre, Ct, ALU.mult)
                self.tt("dve", T1v, Bim, St, ALU.mult)
                self.tt("dve", V(bpre[:, 0, :], "z_bpre0"), V(t4[:, 0, :], "z_t40"), V(t4[:, 1, :], "z_t41"), ALU.add)
                self.tt("dve", T0v, Bim, Ct, ALU.mult)
                self.tt("dve", T1v, Bre, St, ALU.mult)
                self.tt("dve", V(bpre[:, 1, :], "z_bpre1"), V(t4[:, 0, :], "z_t40"), V(t4[:, 1, :], "z_t41"), ALU.subtract)
                for part in range(2):
                    if samp:
                        cv = V(cin[:, :].rearrange("p (a s) -> p a s", a=4), "z_cin")
                        self.tt("dve", cv, V(hin[:, part, quad * 4:quad * 4 + 4, :], "z_hin"),
                                V(self.ap(pv, MAG * 16 + quad * 4, [[1, 4], [0, NSEQ]]), "z_pv"), ALU.mult)
                        b0 = V(self.ap(bpre, part * 512, [[128, 4], [DSEQ, NSEQ]]), "z_bpre%d" % part)
                        self.tt("dve", b0, b0, cv, ALU.add)
                    elif not first:
                        cv = V(cin[:, 0:4], "z_cin")
                        self.tt("dve", cv, V(hst[:, l, part, quad * 4:quad * 4 + 4], ("h_s5", (l, part))),
                                V(pv[:, MAG, quad * 4:quad * 4 + 4], "z_pv"), ALU.mult)
                        b0 = V(self.ap(bpre, part * 512, [[128, 4]]), "z_bpre%d" % part)
                        self.tt("dve", b0, b0, cv, ALU.add)
                    self.add("dve", lambda e, o=gsc[:, part, :], d0=rho.ap, d1=bpre[:, part, :]:
                             e.tensor_tensor_scan(o, d0, d1, 0.0, ALU.mult, ALU.add),
                             r=["z_bpre%d" % part] + rho.keys, w=["z_gsc%d" % part])
                Gre, Gim = V(shp(gsc[:, 0, :]), "z_gsc0"), V(shp(gsc[:, 1, :]), "z_gsc1")
                U0v, U1v = V(shp(t5[:, 0, :]), "z_t50"), V(shp(t5[:, 1, :]), "z_t51")
                self.tt("dve", U0v, Gre, St, ALU.mult)
                self.tt("dve", T0v, Gre, Ct, ALU.mult)
                self.tt("dve", U1v, Gim, Ct, ALU.mult)
                self.tt("dve", T1v, Gim, St, ALU.mult)
                self.tt("dve", V(hh[:, 1, :], "z_hh1"), V(t5[:, 0, :], "z_t50"), V(t5[:, 1, :], "z_t51"), ALU.add)
                self.tt("dve", V(hh[:, 0, :], "z_hh0"), V(t4[:, 0, :], "z_t40"), V(t4[:, 1, :], "z_t41"), ALU.subtract)
                self.cp("act", V(hbf[:, 0, :], "z_hbf0"), V(hh[:, 0, :], "z_hh0"))
                self.cp("act", V(hbf[:, 1, :], "z_hbf1"), V(hh[:, 1, :], "z_hh1"))
                for part in range(2):
                    if samp:
                        self.cp("dve", V(hfin[:, part, quad * 4:quad * 4 + 4, :], "z_hfin"),
                                V(self.ap(hh, part * 512 + DSEQ - 1, [[128, 4], [DSEQ, NSEQ]]), "z_hh%d" % part))
                    else:
                        self.cp("dve", V(hst[:, l, part, quad * 4:quad * 4 + 4], ("h_s5", (l, part))),
                                V(self.ap(hh, part * 512 + 127, [[128, 4]]), "z_hh%d" % part))
                for ctl in range(4):
                    ct = quad * 4 + ctl
                    for part in range(2):
                        self.mm(V(self.ps[:, 7, ctl * 32:(ctl + 1) * 32], ("ps", 7)),
                                V(hbf[:, part, ctl * 128:(ctl + 1) * 128], "z_hbf%d" % part),
                                V(Cch[:, part, ct, :], "z_Cch"), start=(part == 0), stop=(part == 1))
                self.cp("act", V(ytok[:, li, quad * 128:(quad + 1) * 128], "z_ytok"), V(self.ps[:, 7, 0:128], ("ps", 7)))
        self.release(mscan)
        mp = self.mark()
        if self.ntp - 1 in tiles:
            po = sb("z_po", [16, 2, 128])
            for part, nm in enumerate(("s5re_p", "s5im_p")):
                self.tr(V(self.ps[0:16, 4, part * 128:(part + 1) * 128], ("ps", 4)), V(hst[:, l, part, :], ("h_s5", (l, part))), idf)
            self.cp("dve", V(po[:, :, :].rearrange("p a b -> p (a b)"), "z_po"), V(self.ps[0:16, 4, 0:256], ("ps", 4)))
            for part, nm in enumerate(("s5re_p", "s5im_p")):
                self.dma("sp", V(self.dram[nm][l].rearrange("(ct q) -> ct q", q=128)), V(po[:, part, :], "z_po"), out_dma=True)
        if has_samp:
            so = sb("z_so", [NSEQ, 2048])
            for part, nm in enumerate(("s5re_s", "s5im_s")):
                for ct in range(16):
                    bk = 2 + ct // 4
                    self.tr(V(self.ps[0:NSEQ, bk, (ct % 4) * 128:(ct % 4 + 1) * 128], ("ps", bk)), V(hfin[:, part, ct, :], "z_hfin"), idf)
                self.cp("act", V(so[:, :].rearrange("p (a b) -> p a b", a=4), "z_so"), V(self.ps[0:NSEQ, 2:6, :], *[("ps", b) for b in (2, 3, 4, 5)]))
                self.dma("sp", V(self.dram[nm][l]), V(so[:, :], "z_so"), out_dma=True)
        self.release(mp)
        ya = sb("z_ya", [128, 512])
        yb = sb("z_yb", [128, 512])
        zbf = sb("z_zbf", [128, 512], BF16)
        zT = sb("z_zT", [128, 4, 128], BF16)
        sgl = sb("z_sgl", [128, 512])
        for li, t in enumerate(tiles):
            self.proj_tok(li, [su_slot], [0])
            yv = V(ya[:, :], "z_ya")
            self.tt("dve", yv, self.psb(0), V(dtab[:, :], "z_dtab"), ALU.mult)
            self.tt("dve", yv, yv, V(ytok[:, li, :], "z_ytok"), ALU.add)
            wv_ = V(yb[:, :], "z_yb")
            self.tt("dve", wv_, yv, yv, ALU.mult)
            self.ts("dve", wv_, wv_, 0.044715, ALU.mult, 1.0, ALU.add)
            self.tt("dve", wv_, wv_, yv, ALU.mult)
            self.act(wv_, wv_, AF.Sigmoid, scale=2.0 * math.sqrt(2.0 / math.pi))
            self.tt("dve", V(ya[:, :], "z_ya"), yv, wv_, ALU.mult)
            self.cp("act", V(zbf[:, :], "z_zbf"), V(ya[:, :], "z_ya"))
            pT = self.ps[:, 1, :].bitcast(BF16)
            for k in range(4):
                self.tr(V(pT[:, k * 128:(k + 1) * 128], ("ps", 1)), V(zbf[:, k * 128:(k + 1) * 128], "z_zbf"), idb)
            self.cp("dve", V(zT[:, :, :].rearrange("p a b -> p (a b)"), "z_zT"), V(pT[:, 0:512], ("ps", 1)))
            for k in range(4):
                self.mm(self.psb(2), V(zT[:, k, :], "z_zT"), self.wv(glu_slot, 4, 512, k), start=(k == 0), stop=(k == 3))
            self.act(V(sgl[:, :], "z_sgl"), self.psb(2), AF.Sigmoid)
            self.tt("dve", V(self.obf[:, :], "obf"), V(ya[:, :], "z_ya"), V(sgl[:, :], "z_sgl"), ALU.mult)
            self.o_to_oT(li)
        self.release(m0)
        self.hi = SB_HI
        self.s5p = None

    def phase_gate(self, l, b, n, first):
        m0 = self.mark()
        sig = self.sb("sig", [128, 2, 512], scratch=True)
        gtmp = self.sb("gtmp", [128, 2, 512], scratch=True)
        self.split_alias("sig"); self.split_alias("gtmp")
        ntok = n * 128
        gs = [self.win_chunk(l, C_GL + b * 1024 + i * 512) for i in range(2)]
        bsl = self.wload(("wb", l, b))
        blocks = [(t0, min(512, ntok - t0)) for t0 in range(0, ntok, 512)]
        it = 0
        for cb in range(8):
            for (t0, nn) in blocks:
                lis = list(range(t0 // 128, (t0 + nn) // 128))
                bA, bB = (0, 1) if it % 2 == 0 else (2, 3)
                i2 = it % 2
                it += 1
                hk = [("hT", li) for li in lis]
                ok = [("oT", li) for li in lis]
                mk = [("mrg", li) for li in lis]
                for kt in range(8):
                    self.mm(V(self.ps[:, bA, 0:nn], ("ps", bA)), self.wv(gs[cb // 4], 8, 512, kt, (cb % 4) * 128, 128),
                            V(self.hT[:, kt, t0:t0 + nn], *hk), start=(kt == 0), stop=(kt == 7))
                for kt in range(4):
                    self.mm(V(self.ps[:, bB, 0:nn], ("ps", bB)), self.wv(bsl, 4, 1024, kt, cb * 128, 128),
                            V(self.oT[:, kt, t0:t0 + nn], *ok), start=(kt == 0), stop=(kt == 3))
                sg = V(sig[:, i2, 0:nn], "sig%d" % i2)
                self.act(sg, V(self.ps[:, bA, 0:nn], ("ps", bA)), AF.Sigmoid)
                dst = V(self.mrg[:, cb, t0:t0 + nn], *mk)
                if first:
                    self.tt("dve", dst, V(self.ps[:, bB, 0:nn], ("ps", bB)), sg, ALU.mult)
                else:
                    tmp = V(gtmp[:, i2, 0:nn], "gtmp%d" % i2)
                    self.tt("dve", tmp, V(self.ps[:, bB, 0:nn], ("ps", bB)), sg, ALU.mult)
                    self.tt("dve", dst, dst, tmp, ALU.add)
        self.release(m0)

    def resid_add(self, li, src, g, otmp, n=D):
        rstd = self.rms_rstd(src, n)
        tv = otmp[:, :]
        if len(src.ap.shape) == 3:
            tv = tv.rearrange("p (a b) -> p a b", a=src.ap.shape[1])
            gv = V(g.ap.rearrange("p (a b) -> p a b", a=src.ap.shape[1]), *g.keys)
        else:
            gv = g
        self.stt(V(tv, "otmp"), src, rstd, gv, ALU.mult, ALU.mult)
        xs = V(self.x[:, li, :], ("x", li))
        self.tt("dve", xs, xs, V(otmp[:, :], "otmp"), ALU.add)

    def phase_out(self, l, n):
        m0 = self.mark()
        mrgb = self.sb("mrgb", [128, 2, 1024], BF16, scratch=True)
        self.split_alias("mrgb")
        otmp = self.sb("otmp", [128, 1024], scratch=True)
        sl = [self.wload(("wout", l, i)) for i in range(2)]
        g = self.load_gain("g_post_mix", l)
        for li in range(n):
            i2 = li % 2
            mb = mrgb[:, i2, :].rearrange("p (k t) -> p k t", k=8)
            self.cp("act", V(mb, "mrgb%d" % i2), V(self.mrg[:, :, li * 128:(li + 1) * 128], ("mrg", li)))
            banks = (0, 1) if li % 2 == 0 else (2, 3)
            for ch in range(2):
                for kt in range(8):
                    self.mm(self.psb(banks[ch]), V(mb[:, kt, :], "mrgb%d" % i2), self.wv(sl[ch], 8, 512, kt),
                            start=(kt == 0), stop=(kt == 7))
            src = V(self.ps[:, banks[0]:banks[0] + 2, :], ("ps", banks[0]), ("ps", banks[1]))
            self.resid_add(li, src, g, otmp)
        self.release(m0)

    def phase_ffn(self, l, n):
        m0 = self.mark()
        uT = self.sb("uT", [128, 8, 512], BF16, scratch=True)
        urel = self.sb("urel", [128, 2, 512], scratch=True)
        self.split_alias("urel")
        otmp = self.sb("otmp", [128, 1024], scratch=True)
        ntok = n * 128
        g = self.load_gain("g_pre_ffn", l)
        for li in range(n):
            self.norm_to_hT(li, g)
        blocks = [(t0, min(512, ntok - t0)) for t0 in range(0, ntok, 512)]

        def yv(li, three=False):
            if three:
                return V(self.ap(self.mrg, li * 1024, [[512, 2], [1, 512]]), "mrg")
            return V(self.ap(self.mrg, li * 1024, [[1, 1024]]), "mrg")
        it = 0
        for q in range(4):
            if q > 0:
                self.phase_end()
            w1 = [self.wload(("ff1", l, q * 1024 + i * 512)) for i in range(2)]
            w2 = [self.wload(("ff2", l, q, i)) for i in range(2)]
            for (t0, nn) in blocks:
                lis = list(range(t0 // 128, (t0 + nn) // 128))
                hk = [("hT", li) for li in lis]
                for fb in range(8):
                    bU = 4 + (it % 2)
                    i2 = it % 2
                    it += 1
                    for kt in range(8):
                        self.mm(V(self.ps[:, bU, 0:nn], ("ps", bU)), self.wv(w1[fb // 4], 8, 512, kt, (fb % 4) * 128, 128),
                                V(self.hT[:, kt, t0:t0 + nn], *hk), start=(kt == 0), stop=(kt == 7))
                    ur = V(urel[:, i2, 0:nn], "urel%d" % i2)
                    self.ts("dve", ur, V(self.ps[:, bU, 0:nn], ("ps", bU)), 0.0, ALU.max)
                    self.act(V(uT[:, fb, 0:nn], "uT"), ur, AF.Square)
                for li in lis:
                    banks = (0, 1) if li % 2 == 0 else (2, 3)
                    tl = (li * 128 - t0)
                    for ch in range(2):
                        for fb in range(8):
                            self.mm(self.psb(banks[ch]), V(uT[:, fb, tl:tl + 128], "uT"),
                                    self.wv(w2[ch], 8, 512, fb), start=(fb == 0), stop=(fb == 7))
                    src = V(self.ps[:, banks[0]:banks[0] + 2, :], ("ps", banks[0]), ("ps", banks[1]))
                    if q == 0:
                        self.cp("act", yv(li, True), src)
                    else:
                        self.tt("dve", yv(li, True), src, yv(li, True), ALU.add)
        g2 = self.load_gain("g_post_ffn", l)
        for li in range(n):
            self.resid_add(li, yv(li), g2, otmp)
        self.release(m0)

    def build(self, emit=True):
        self.setup()
        for gi, tiles in enumerate(self.groups):
            n = len(tiles)
            for li, t in enumerate(tiles):
                self.dma("sp", V(self.x[:, li, :], ("x", li)), V(self.dram["xin"][t * 128:(t + 1) * 128, :]))
                self.dma("sp", V(self.c["rope"][:, li], "k_rope"), V(self.dram["c_rope"][:, t]))
            for l in range(self.depth):
                g = self.load_gain("g_pre_mix", l)
                for li in range(n):
                    self.norm_to_hT(li, g)
                first = True
                for b, name in enumerate(("ret", "ssd", "hg", "s5")):
                    if name not in self.branches:
                        continue
                    getattr(self, "phase_" + name)(l, tiles)
                    self.phase_end()
                    self.phase_gate(l, b, n, first)
                    self.phase_end()
                    first = False
                self.phase_out(l, n)
                self.phase_end()
                if self.ffn:
                    self.phase_ffn(l, n)
                    self.phase_end()
            for li, t in enumerate(tiles):
                self.dma("sp", V(self.dram["y"][t * 128:(t + 1) * 128, :]), V(self.x[:, li, :], ("x", li)), out_dma=True)
        if not emit:
            return None
        stats = self.s.emit(self.nc)
        stats["sbuf_peak"] = self.peak - SB_LO
        return stats


def make_prog(ntp, groups, depth=2, branches=("ret", "ssd", "hg", "s5"), ffn=True):
    p1 = Prog(ntp, groups, depth=depth, branches=branches, ffn=ffn)
    p1.build(emit=False)
    p2 = Prog(ntp, groups, depth=depth, branches=branches, ffn=ffn, plan=list(p1.rec))
    stats = p2.build()
    return p2, stats


NTP = 16
GROUPS = [[0, 1, 2, 3], [4, 5, 6, 7], [8, 9, 10, 11], [12, 13, 14, 15], [16]]
NCORES = 8
DEPTH = 2


def kernel(**inp):
    inp = {k: np.asarray(v) for k, v in inp.items()}
    prog, _ = make_prog(NTP, GROUPS, depth=DEPTH, branches=("ret", "ssd", "hg", "s5"), ffn=True)
    L = DEPTH
    shared = {}
    for k in ("g_pre_mix", "g_post_mix", "g_pre_ffn", "g_post_ffn", "w_in", "ssd_conv_w", "ssd_conv_b", "ssd_a_log",
              "ssd_dt_bias", "ssd_d", "ssd_norm", "hg_lb_logits", "hg_norm", "s5_log_dt", "s5_w_glu", "w_branch",
              "w_out", "w_ff1", "w_ff2", "s5_c_re", "s5_c_im"):
        shared[k] = np.ascontiguousarray(inp[k], dtype=np.float32)
    shared["ret_gn"] = np.ascontiguousarray(inp["ret_gn"].reshape(L, 512))
    shared["s5_a_re"] = np.ascontiguousarray(inp["s5_a_re"].reshape(L, 2048))
    shared["s5_a_im"] = np.ascontiguousarray(inp["s5_a_im"].reshape(L, 2048))
    shared["s5_b_re"] = np.ascontiguousarray(inp["s5_b_re"].reshape(L, 2048, 16))
    shared["s5_b_im"] = np.ascontiguousarray(inp["s5_b_im"].reshape(L, 2048, 16))
    shared["s5_d"] = np.ascontiguousarray(inp["s5_d"].reshape(L, 512))
    for k, v in prog.consts.items():
        shared["c_" + k] = np.ascontiguousarray(v)
    maps = []
    for c in range(NCORES):
        sl = slice(c * NSEQ, (c + 1) * NSEQ)
        m = dict(shared)
        m["xin"] = np.ascontiguousarray(np.concatenate(
            [inp["x_prompt"][c], inp["x_sample"][sl].reshape(NSEQ * DSEQ, D)], 0), dtype=np.float32)
        m["st_ret"] = np.ascontiguousarray(inp["state_ret"][:, sl])
        m["st_hg"] = np.ascontiguousarray(inp["state_hgrn"][:, sl])
        m["st_ssd"] = np.ascontiguousarray(inp["state_ssd"][:, sl])
        m["st_conv"] = np.ascontiguousarray(inp["state_conv"][:, sl])
        m["st_s5re"] = np.ascontiguousarray(inp["state_s5_re"][:, sl].reshape(L, NSEQ, 2048))
        m["st_s5im"] = np.ascontiguousarray(inp["state_s5_im"][:, sl].reshape(L, NSEQ, 2048))
        maps.append(m)
    res = run_bass_kernel_spmd(prog.nc, maps, core_ids=list(range(NCORES)))
    R = res.results
    f32 = np.float32
    y_prompt = np.stack([R[c]["y"][:NTP * 128] for c in range(NCORES)], 0).astype(f32)
    y_sample = np.concatenate([R[c]["y"][NTP * 128:].reshape(NSEQ, DSEQ, D) for c in range(NCORES)], 0).astype(f32)

    def pstack(name, shape=None):
        a = np.stack([R[c][name] for c in range(NCORES)], 1)
        return a.reshape(shape).astype(f32) if shape is not None else a.astype(f32)

    def scat(name, shape=None):
        a = np.concatenate([R[c][name] for c in range(NCORES)], 1)
        return a.reshape(shape).astype(f32) if shape is not None else a.astype(f32)

    B, BS = NCORES, NCORES * NSEQ
    return (y_prompt, y_sample,
            pstack("ret_p"), scat("ret_s"),
            pstack("ssd_p"), scat("ssd_s"),
            pstack("conv_p"), scat("conv_s"),
            pstack("hg_p"), scat("hg_s"),
            pstack("s5re_p", (L, B, 32, 64)), scat("s5re_s", (L, BS, 32, 64)),
            pstack("s5im_p", (L, B, 32, 64)), scat("s5im_s", (L, BS, 32, 64)))
```

```python
import math
import numpy as np
import ml_dtypes
import concourse.bass as bass
import concourse.mybir as mybir
from concourse.bass_utils import run_bass_kernel_spmd

COMPUTE = ("pe", "act", "dve", "pool")
EPOCH = 2000
RING = {"sp": 16, "act": 8, "pool": 16}


def _norm(k):
    return k if isinstance(k, tuple) else (k, None)


class Sched:
    def __init__(self):
        self.ins = []

    def add(self, eng, fn, r=(), w=(), dma=False, out_dma=False):
        self.ins.append(dict(eng=eng, fn=fn, r=[_norm(k) for k in r], w=[_norm(k) for k in w],
                             dma=dma, out_dma=out_dma, deps=set(), sig=False))
        return len(self.ins) - 1

    def analyze(self):
        lastw = {}
        readers = {}
        for i, ins in enumerate(self.ins):
            deps = set()

            def conf_slots(d, slot):
                if slot is None:
                    return list(d.keys())
                return [s for s in (slot, None) if s in d]

            for (n, s) in ins["r"]:
                d = lastw.get(n, {})
                for ss in conf_slots(d, s):
                    deps.add(d[ss])
            for (n, s) in ins["w"]:
                d = lastw.get(n, {})
                for ss in conf_slots(d, s):
                    deps.add(d[ss])
                rd = readers.get(n, {})
                for ss in conf_slots(rd, s):
                    deps.update(rd[ss].values())
            deps.discard(i)
            keep = set()
            for dpi in deps:
                p = self.ins[dpi]
                if (not p["dma"]) and (not ins["dma"]) and p["eng"] == ins["eng"]:
                    if ins["eng"] == "pe":
                        continue
                    pass
                keep.add(dpi)
            ins["deps"] = keep
            for dpi in keep:
                self.ins[dpi]["sig"] = True
            for (n, s) in ins["w"]:
                d = lastw.setdefault(n, {})
                rd = readers.setdefault(n, {})
                if s is None:
                    d.clear()
                    rd.clear()
                d[s] = i
                rd[s] = {}
            for (n, s) in ins["r"]:
                rd = readers.setdefault(n, {})
                e = rd.setdefault(s, {})
                key = ("dma", i) if ins["dma"] else ins["eng"]
                e[key] = i

    def emit(self, nc):
        self.analyze()
        cnt = {e: 0 for e in COMPUTE}
        dcnt = {q: 0 for q in RING}
        for ins in self.ins:
            if ins["dma"]:
                q = ins["eng"]
                ins["dq"] = dcnt[q]
                dcnt[q] += 1
            elif ins["sig"]:
                cnt[ins["eng"]] += 1
                ins["cnt"] = cnt[ins["eng"]]
        sems = {}
        for e in COMPUTE:
            nep = (cnt[e] + EPOCH - 1) // EPOCH + 1
            sems[e] = [nc.alloc_semaphore(f"s_{e}_{k}") for k in range(nep)]
        rings = {q: [nc.alloc_semaphore(f"r_{q}_{k}") for k in range(RING[q])] for q in RING if dcnt[q] > 0}

        def sig_of(p):
            if p["dma"]:
                q = p["eng"]
                k = p["dq"]
                return rings[q][k % RING[q]], 16 * (k // RING[q] + 1), ("d", q, k % RING[q])
            c = p["cnt"] - 1
            return sems[p["eng"]][c // EPOCH], (c % EPOCH) + 1, ("c", p["eng"], c // EPOCH)

        per_eng = {}
        for i, ins in enumerate(self.ins):
            per_eng.setdefault(ins["eng"], []).append(i)
        out_dmas = [i for i, ins in enumerate(self.ins) if ins["out_dma"]]

        def run(engname, e):
            waited = {}
            maxep = {}

            def do_wait(p):
                sem, val, ok = sig_of(p)
                if ok[0] == "c":
                    me = maxep.get(ok[1], -1)
                    if ok[2] < me:
                        return
                    if ok[2] > me:
                        maxep[ok[1]] = ok[2]
                if waited.get(ok, 0) >= val:
                    return
                waited[ok] = val
                e.wait_ge(sem, val)

            for i in per_eng.get(engname, []):
                ins = self.ins[i]
                for dpi in sorted(ins["deps"]):
                    do_wait(self.ins[dpi])
                if ins["dma"]:
                    q = ins["eng"]
                    k = ins["dq"]
                    if k >= RING[q]:
                        sem = rings[q][k % RING[q]]
                        val = 16 * (k // RING[q])
                        ok = ("d", q, k % RING[q])
                        if waited.get(ok, 0) < val:
                            waited[ok] = val
                            e.wait_ge(sem, val)
                    sem, val, _ = sig_of(ins)
                    ins["fn"](e).then_inc(sem, 16)
                else:
                    bi = ins["fn"](e)
                    if ins["sig"]:
                        sem, val, _ = sig_of(ins)
                        bi.then_inc(sem, 1)
            if engname == "sp":
                for i in out_dmas:
                    do_wait(self.ins[i])

        with nc.Block() as block:
            @block.tensor
            def _(e):
                run("pe", e)

            @block.scalar
            def _(e):
                run("act", e)

            @block.vector
            def _(e):
                run("dve", e)

            @block.gpsimd
            def _(e):
                run("pool", e)

            @block.sync
            def _(e):
                run("sp", e)
        return {e: len(per_eng.get(e, [])) for e in ("pe", "act", "dve", "pool", "sp")}


F32, BF16 = mybir.dt.float32, mybir.dt.bfloat16
AF = mybir.ActivationFunctionType
ALU = mybir.AluOpType
AX = mybir.AxisListType

D = 1024
MIXW = 512
DFF = 4096
IN_COLS = 10248
PAST_LEN = 16384
EPS = 1e-6
NSEQ = 16
DSEQ = 8
C_RQ, C_RK, C_RV, C_RG = 0, 512, 1024, 1536
C_SZ, C_SXBC, C_SDT = 2048, 2560, 3584
C_HQ, C_HF, C_HI, C_HG = 3592, 4104, 4616, 5128
C_SU = 5640
C_GL = 6152
SB_LO, SB_HI = 16512, 229344
RET_DEC = [float(np.exp(np.log(1.0 - 2.0 ** (-5.0 - h)) * 128.0)) for h in range(4)]


class V:
    def __init__(self, ap, *keys):
        self.ap = ap
        self.keys = list(keys)


def host_consts(ntp):
    nt = ntp + 1
    c = {}
    c["ident_bf"] = np.eye(128, dtype=np.float32).astype(ml_dtypes.bfloat16)
    c["ident_f"] = np.eye(128, dtype=np.float32)
    p = np.arange(128)
    pos = np.zeros((128, nt), np.float32)
    for t in range(ntp):
        pos[:, t] = t * 128 + p
    pos[:, ntp] = PAST_LEN + (p % DSEQ)
    half = 64
    inv = (10000.0 ** (-np.arange(half, dtype=np.float32) / half)).astype(np.float32)
    ang = (pos[:, :, None] * inv[None, None, :]).astype(np.float32)
    rope = np.zeros((128, nt, 3, 64), np.float32)
    rope[:, :, 0] = np.cos(ang)
    rope[:, :, 1] = np.sin(ang)
    rope[:, :, 2] = -np.sin(ang)
    c["rope"] = rope
    jj, ii = np.meshgrid(p, p, indexing="ij")
    mp = (jj <= ii).astype(np.float32)
    ms = ((jj <= ii) & (jj // DSEQ == ii // DSEQ)).astype(np.float32)
    c["maskT"] = np.stack([mp, ms], 1).astype(ml_dtypes.bfloat16)
    logg = np.log(1.0 - 2.0 ** (-5.0 - np.arange(4, dtype=np.float64)))
    rdec = np.zeros((128, 2, 3, 4), np.float32)
    for kind, (C, idx) in enumerate(((128, p), (DSEQ, p % DSEQ))):
        for h in range(4):
            rdec[:, kind, 0, h] = np.exp(logg[h] * (idx + 1.0 - C))
            rdec[:, kind, 1, h] = np.exp(logg[h] * (C - 1.0 - idx)) * 128 ** -0.5
            rdec[:, kind, 2, h] = np.exp(logg[h] * C)
    c["rdec"] = rdec
    c["seqsel"] = (p[:, None] // DSEQ == np.arange(NSEQ)[None, :]).astype(np.float32)
    tric = np.zeros((128, 2, 128), np.float32)
    tric[:, 0, :] = (jj <= ii).astype(np.float32) - (jj <= 63).astype(np.float32)
    tric[:, 1, :] = ms
    c["tric"] = tric
    sel3 = np.zeros((128, 3), np.float32)
    sel3[:, 0] = (p <= 63); sel3[:, 1] = 1.0; sel3[:, 2] = (p > 63)
    c["sel3"] = sel3
    trif = np.stack([mp, ms], 1).astype(np.float32)
    c["trif"] = trif
    c["ntrif"] = -trif
    c["negm"] = (-30000.0 * (1.0 - trif)).astype(np.float32)
    blk = np.ones((128, 2, 128), np.float32)
    blk[:, 1, :] = (jj // DSEQ == ii // DSEQ)
    c["blk"] = blk
    c["nhalf"] = np.full((128, 16), -0.5, np.float32)
    c["t1"] = np.tile((p + 1.0)[None, :], (128, 1)).astype(np.float32)
    zm = np.ones((128, 2, 128), np.float32)
    zm[:, 0, 0] = 0.0
    zm[:, 1, ::DSEQ] = 0.0
    c["zmask"] = zm
    return c


class KB:
    def __init__(self, ntp, groups, depth=2, branches=("ret",), ffn=True, plan=None):
        self._plan_arg = plan
        self.ntp, self.nt, self.groups, self.depth = ntp, ntp + 1, groups, depth
        self.branches, self.ffn = branches, ffn
        self.nc = bass.Bass("TRN2", target_bir_lowering=False)
        self.s = Sched()
        self.smax = max(len(g) for g in groups)
        self.consts = host_consts(ntp)
        self.dram = {}
        self._slot_ctr = 0
        self.top = SB_LO
        self.alias = {}
        self.uid = 0
        self.peak = SB_LO
        self.hi = SB_HI
        self.lo_top = SB_HI
        self.top_watermark = 0
        self.s5p = None
        self.plan = self._plan_arg
        self.rec = []
        self.w_next = 0
        self.w_done = 0

    def din(self, name, shape, dt=F32):
        self.dram[name] = self.nc.dram_tensor(name, list(shape), dt, kind="ExternalInput").ap()
        return self.dram[name]

    def dout(self, name, shape, dt=F32):
        self.dram[name] = self.nc.dram_tensor(name, list(shape), dt, kind="ExternalOutput").ap()
        return self.dram[name]

    def sb(self, name, shape, dt=F32, scratch=False):
        n = 1
        for x in shape[1:]:
            n *= x
        nbytes = n * (2 if dt == BF16 else 4)
        off = (self.top + 63) // 64 * 64
        assert off + nbytes <= self.hi, f"SBUF overflow allocating {name}: {off + nbytes - self.hi} bytes over"
        self.uid += 1
        h = self.nc.alloc_sbuf_tensor_at(f"{name}_{self.uid}", list(shape), dt, offset=off)
        self.top = off + nbytes
        self.peak = max(self.peak, self.top)
        if scratch:
            g0, g1 = off // 512, (off + nbytes + 511) // 512
            self.alias[name] = [("A", g) for g in range(g0, g1)]
        return h

    def sb_top(self, name, shape, dt=F32):
        n = 1
        for x in shape[1:]:
            n *= x
        nbytes = n * (2 if dt == BF16 else 4)
        off = (self.hi - nbytes) // 64 * 64
        assert off >= self.top, f"top allocation {name} collides with scratch ({off} < {self.top})"
        self.uid += 1
        h = self.nc.alloc_sbuf_tensor_at(f"{name}_{self.uid}", list(shape), dt, offset=off)
        self.hi = off
        self.lo_top = min(self.lo_top, off)
        g0, g1 = off // 512, (off + nbytes + 511) // 512
        self.alias[name] = [("A", g) for g in range(g0, g1)]
        return h

    def split_alias(self, name, n=2):
        g = self.alias[name]
        m = len(g) // n
        for i in range(n):
            self.alias[name + str(i)] = g[i * m:(i + 1) * m]

    def mark(self):
        return self.top

    def release(self, m):
        self.top = m

    def K(self, keys):
        out = []
        for k in keys:
            n = k[0] if isinstance(k, tuple) else k
            if n in self.alias:
                out += self.alias[n]
            else:
                out.append(k)
        return out

    @staticmethod
    def ap(h, off, dims):
        fs = 1
        for x in h.shape[1:]:
            fs *= x
        return bass.AP(h, off, [[fs, h.shape[0]]] + [list(d) for d in dims])

    def add(self, eng, fn, r, w, **kw):
        self.s.add(eng, fn, r=self.K(r), w=self.K(w), **kw)

    def mm(self, out, lhsT, rhs, start=True, stop=True):
        self.add("pe", lambda e, o=out.ap, l=lhsT.ap, r=rhs.ap: e.matmul(o, lhsT=l, rhs=r, start=start, stop=stop),
                 r=lhsT.keys + rhs.keys + ([] if start else out.keys), w=out.keys)

    def tr(self, out, in_, ident):
        self.add("pe", lambda e, o=out.ap, i=in_.ap, d=ident.ap: e.transpose(o, i, d),
                 r=in_.keys + ident.keys, w=out.keys)

    def tt(self, eng, out, in0, in1, op):
        self.add(eng, lambda e, o=out.ap, a=in0.ap, b=in1.ap: e.tensor_tensor(o, a, b, op),
                 r=in0.keys + in1.keys, w=out.keys)

    def ts(self, eng, out, in0, s1, op0, s2=None, op1=None):
        def fn(e, o=out.ap, a=in0.ap):
            sc1 = s1.ap if isinstance(s1, V) else s1
            sc2 = s2.ap if isinstance(s2, V) else s2
            if op1 is None:
                return e.tensor_scalar(o, a, sc1, None, op0)
            return e.tensor_scalar(o, a, sc1, sc2, op0, op1)
        rk = list(in0.keys)
        for sx in (s1, s2):
            if isinstance(sx, V):
                rk += sx.keys
        self.add(eng, fn, r=rk, w=out.keys)

    def stt(self, out, in0, sc, in1, op0, op1):
        def fn(e, o=out.ap, a=in0.ap, b=in1.ap):
            return e.scalar_tensor_tensor(o, a, sc.ap if isinstance(sc, V) else sc, b, op0, op1)
        rk = in0.keys + in1.keys + (sc.keys if isinstance(sc, V) else [])
        self.add("dve", fn, r=rk, w=out.keys)

    def act(self, out, in_, func, bias=0.0, scale=1.0, accum=None):
        def fn(e, o=out.ap, i=in_.ap):
            kw = {}
            if accum is not None:
                kw["accum_out"] = accum.ap
            return e.activation(o, i, func, bias=(bias.ap if isinstance(bias, V) else bias),
                                scale=(scale.ap if isinstance(scale, V) else scale), **kw)
        rk = list(in_.keys)
        for sx in (bias, scale):
            if isinstance(sx, V):
                rk += sx.keys
        wk = out.keys + (accum.keys if accum is not None else [])
        self.add("act", fn, r=rk, w=wk)

    def cp(self, eng, out, in_):
        if eng == "act":
            self.add("act", lambda e, o=out.ap, i=in_.ap: e.copy(o, i), r=in_.keys, w=out.keys)
        else:
            self.add(eng, lambda e, o=out.ap, i=in_.ap: e.tensor_copy(o, i), r=in_.keys, w=out.keys)

    def red(self, out, in_, op=ALU.add):
        self.add("dve", lambda e, o=out.ap, i=in_.ap: e.tensor_reduce(o, i, AX.X, op), r=in_.keys, w=out.keys)

    def recip(self, out, in_):
        self.add("dve", lambda e, o=out.ap, i=in_.ap: e.reciprocal(o, i), r=in_.keys, w=out.keys)

    def memset(self, eng, out, val):
        self.add(eng, lambda e, o=out.ap: e.memset(o, val), r=[], w=out.keys)

    def dma(self, q, out, in_, out_dma=False):
        self.add(q, lambda e, o=out.ap, i=in_.ap: e.dma_start(out=o, in_=i), r=in_.keys, w=out.keys,
                 dma=True, out_dma=out_dma)


class Prog(KB):
    NSLOT = 8

    def setup(self):
        nc, nt, S = self.nc, self.nt, self.smax
        L = self.depth
        self.din("xin", [nt * 128, D])
        self.din("st_ret", [L, NSEQ, 4, 128, 128])
        self.din("st_hg", [L, NSEQ, 4, 128, 128])
        self.din("st_ssd", [L, NSEQ, 8, 64, 128])
        self.din("st_conv", [L, NSEQ, 3, 1024])
        self.din("st_s5re", [L, NSEQ, 2048])
        self.din("st_s5im", [L, NSEQ, 2048])
        for n, shp in (("g_pre_mix", [L, D]), ("g_post_mix", [L, D]), ("g_pre_ffn", [L, D]), ("g_post_ffn", [L, D]),
                       ("w_in", [L, D, IN_COLS]), ("ret_gn", [L, 512]), ("ssd_conv_w", [L, 4, 1024]),
                       ("ssd_conv_b", [L, 1024]), ("ssd_a_log", [L, 8]), ("ssd_dt_bias", [L, 8]), ("ssd_d", [L, 8]),
                       ("ssd_norm", [L, 512]), ("hg_lb_logits", [L, 512]), ("hg_norm", [L, 128]),
                       ("s5_a_re", [L, 2048]), ("s5_a_im", [L, 2048]), ("s5_b_re", [L, 2048, 16]),
                       ("s5_b_im", [L, 2048, 16]), ("s5_c_re", [L, 32, 16, 64]), ("s5_c_im", [L, 32, 16, 64]),
                       ("s5_d", [L, 512]), ("s5_log_dt", [L, 32]), ("s5_w_glu", [L, 512, 512]),
                       ("w_branch", [L, 4, 512, D]), ("w_out", [L, D, D]), ("w_ff1", [L, D, DFF]), ("w_ff2", [L, DFF, D])):
            self.din(n, shp)
        self.dout("y", [nt * 128, D])
        self.dout("ret_p", [L, 4, 128, 128]); self.dout("ret_s", [L, NSEQ, 4, 128, 128])
        self.dout("ssd_p", [L, 8, 64, 128]); self.dout("ssd_s", [L, NSEQ, 8, 64, 128])
        self.dout("conv_p", [L, 3, 1024]); self.dout("conv_s", [L, NSEQ, 3, 1024])
        self.dout("hg_p", [L, 4, 128, 128]); self.dout("hg_s", [L, NSEQ, 4, 128, 128])
        self.dout("s5re_p", [L, 2048]); self.dout("s5re_s", [L, NSEQ, 2048])
        self.dout("s5im_p", [L, 2048]); self.dout("s5im_s", [L, NSEQ, 2048])
        self.c = {}
        for name, arr in self.consts.items():
            dt = BF16 if arr.dtype == ml_dtypes.bfloat16 else F32
            d = self.din("c_" + name, arr.shape, dt)
            if name == "rope":
                continue
            h = self.sb("k_" + name, arr.shape, dt)
            self.c[name] = h
            nd = len(arr.shape)
            self.dma("sp", V(h[(slice(None),) * nd], "k_" + name), V(d[(slice(None),) * nd]))
        self.c["rope"] = self.sb("k_rope", [128, S, 3, 64])
        self.x = self.sb("x", [128, S, D])
        self.hT = self.sb("hT", [128, 8, S * 128], BF16)
        self.mrg = self.sb("mrg", [128, 8, S * 128])
        self.oT = self.sb("oT", [128, 4, S * 128], BF16)
        self.wr = self.sb("wr", [128, self.NSLOT, 4096], BF16)
        self.ps = nc.alloc_psum_tensor("ps", [128, 8, 512], F32)
        self.gtab = self.sb("gtab", [128, D])
        self.junk = self.sb("junk", [128, D], BF16)
        self.st = self.sb("st", [128, 8, 16])
        self.stc = 0
        self.hb = self.sb("hb", [128, 2, D], BF16)
        self.hbc = 0
        self.c_eps = EPS
        self.S_ret = self.sb("S_ret", [128, L, 512])
        self.Sdbf_ret = self.sb("Sdbf_ret", [128, L, 512], BF16)
        self.S_hg = self.sb("S_hg", [128, L, 512])
        self.ST_ssd = self.sb("ST_ssd", [128, L, 512])
        self.STbf_ssd = self.sb("STbf_ssd", [128, L, 512], BF16)
        self.xcarry = self.sb("xcarry", [128, L, 8, 3])
        self.h_s5 = self.sb("h_s5", [128, L, 2, 16])

    def psb(self, b):
        return V(self.ps[:, b, :], ("ps", b))

    def stat(self):
        i = self.stc % 8
        self.stc += 1
        return i

    def wsrc(self, d):
        kind = d[0]
        if kind == "win":
            _, l, c0, ncol = d
            return self.dram["w_in"][l, :, c0:c0 + ncol].rearrange("(k p) c -> p k c", p=128), 8, ncol
        if kind == "wb":
            return self.dram["w_branch"][d[1], d[2]].rearrange("(k p) c -> p k c", p=128), 4, 1024
        if kind == "wout":
            return self.dram["w_out"][d[1], :, d[2] * 512:(d[2] + 1) * 512].rearrange("(k p) c -> p k c", p=128), 8, 512
        if kind == "ff1":
            return self.dram["w_ff1"][d[1], :, d[2]:d[2] + 512].rearrange("(k p) c -> p k c", p=128), 8, 512
        if kind == "ff2":
            _, l, q, i = d
            return self.dram["w_ff2"][l, q * 1024:(q + 1) * 1024, i * 512:(i + 1) * 512].rearrange("(k p) c -> p k c", p=128), 8, 512
        if kind == "glu":
            return self.dram["s5_w_glu"][d[1]].rearrange("(k p) c -> p k c", p=128), 4, 512
        raise ValueError(d)

    def _issue(self, i):
        src3, a, b = self.wsrc(self.plan[i])
        slot = i % self.NSLOT
        dst = self.ap(self.wr, slot * 4096, [[b, a], [1, b]])
        self.dma("pool", V(dst, ("wr", slot)), V(src3))

    def _pump(self):
        if self.plan is None:
            return
        while self.w_next < len(self.plan) and (self.w_next < self.NSLOT or self.w_done > self.w_next - self.NSLOT):
            self._issue(self.w_next)
            self.w_next += 1

    def phase_end(self):
        self.w_done = self._slot_ctr
        self._pump()

    def wload(self, d):
        i = self._slot_ctr
        self._slot_ctr += 1
        if self.plan is None:
            self.rec.append(d)
            src3, a, b = self.wsrc(d)
            slot = i % self.NSLOT
            dst = self.ap(self.wr, slot * 4096, [[b, a], [1, b]])
            self.dma("pool", V(dst, ("wr", slot)), V(src3))
            return slot
        assert self.plan[i] == d, (self.plan[i], d)
        while self.w_next <= i:
            self._issue(self.w_next)
            self.w_next += 1
        return i % self.NSLOT

    def wv(self, slot, a, b, i, j0=0, jn=None):
        jn = b if jn is None else jn
        return V(self.ap(self.wr, slot * 4096 + i * b + j0, [[1, jn]]), ("wr", slot))

    def win_chunk(self, l, c0, ncol=512):
        return self.wload(("win", l, c0, ncol))

    def load_gain(self, name, l):
        src = self.dram[name][l:l + 1, :].partition_broadcast(128)
        self.dma("sp", V(self.gtab[:, :], "gtab"), V(src))
        return V(self.gtab[:, :], "gtab")

    def bcast_load(self, dst, key, src_row):
        self.dma("sp", V(dst, key), V(src_row.partition_broadcast(128)))

    def rms_rstd(self, src, n):
        si = self.stat()
        ssq = V(self.st[:, si, 0:1], ("st", si))
        jv = self.junk[:, 0:n]
        if len(src.ap.shape) == 3:
            jv = jv.rearrange("p (a b) -> p a b", a=src.ap.shape[1])
        self.act(V(jv, "junk"), src, AF.Square, accum=ssq)
        rms = V(self.st[:, si, 1:2], ("st", si))
        self.ts("dve", rms, ssq, 1.0 / n, ALU.mult, EPS, ALU.add)
        rstd = V(self.st[:, si, 2:3], ("st", si))
        self.tt("pool", rstd, rms, V(self.c["nhalf"][:, 0:1], "k_nhalf"), ALU.pow)
        return rstd

    def norm_to_hT(self, li, g):
        xs = V(self.x[:, li, :], ("x", li))
        rstd = self.rms_rstd(xs, D)
        hi = self.hbc % 2
        self.hbc += 1
        hb = V(self.hb[:, hi, :], ("hb", hi))
        self.stt(hb, xs, rstd, g, ALU.mult, ALU.mult)
        bT = 4
        pT = self.ps[:, bT, :].bitcast(BF16)
        for kt in range(8):
            self.tr(V(pT[:, kt * 128:(kt + 1) * 128], ("ps", bT)), V(self.hb[:, hi, kt * 128:(kt + 1) * 128], ("hb", hi)),
                    V(self.c["ident_bf"][:, :], "k_ident_bf"))
        dst = self.ap(self.hT, li * 128, [[self.smax * 128, 8], [1, 128]])
        self.cp("act", V(dst, ("hT", li)), V(pT.rearrange("p (k t) -> p k t", k=8), ("ps", bT)))

    def proj_tok(self, li, slots, banks):
        for slot, b in zip(slots, banks):
            for kt in range(8):
                self.mm(self.psb(b), V(self.hT[:, kt, li * 128:(li + 1) * 128], ("hT", li)),
                        self.wv(slot, 8, 512, kt), start=(kt == 0), stop=(kt == 7))

    def transpose_to(self, srcs, bank, dsts, eng="act"):
        pT = self.ps[:, bank, :].bitcast(BF16)
        for i, (src, nm) in enumerate(srcs):
            for h in range(4):
                self.tr(V(pT[:, (i * 4 + h) * 128:(i * 4 + h + 1) * 128], ("ps", bank)),
                        V(src[:, h * 128:(h + 1) * 128], nm), V(self.c["ident_bf"][:, :], "k_ident_bf"))
        for i, (dst, nm) in enumerate(dsts):
            self.cp(eng if i % 2 == 0 else "dve", V(dst[:, :], nm), V(pT[:, i * 512:(i + 1) * 512], ("ps", bank)))

    def phase_ret(self, l, tiles):
        c = self.c
        m0 = self.mark()
        sb = lambda n, shp, dt=F32: self.sb(n, shp, dt, scratch=True)
        qr, kr, vbf, qT, kT, attm = [sb(n, [128, 512], BF16) for n in ("qr", "kr", "vbf", "qT", "kT", "attm")]
        ropeA, ropeB, sil, gn = [sb(n, [128, 512]) for n in ("ropeA", "ropeB", "sil", "gn_ret")]
        self.norm_scratch()
        slots = [self.win_chunk(l, c0) for c0 in (C_RQ, C_RK, C_RV, C_RG)]
        self.bcast_load(gn[:, :], "gn_ret", self.dram["ret_gn"][l:l + 1, :])
        SK = [("S_ret", (l, hh_)) for hh_ in range(4)]
        Sl = V(self.S_ret[:, l, :], *SK)
        Sdb = self.Sdbf_ret
        self.proj_tok(0, slots, [0, 1, 2, 3])
        for li, t in enumerate(tiles):
            samp = (t == self.ntp)
            kind = 1 if samp else 0
            cosb = V(self.ap(c["rope"], li * 192, [[0, 4], [0, 2], [1, 64]]), "k_rope")
            sinb = V(self.ap(c["rope"], li * 192 + 64, [[0, 4], [1, 64]]), "k_rope")
            nsinb = V(self.ap(c["rope"], li * 192 + 128, [[0, 4], [1, 64]]), "k_rope")
            for b, dst, nm, di in ((0, qr, "qr", 0), (1, kr, "kr", 1)):
                pv = self.ps[:, b, :].rearrange("p (h two e) -> p h two e", h=4, two=2)
                A = V(ropeA[:, :].rearrange("p (h two e) -> p h two e", h=4, two=2), "ropeA")
                Bv = ropeB[:, :].rearrange("p (h two e) -> p h two e", h=4, two=2)
                self.tt("dve", A, V(pv, ("ps", b)), cosb, ALU.mult)
                self.tt("dve", V(Bv[:, :, 0, :], "ropeB"), V(pv[:, :, 1, :], ("ps", b)), nsinb, ALU.mult)
                self.tt("dve", V(Bv[:, :, 1, :], "ropeB"), V(pv[:, :, 0, :], ("ps", b)), sinb, ALU.mult)
                self.tt("dve", V(ropeA[:, :], "ropeA"), V(ropeA[:, :], "ropeA"), V(ropeB[:, :], "ropeB"), ALU.add)
                dcb = V(self.ap(c["rdec"], kind * 12 + di * 4, [[1, 4], [0, 128]]), "k_rdec")
                self.tt("dve", V(dst[:, :].rearrange("p (h e) -> p h e", h=4), nm),
                        V(ropeA[:, :].rearrange("p (h e) -> p h e", h=4), "ropeA"), dcb, ALU.mult)
            self.cp("act", V(vbf[:, :], "vbf"), self.psb(2))
            self.act(V(sil[:, :], "sil"), self.psb(3), AF.Silu)
            self.tt("dve", V(sil[:, :], "sil"), V(sil[:, :], "sil"), V(gn[:, :], "gn_ret"), ALU.mult)
            self.transpose_to([(qr, "qr"), (kr, "kr")], 4, [(qT, "qT"), (kT, "kT")])
            for h in range(4):
                hs = slice(h * 128, (h + 1) * 128)
                self.mm(V(self.ps[:, 5, hs], ("ps", 5)), V(kT[:, hs], "kT"), V(qT[:, hs], "qT"))
            mk = V(self.ap(c["maskT"], kind * 128, [[0, 4], [1, 128]]), "k_maskT")
            self.tt("dve", V(attm[:, :].rearrange("p (h i) -> p h i", h=4), "attm"),
                    V(self.ps[:, 5, :].rearrange("p (h i) -> p h i", h=4), ("ps", 5)), mk, ALU.mult)
            if not samp:
                first = (t == 0)
                for h in range(4):
                    hs = slice(h * 128, (h + 1) * 128)
                    self.mm(V(self.ps[:, 6, hs], ("ps", 6)), V(attm[:, hs], "attm"), V(vbf[:, hs], "vbf"),
                            start=True, stop=first)
                    if not first:
                        self.mm(V(self.ps[:, 6, hs], ("ps", 6)), V(qT[:, hs], "qT"), V(Sdb[:, l, hs], ("Sdbf_ret", l)),
                                start=False, stop=True)
                for h in range(4):
                    hs = slice(h * 128, (h + 1) * 128)
                    self.mm(V(self.ps[:, 7, hs], ("ps", 7)), V(kr[:, hs], "kr"), V(vbf[:, hs], "vbf"))
                if first:
                    self.cp("dve", Sl, self.psb(7))
                else:
                    for h in range(4):
                        hs = slice(h * 128, (h + 1) * 128)
                        self.stt(V(self.S_ret[:, l, hs], ("S_ret", (l, h))), V(self.S_ret[:, l, hs], ("S_ret", (l, h))), RET_DEC[h],
                                 V(self.ps[:, 7, hs], ("ps", 7)), ALU.mult, ALU.add)
                if t == self.ntp - 1:
                    self.dma("sp", V(self.dram["ret_p"][l].rearrange("h d e -> d h e")),
                             V(self.S_ret[:, l, :].rearrange("p (h e) -> p h e", h=4), *SK), out_dma=True)
                else:
                    decb = V(self.ap(c["rdec"], 8, [[1, 4], [0, 128]]), "k_rdec")
                    self.tt("dve", V(Sdb[:, l, :].rearrange("p (h e) -> p h e", h=4), ("Sdbf_ret", l)),
                            V(self.S_ret[:, l, :].rearrange("p (h e) -> p h e", h=4), *SK), decb, ALU.mult)
            else:
                self.sample_states(l, "st_ret", "ret_s", qT, kr, vbf, attm, 6,
                                   lambda h, dst, src: self.ts("dve", dst, src, V(c["rdec"][:, 1, 2, h:h + 1], "k_rdec"), ALU.mult))
            if li + 1 < len(tiles):
                self.proj_tok(li + 1, slots, [0, 1, 2, 3])
            ov = V(self.ps[:, 6, :].rearrange("p (h e) -> p h e", h=4), ("ps", 6))
            self.headnorm(ov, V(sil[:, :], "sil"), li, center=True)
        self.release(m0)

    def sample_states(self, l, st_in, st_out, qT, kk, vv, attm, obank, decay_fn, post_fn=None, nm=("qT", "kr", "vbf", "attm")):
        SBN = 4
        m0 = self.mark()
        sb = lambda n, shp, dt=F32: self.sb(n, shp, dt, scratch=True)
        qTblk = sb("qTblk", [128, NSEQ, 128], BF16)
        vblk = sb("vblk", [128, NSEQ, 128], BF16)
        sst, sso = [sb(n, [128, 2, SBN * 128]) for n in ("sst", "sso")]
        ssd = sb("ssd_", [128, 2, SBN * 128]) if decay_fn is not None else sst
        if decay_fn is None:
            self.alias["ssd_"] = self.alias["sst"]
        ssb = sb("ssb", [128, 2, SBN * 128], BF16)
        for nm_ in ("sst", "sso", "ssb") + (("ssd_",) if decay_fn is not None else ()):
            g = self.alias[nm_]
            hlf = len(g) // 2
            self.alias[nm_ + "0"] = g[:hlf]
            self.alias[nm_ + "1"] = g[hlf:]
        if decay_fn is None:
            self.alias["ssd_0"], self.alias["ssd_1"] = self.alias["sst0"], self.alias["sst1"]
        self.memset("pool", V(qTblk[:, :, :], "qTblk"), 0.0)
        c = self.c
        it = 0
        steps = [(h, sg) for h in range(4) for sg in range(0, NSEQ, SBN)]

        def load(step):
            hh, sgg = steps[step]
            self.dma("sp", V(sst[:, step % 2, :].rearrange("p (s e) -> p s e", s=SBN), "sst%d" % (step % 2)),
                     V(self.dram[st_in][l, sgg:sgg + SBN, hh].rearrange("s d e -> d s e")))
        load(0)
        for h in range(4):
            hs = slice(h * 128, (h + 1) * 128)
            dst = self.ap(qTblk, 0, [[128 + DSEQ, NSEQ], [1, DSEQ]])
            src = self.ap(qT, h * 128, [[DSEQ, NSEQ], [1, DSEQ]])
            self.cp("dve", V(dst, "qTblk"), V(src, nm[0]))
            self.tt("dve", V(vblk[:, :, :], "vblk"), V(self.ap(vv, h * 128, [[0, NSEQ], [1, 128]]), nm[2]),
                    V(self.ap(c["seqsel"], 0, [[1, NSEQ], [0, 128]]), "k_seqsel"), ALU.mult)
            self.mm(V(self.ps[:, obank, hs], ("ps", obank)), V(attm[:, hs], nm[3]), V(vv[:, hs], nm[2]), start=True, stop=False)
            for sg in range(0, NSEQ, SBN):
                i2 = it % 2
                it += 1
                if it < len(steps):
                    load(it)
                if decay_fn is not None:
                    sd = V(ssd[:, i2, :], "ssd_%d" % i2)
                    decay_fn(h, sd, V(sst[:, i2, :], "sst%d" % i2))
                else:
                    sd = V(sst[:, i2, :], "sst%d" % i2)
                self.cp("act", V(ssb[:, i2, :], "ssb%d" % i2), sd)
                for s in range(SBN):
                    self.mm(V(self.ps[:, obank, hs], ("ps", obank)), V(qTblk[:, sg + s, :], "qTblk"),
                            V(ssb[:, i2, s * 128:(s + 1) * 128], "ssb%d" % i2), start=False, stop=(sg + s == NSEQ - 1))
                pb = i2
                for s in range(SBN):
                    self.mm(V(self.ps[:, pb, s * 128:(s + 1) * 128], ("ps", pb)), V(kk[:, hs], nm[1]), V(vblk[:, sg + s, :], "vblk"))
                so = V(sso[:, i2, :], "sso%d" % i2)
                self.tt("dve", so, self.psb(pb), sd, ALU.add)
                if post_fn is not None:
                    post_fn(h, sg, SBN, V(sso[:, i2, :].rearrange("p (s e) -> p s e", s=SBN), "sso%d" % i2))
                self.dma("sp", V(self.dram[st_out][l, sg:sg + SBN, h].rearrange("s d e -> d s e")),
                         V(sso[:, i2, :].rearrange("p (s e) -> p s e", s=SBN), "sso%d" % i2), out_dma=True)
        self.release(m0)

    def norm_scratch(self, reuse=None):
        sb = lambda n, shp, dt=F32: self.sb(n, shp, dt, scratch=True)
        if reuse is not None:
            self.sqb = reuse[0]
            self.alias["sqb"] = self.alias[reuse[1]]
        else:
            self.sqb = sb("sqb", [128, 512])
        self.onb = sb("onb", [128, 512])
        self.obf = sb("obf", [128, 512], BF16)

    def headnorm(self, ov, mulv, li, center, nh=4, hd=128):
        si = self.stat()
        sk = ("st", si)
        sq = V(self.sqb[:, :].rearrange("p (h e) -> p h e", h=nh), "sqb")
        self.act(sq, ov, AF.Square)
        ssq = V(self.st[:, si, 4:4 + nh], sk)
        self.red(ssq, sq)
        var = V(self.st[:, si, 8:8 + nh], sk)
        if center:
            sm = V(self.st[:, si, 0:nh], sk)
            self.red(sm, ov)
            self.ts("dve", sm, sm, 1.0 / hd, ALU.mult)
            msq = V(self.st[:, si, 12:12 + nh], sk)
            self.tt("dve", msq, sm, sm, ALU.mult)
            self.ts("dve", msq, msq, -EPS, ALU.add)
            self.stt(var, ssq, 1.0 / hd, msq, ALU.mult, ALU.subtract)
        else:
            self.ts("dve", var, ssq, 1.0 / hd, ALU.mult, EPS, ALU.add)
        self.tt("pool", var, var, V(self.c["nhalf"][:, 0:nh], "k_nhalf"), ALU.pow)
        on = self.onb
        self.split_alias("onb", nh)
        if center:
            nb = V(self.st[:, si, 12:12 + nh], sk)
            self.stt(nb, V(self.st[:, si, 0:nh], sk), -1.0, var, ALU.mult, ALU.mult)
        for h in range(nh):
            o_h = V(ov.ap[:, h, :], *ov.keys)
            dst = V(on[:, h * hd:(h + 1) * hd], "onb%d" % h)
            if center:
                self.act(dst, o_h, AF.Identity, bias=V(self.st[:, si, 12 + h:13 + h], sk), scale=V(self.st[:, si, 8 + h:9 + h], sk))
            else:
                self.act(dst, o_h, AF.Copy, scale=V(self.st[:, si, 8 + h:9 + h], sk))
        self.tt("dve", V(self.obf[:, :], "obf"), V(on[:, :], "onb"), mulv, ALU.mult)
        self.o_to_oT(li)

    def o_to_oT(self, li):
        pT = self.ps[:, 4, :].bitcast(BF16)
        for k in range(4):
            self.tr(V(pT[:, k * 128:(k + 1) * 128], ("ps", 4)), V(self.obf[:, k * 128:(k + 1) * 128], "obf"),
                    V(self.c["ident_bf"][:, :], "k_ident_bf"))
        dst = self.ap(self.oT, li * 128, [[self.smax * 128, 4], [1, 128]])
        self.cp("act", V(dst, ("oT", li)), V(pT[:, 0:512].rearrange("p (k t) -> p k t", k=4), ("ps", 4)))


    def phase_hg(self, l, tiles):
        c = self.c
        L = self.depth
        m0 = self.mark()
        sb = lambda n, shp, dt=F32: self.sb(n, shp, dt, scratch=True)
        qh, kh, ivb, qT, kT, attm = [sb(n, [128, 512], BF16) for n in ("hqh", "hkh", "hiv", "hqT", "hkT", "hattm")]
        lb, oml, sg, uu, kf, logf, eG, mulv = [sb(n, [128, 512]) for n in ("hlb", "homl", "hsg", "huu", "hkf", "hlogf", "heG", "hmulv")]
        hgn = sb("hgn", [128, 128])
        scal = sb("hscal", [128, 4, NSEQ])
        smid = sb("hsmid", [128, 512], BF16) if any(t != self.ntp for t in tiles) else None
        ptmp = sb("hptmp", [128, 512]) if any(t != self.ntp for t in tiles) else None
        if ptmp is not None:
            self.split_alias("hptmp", 4)
        HK = [("S_hg", (l, hh_)) for hh_ in range(4)]
        self.norm_scratch(reuse=(sg, "hsg"))
        if l == 0:
            self.memset("pool", V(lb[:, :], "hlb"), 0.0)
        else:
            mlb = self.mark()
            lg = sb("hlg", [128, L, 512])
            mx = sb("hmx", [128, 512])
            den = sb("hden", [128, 512])
            self.dma("sp", V(lg[:, :, :], "hlg"), V(self.dram["hg_lb_logits"][:, :].partition_broadcast(128)))
            self.tt("dve", V(mx[:, :], "hmx"), V(lg[:, 0, :], "hlg"), V(lg[:, 1, :], "hlg"), ALU.max)
            for i in range(2, L):
                self.tt("dve", V(mx[:, :], "hmx"), V(mx[:, :], "hmx"), V(lg[:, i, :], "hlg"), ALU.max)
            for i in range(L):
                self.tt("dve", V(lg[:, i, :], "hlg"), V(lg[:, i, :], "hlg"), V(mx[:, :], "hmx"), ALU.subtract)
            self.act(V(lg[:, :, :], "hlg"), V(lg[:, :, :], "hlg"), AF.Exp)
            self.tt("dve", V(den[:, :], "hden"), V(lg[:, 0, :], "hlg"), V(lg[:, 1, :], "hlg"), ALU.add)
            for i in range(2, L):
                self.tt("dve", V(den[:, :], "hden"), V(den[:, :], "hden"), V(lg[:, i, :], "hlg"), ALU.add)
            self.recip(V(den[:, :], "hden"), V(den[:, :], "hden"))
            self.cp("dve", V(lb[:, :], "hlb"), V(lg[:, 1, :], "hlg"))
            for i in range(2, l + 1):
                self.tt("dve", V(lb[:, :], "hlb"), V(lb[:, :], "hlb"), V(lg[:, i, :], "hlg"), ALU.add)
            self.tt("dve", V(lb[:, :], "hlb"), V(lb[:, :], "hlb"), V(den[:, :], "hden"), ALU.mult)
            self.release(mlb)
        self.ts("dve", V(oml[:, :], "homl"), V(lb[:, :], "hlb"), -1.0, ALU.mult, 1.0, ALU.add)
        self.bcast_load(hgn[:, :], "hgn", self.dram["hg_norm"][l:l + 1, :])
        self.memset("pool", V(attm[:, :], "hattm"), 0.0)
        slots = [self.win_chunk(l, c0) for c0 in (C_HQ, C_HF, C_HI, C_HG)]
        hoist_s5 = ("s5" in self.branches) and any(t != self.ntp for t in tiles)
        if hoist_s5:
            self.s5_prep_dma(l)
        Sl = self.S_hg
        self.proj_tok(0, slots, [0, 1, 2, 3])
        for li, t in enumerate(tiles):
            samp = (t == self.ntp)
            kind = 1 if samp else 0
            first = (t == 0)
            self.act(V(sg[:, :], "hsg"), self.psb(1), AF.Sigmoid)
            self.act(V(mulv[:, :], "hmulv"), self.psb(3), AF.Sigmoid)
            self.tt("dve", V(mulv[:, :].rearrange("p (h e) -> p h e", h=4), "hmulv"),
                    V(mulv[:, :].rearrange("p (h e) -> p h e", h=4), "hmulv"),
                    V(self.ap(hgn, 0, [[0, 4], [1, 128]]), "hgn"), ALU.mult)
            self.tt("dve", V(uu[:, :], "huu"), V(sg[:, :], "hsg"), V(oml[:, :], "homl"), ALU.mult)
            self.tt("dve", V(kf[:, :], "hkf"), V(oml[:, :], "homl"), V(uu[:, :], "huu"), ALU.subtract)
            self.tt("dve", V(uu[:, :], "huu"), V(uu[:, :], "huu"), V(lb[:, :], "hlb"), ALU.add)
            self.act(V(logf[:, :], "hlogf"), V(uu[:, :], "huu"), AF.Ln)
            self.mm(self.psb(5), V(c["tric"][:, kind, :], "k_tric"), V(logf[:, :], "hlogf"))
            ncol = NSEQ if samp else 3
            selv = V(c["seqsel"][:, :], "k_seqsel") if samp else V(c["sel3"][:, :], "k_sel3")
            for h in range(4):
                self.mm(V(self.ps[:, 7, h * NSEQ:h * NSEQ + ncol], ("ps", 7)), V(logf[:, h * 128:(h + 1) * 128], "hlogf"), selv)
            sc_ps = self.ps[:, 7, 0:4 * NSEQ].rearrange("p (h n) -> p h n", h=4)
            self.act(V(scal[:, :, 0:ncol], "hscal"), V(sc_ps[:, :, 0:ncol], ("ps", 7)), AF.Exp)
            self.act(V(eG[:, :], "heG"), self.psb(5), AF.Exp)
            self.act(V(uu[:, :], "huu"), self.psb(5), AF.Exp, scale=-1.0)
            self.act(V(sg[:, :], "hsg"), self.psb(0), AF.Silu)
            self.tt("dve", V(qh[:, :], "hqh"), V(sg[:, :], "hsg"), V(eG[:, :], "heG"), ALU.mult)
            self.tt("dve", V(kh[:, :], "hkh"), V(kf[:, :], "hkf"), V(uu[:, :], "huu"), ALU.mult)
            self.cp("act", V(ivb[:, :], "hiv"), self.psb(2))
            self.transpose_to([(qh, "hqh"), (kh, "hkh")], 4, [(qT, "hqT"), (kT, "hkT")])
            for h in range(4):
                hs = slice(h * 128, (h + 1) * 128)
                self.mm(V(self.ps[:, 5, hs], ("ps", 5)), V(kT[:, hs], "hkT"), V(qT[:, hs], "hqT"))
            if samp:
                mk = V(self.ap(c["maskT"], 128, [[0, 4], [1, 128]]), "k_maskT")
                self.tt("dve", V(attm[:, :].rearrange("p (h i) -> p h i", h=4), "hattm"),
                        V(self.ps[:, 5, :].rearrange("p (h i) -> p h i", h=4), ("ps", 5)), mk, ALU.mult)
            else:
                a3 = attm[:, :].rearrange("p (h i) -> p h i", h=4)
                p3 = self.ps[:, 5, :].rearrange("p (h i) -> p h i", h=4)
                m3 = c["maskT"][:, 0, :]
                self.tt("dve", V(a3[0:64, :, :], "hattm"), V(p3[0:64, :, :], ("ps", 5)),
                        V(bass.AP(c["maskT"], 0, [[256, 64], [0, 4], [1, 128]]), "k_maskT"), ALU.mult)
                self.tt("dve", V(a3[64:128, :, 64:128], "hattm"), V(p3[64:128, :, 64:128], ("ps", 5)),
                        V(bass.AP(c["maskT"], 64 * 256 + 64, [[256, 64], [0, 4], [1, 64]]), "k_maskT"), ALU.mult)
            if not samp:
                if not first:
                    for h in range(4):
                        hs = slice(h * 128, (h + 1) * 128)
                        self.ts("dve", V(smid[:, hs], "hsmid"), V(Sl[:, l, hs], ("S_hg", (l, h))), V(scal[:, h, 0:1], "hscal"), ALU.mult)
                for h in range(4):
                    hs = slice(h * 128, (h + 1) * 128)
                    self.mm(V(self.ps[:, 6, hs], ("ps", 6)), V(attm[:, hs], "hattm"), V(ivb[:, hs], "hiv"), start=True, stop=first)
                    if not first:
                        self.mm(V(self.ps[:, 6, hs], ("ps", 6)), V(qT[:, hs], "hqT"), V(smid[:, hs], "hsmid"), start=False, stop=True)
                for h in range(4):
                    hs = slice(h * 128, (h + 1) * 128)
                    self.mm(V(self.ps[:, 7, hs], ("ps", 7)), V(kh[:, hs], "hkh"), V(ivb[:, hs], "hiv"))
                for h in range(4):
                    hs = slice(h * 128, (h + 1) * 128)
                    if first:
                        self.ts("dve", V(Sl[:, l, hs], ("S_hg", (l, h))), V(self.ps[:, 7, hs], ("ps", 7)), V(scal[:, h, 2:3], "hscal"), ALU.mult)
                    else:
                        self.ts("dve", V(ptmp[:, hs], "hptmp%d" % h), V(self.ps[:, 7, hs], ("ps", 7)), V(scal[:, h, 2:3], "hscal"), ALU.mult)
                        self.stt(V(Sl[:, l, hs], ("S_hg", (l, h))), V(Sl[:, l, hs], ("S_hg", (l, h))), V(scal[:, h, 1:2], "hscal"),
                                 V(ptmp[:, hs], "hptmp%d" % h), ALU.mult, ALU.add)
                if t == self.ntp - 1:
                    self.dma("sp", V(self.dram["hg_p"][l].rearrange("h d e -> d h e")),
                             V(Sl[:, l, :].rearrange("p (h e) -> p h e", h=4), *HK), out_dma=True)
            else:
                def post(h, sg, nb, so3):
                    self.tt("dve", so3, so3, V(self.ap(scal, h * NSEQ + sg, [[1, nb], [0, 128]]), "hscal"), ALU.mult)
                self.sample_states(l, "st_hg", "hg_s", qT, kh, ivb, attm, 6, None, post_fn=post,
                                   nm=("hqT", "hkh", "hiv", "hattm"))
            if li + 1 < len(tiles):
                self.proj_tok(li + 1, slots, [0, 1, 2, 3])
            ov = V(self.ps[:, 6, :].rearrange("p (h e) -> p h e", h=4), ("ps", 6))
            self.headnorm(ov, V(mulv[:, :], "hmulv"), li, center=False)
        if hoist_s5:
            self.s5_prep_compute()
        self.release(m0)


    def phase_ssd(self, l, tiles):
        c = self.c
        m0 = self.mark()
        sb = lambda n, shp, dt=F32: self.sb(n, shp, dt, scratch=True)
        idf = V(c["ident_f"][:, :], "k_ident_f")
        idb = V(c["ident_bf"][:, :], "k_ident_bf")
        cw = sb("s_cw", [128, 8, 5])
        abd = sb("s_abd", [128, 3, 8])
        gno = sb("s_gno", [128, 512])
        self.norm_scratch()
        mw = self.mark()
        wrow = sb("s_wrow", [5, 1024])
        self.dma("sp", V(wrow[0:4, :], "s_wrow"), V(self.dram["ssd_conv_w"][l]))
        self.dma("sp", V(wrow[4:5, :], "s_wrow"), V(self.dram["ssd_conv_b"][l:l + 1, :]))
        for cb in range(8):
            self.tr(V(self.ps[:, 1, cb * 5:cb * 5 + 5], ("ps", 1)), V(wrow[0:5, cb * 128:(cb + 1) * 128], "s_wrow"),
                    V(c["ident_f"][0:5, 0:5], "k_ident_f"))
        self.cp("dve", V(cw[:, :, :], "s_cw"), V(self.ps[:, 1, 0:40].rearrange("p (a b) -> p a b", a=8), ("ps", 1)))
        self.release(mw)
        for i, nm in enumerate(("ssd_a_log", "ssd_dt_bias", "ssd_d")):
            self.bcast_load(abd[:, i, :], "s_abd", self.dram[nm][l:l + 1, :])
        self.act(V(abd[:, 0, :], "s_abd"), V(abd[:, 0, :], "s_abd"), AF.Exp)
        self.ts("dve", V(abd[:, 0, :], "s_abd"), V(abd[:, 0, :], "s_abd"), -1.0, ALU.mult)
        self.bcast_load(gno[:, :], "s_gno", self.dram["ssd_norm"][l:l + 1, :])
        sz_slot = self.win_chunk(l, C_SZ)
        xb_slots = [self.win_chunk(l, C_SXBC + i * 512) for i in range(2)]
        dt_slot = self.wload(("win", l, C_SDT, 8))
        ST, STb, XC = self.ST_ssd, self.STbf_ssd, self.xcarry
        for li, t in enumerate(tiles):
            samp = (t == self.ntp)
            kind = 1 if samp else 0
            first = (t == 0)
            nseq, T = (NSEQ, DSEQ) if samp else (1, 128)
            W = 3 + T
            m1 = self.mark()
            xpre = sb("s_xpre", [128, 8, nseq * W])
            xact = sb("s_xact", [128, 8, 128])
            tailc = sb("s_tail", [128, 8, nseq * 3])
            bcb = sb("s_bcb", [128, 4, 128], BF16)
            bmT = sb("s_bmT", [128, 256], BF16)
            sm = sb("s_sm", [128, 64])
            wm = sb("s_wm", [128, 1024], BF16)
            xdt = sb("s_xdt", [128, 512], BF16)
            xdw = sb("s_xdw", [128, 512], BF16)
            xsD = sb("s_xsD", [128, 512])
            ysl = sb("s_ysl", [128, 512])
            yy = sb("s_yy", [128, 512])
            m2 = self.mark()
            dAtri = sb("s_dAtri", [128, 1024])
            Lm = dAtri
            self.alias["s_L"] = self.alias["s_dAtri"]
            self.release(m2)
            hk = ("hT", li)
            self.proj_tok(li, [sz_slot], [0])
            for kt in range(8):
                self.mm(V(self.ps[:, 1, 0:8], ("ps", 1)), V(self.hT[:, kt, li * 128:(li + 1) * 128], hk),
                        self.wv(dt_slot, 8, 8, kt), start=(kt == 0), stop=(kt == 7))
            for cb in range(8):
                bk = 2 + cb // 4
                for kt in range(8):
                    self.mm(V(self.ps[:, bk, (cb % 4) * 128:(cb % 4 + 1) * 128], ("ps", bk)),
                            self.wv(xb_slots[cb // 4], 8, 512, kt, (cb % 4) * 128, 128),
                            V(self.hT[:, kt, li * 128:(li + 1) * 128], hk), start=(kt == 0), stop=(kt == 7))
            x4 = xpre[:, :, :].rearrange("p c (s w) -> p c s w", s=nseq)
            src = self.ps[:, 2:4, :].rearrange("p a (b s t) -> p (a b) s t", b=4, s=nseq)
            self.cp("act", V(x4[:, :, :, 3:W], "s_xpre"), V(src, ("ps", 2), ("ps", 3)))
            if samp:
                mh = self.mark()
                hist = sb("s_hist", [NSEQ * 3, 1024])
                self.dma("sp", V(hist[:, :], "s_hist"), V(self.dram["st_conv"][l].rearrange("s j c -> (s j) c")))
                for cb in range(8):
                    self.tr(V(self.ps[:, 5, cb * 48:(cb + 1) * 48], ("ps", 5)), V(hist[:, cb * 128:(cb + 1) * 128], "s_hist"),
                            V(c["ident_f"][0:48, 0:48], "k_ident_f"))
                self.cp("dve", V(x4[:, :, :, 0:3], "s_xpre"),
                        V(self.ps[:, 5, 0:384].rearrange("p (c s j) -> p c s j", c=8, s=NSEQ), ("ps", 5)))
                self.release(mh)
            elif first:
                self.memset("pool", V(x4[:, :, :, 0:3], "s_xpre"), 0.0)
            else:
                self.cp("dve", V(x4[:, :, 0, 0:3], "s_xpre"), V(XC[:, l, :, :], ("xcarry", l)))
            t4 = tailc[:, :, :].rearrange("p c (s j) -> p c s j", s=nseq)
            self.cp("dve", V(t4, "s_tail"), V(x4[:, :, :, T:W], "s_xpre"))
            if not samp:
                self.cp("dve", V(XC[:, l, :, :], ("xcarry", l)), V(tailc[:, :, :], "s_tail"))
            if samp or t == self.ntp - 1:
                nr = nseq * 3
                mh = self.mark()
                tout = sb("s_tout", [nr, 1024])
                for cb in range(8):
                    bk = 6 + cb // 4
                    self.tr(V(self.ps[0:nr, bk, (cb % 4) * 128:(cb % 4 + 1) * 128], ("ps", bk)), V(tailc[:, cb, :], "s_tail"), idf)
                self.cp("act", V(tout[:, :].rearrange("p (a b) -> p a b", a=2), "s_tout"), V(self.ps[0:nr, 6:8, :], ("ps", 6), ("ps", 7)))
                dst = self.dram["conv_s"][l].rearrange("s j c -> (s j) c") if samp else self.dram["conv_p"][l]
                self.dma("sp", V(dst), V(tout[:, :], "s_tout"), out_dma=True)
                self.release(mh)
            self.split_alias("s_xact", 8)
            for cb in range(8):
                a4 = xact[:, cb, :].rearrange("p (s t) -> p s t", s=nseq)
                xin = x4[:, cb, :, :]
                self.act(V(a4, "s_xact%d" % cb), V(xin[:, :, 0:T], "s_xpre"), AF.Identity,
                         bias=V(cw[:, cb, 4:5], "s_cw"), scale=V(cw[:, cb, 0:1], "s_cw"))
            for cb in range(8):
                a4 = xact[:, cb, :].rearrange("p (s t) -> p s t", s=nseq)
                xin = x4[:, cb, :, :]
                for j in range(1, 4):
                    self.stt(V(a4, "s_xact%d" % cb), V(xin[:, :, j:j + T], "s_xpre"), V(cw[:, cb, j:j + 1], "s_cw"),
                             V(a4, "s_xact%d" % cb), ALU.mult, ALU.add)
            self.act(V(xact[:, :, :], "s_xact"), V(xact[:, :, :], "s_xact"), AF.Silu)
            self.act(V(ysl[:, :], "s_ysl"), self.psb(0), AF.Silu)
            self.cp("dve", V(bcb[:, :, :], "s_bcb"), V(xact[:, 4:8, :], "s_xact"))
            for cb in range(4):
                self.tr(V(self.ps[:, 4, cb * 128:(cb + 1) * 128], ("ps", 4)), V(xact[:, cb, :], "s_xact"), idf)
            pT5 = self.ps[:, 5, 256:384].bitcast(BF16)
            for g in range(2):
                self.tr(V(pT5[:, g * 128:(g + 1) * 128], ("ps", 5)), V(bcb[:, g, :], "s_bcb"), idb)
            self.cp("act", V(bmT[:, :], "s_bmT"), V(pT5, ("ps", 5)))
            dt = V(sm[:, 0:8], "s_sm"); dA = V(sm[:, 8:16], "s_sm"); E = V(sm[:, 16:24], "s_sm")
            dend = V(sm[:, 24:32], "s_sm"); etot = V(sm[:, 32:40], "s_sm"); cum = V(sm[:, 40:48], "s_sm")
            self.tt("dve", dt, V(self.ps[:, 1, 0:8], ("ps", 1)), V(abd[:, 1, :], "s_abd"), ALU.add)
            self.act(dt, dt, AF.Exp)
            self.act(dt, dt, AF.Ln, bias=1.0)
            self.tt("dve", dA, dt, V(abd[:, 0, :], "s_abd"), ALU.mult)
            self.mm(V(self.ps[:, 1, 8:16], ("ps", 1)), V(c["trif"][:, kind, :], "k_trif"), dA)
            self.mm(V(self.ps[:, 1, 16:24], ("ps", 1)), V(c["blk"][:, kind, :], "k_blk"), dA)
            self.cp("dve", cum, V(self.ps[:, 1, 8:16], ("ps", 1)))
            self.act(E, cum, AF.Exp)
            self.tt("dve", dend, V(self.ps[:, 1, 16:24], ("ps", 1)), cum, ALU.subtract)
            self.act(dend, dend, AF.Exp)
            self.act(etot, V(self.ps[:, 1, 16:24], ("ps", 1)), AF.Exp)
            self.tt("dve", V(dAtri[:, :].rearrange("p (h i) -> p h i", h=8), "s_dAtri"),
                    V(self.ap(c["trif"], kind * 128, [[0, 8], [1, 128]]), "k_trif"),
                    V(self.ap(sm, 8, [[1, 8], [0, 128]]), "s_sm"), ALU.mult)
            for hf in range(2):
                bk = 2 + hf
                self.mm(self.psb(bk), V(c["blk"][:, 0, :], "k_blk"), V(dAtri[:, hf * 512:(hf + 1) * 512], "s_dAtri"),
                        start=True, stop=False)
                self.mm(self.psb(bk), V(c["ntrif"][:, kind, :], "k_ntrif"),
                        V(self.ap(sm, 8 + hf * 4, [[1, 4], [0, 128]]), "s_sm"), start=False, stop=False)
                self.mm(self.psb(bk), idf, V(self.ap(c["negm"], kind * 128, [[0, 4], [1, 128]]), "k_negm"),
                        start=False, stop=True)
            self.act(V(Lm[:, :].rearrange("p (a b) -> p a b", a=2), "s_L"), V(self.ps[:, 2:4, :], ("ps", 2), ("ps", 3)), AF.Exp)
            for g in range(2):
                self.mm(V(self.ps[:, 5, g * 128:(g + 1) * 128], ("ps", 5)), V(bcb[:, g, :], "s_bcb"), V(bcb[:, 2 + g, :], "s_bcb"))
            self.tt("dve", V(wm[:, :].rearrange("p (g r i) -> p g r i", g=2, r=4), "s_wm"),
                    V(Lm[:, :].rearrange("p (g r i) -> p g r i", g=2, r=4), "s_L"),
                    V(self.ap(self.ps, 5 * 512, [[128, 2], [0, 4], [1, 128]]), ("ps", 5)), ALU.mult)
            xs3 = V(self.ps[:, 4, :].rearrange("p (h q) -> p h q", h=8), ("ps", 4))
            bc = lambda o: V(self.ap(sm, o, [[1, 8], [0, 64]]), "s_sm")
            self.tt("dve", V(xdt[:, :].rearrange("p (h q) -> p h q", h=8), "s_xdt"), xs3, bc(0), ALU.mult)
            self.tt("dve", V(xdw[:, :].rearrange("p (h q) -> p h q", h=8), "s_xdw"),
                    V(xdt[:, :].rearrange("p (h q) -> p h q", h=8), "s_xdt"), bc(24), ALU.mult)
            self.tt("dve", V(xsD[:, :].rearrange("p (h q) -> p h q", h=8), "s_xsD"), xs3,
                    V(self.ap(abd, 16, [[1, 8], [0, 64]]), "s_abd"), ALU.mult)
            for h in range(8):
                self.mm(V(self.ps[:, 6, h * 64:(h + 1) * 64], ("ps", 6)), V(wm[:, h * 128:(h + 1) * 128], "s_wm"),
                        V(xdt[:, h * 64:(h + 1) * 64], "s_xdt"))
            et3 = V(self.ap(sm, 32, [[1, 8], [0, 64]]), "s_sm")
            have_inter = samp or not first
            if not samp:
                if not first:
                    for g in range(2):
                        self.mm(V(self.ps[:, 7, g * 256:(g + 1) * 256], ("ps", 7)), V(bcb[:, 2 + g, :], "s_bcb"),
                                V(STb[:, l, g * 256:(g + 1) * 256], ("STbf_ssd", l)))
                for g in range(2):
                    self.mm(V(self.ps[:, 2, g * 256:(g + 1) * 256], ("ps", 2)), V(bmT[:, g * 128:(g + 1) * 128], "s_bmT"),
                            V(xdw[:, g * 256:(g + 1) * 256], "s_xdw"))
                Sl = V(ST[:, l, :], ("ST_ssd", l))
                if first:
                    self.cp("dve", Sl, self.psb(2))
                else:
                    self.tt("dve", V(ST[:, l, :].rearrange("p (h q) -> p h q", h=8), ("ST_ssd", l)),
                            V(ST[:, l, :].rearrange("p (h q) -> p h q", h=8), ("ST_ssd", l)), et3, ALU.mult)
                    self.tt("dve", Sl, self.psb(2), Sl, ALU.add)
                inter_v = [V(self.ps[:, 7, :].rearrange("p (h q) -> p h q", h=8), ("ps", 7))]
            else:
                self.top = max(self.top, m2 + 4096 + 64)
                inter_v = self.ssd_sample(l, bcb, bmT, xdw, sm)
            y3 = V(yy[:, :].rearrange("p (h q) -> p h q", h=8), "s_yy")
            if have_inter:
                if len(inter_v) == 1:
                    self.tt("dve", y3, inter_v[0], bc(16), ALU.mult)
                else:
                    for g in range(2):
                        self.tt("dve", V(yy[:, g * 256:(g + 1) * 256].rearrange("p (h q) -> p h q", h=4), "s_yy"), inter_v[g],
                                V(self.ap(sm, 16 + g * 4, [[1, 4], [0, 64]]), "s_sm"), ALU.mult)
                self.tt("dve", V(yy[:, :], "s_yy"), self.psb(6), V(yy[:, :], "s_yy"), ALU.add)
                self.tt("dve", V(yy[:, :], "s_yy"), V(yy[:, :], "s_yy"), V(xsD[:, :], "s_xsD"), ALU.add)
            else:
                self.tt("dve", V(yy[:, :], "s_yy"), self.psb(6), V(xsD[:, :], "s_xsD"), ALU.add)
            self.tt("dve", V(yy[:, :], "s_yy"), V(yy[:, :], "s_yy"), V(ysl[:, :], "s_ysl"), ALU.mult)
            if (not samp) and t != self.ntp - 1:
                self.cp("act", V(STb[:, l, :], ("STbf_ssd", l)), V(ST[:, l, :], ("ST_ssd", l)))
            if (not samp) and t == self.ntp - 1:
                sto = sb("s_sto", [128, 4, 128])
                for k in range(4):
                    self.tr(V(self.ps[:, 3, k * 128:(k + 1) * 128], ("ps", 3)), V(ST[:, l, k * 128:(k + 1) * 128], ("ST_ssd", l)), idf)
                self.cp("act", V(sto[:, :, :], "s_sto"), V(self.ps[:, 3, :].rearrange("p (k n) -> p k n", k=4), ("ps", 3)))
                self.dma("sp", V(self.dram["ssd_p"][l].rearrange("h p n -> (h p) n").rearrange("(k q) n -> q k n", q=128)),
                         V(sto[:, :, :], "s_sto"), out_dma=True)
            self.headnorm(V(yy[:, :].rearrange("p (g e) -> p g e", g=2), "s_yy"), V(gno[:, :], "s_gno"), li, center=False, nh=2, hd=256)
            self.release(m1)
        self.release(m0)

    def ssd_sample(self, l, bcb, bmT, xdw, sm):
        c = self.c
        sb = lambda n, shp, dt=F32: self.sb(n, shp, dt, scratch=True)
        idf = V(c["ident_f"][:, :], "k_ident_f")
        dAs = sb("s_dAs", [128, NSEQ, 8])
        etS = sb("s_etS", [128, NSEQ, 8])
        sin2 = sb("s_sin", [128, 2, 512])
        g_ = self.alias["s_sin"]
        self.alias["s_sin0"], self.alias["s_sin1"] = g_[:len(g_) // 2], g_[len(g_) // 2:]

        def load_state(s_):
            self.dma("sp", V(sin2[:, s_ % 2, :].rearrange("q (k n) -> q k n", k=4), "s_sin%d" % (s_ % 2)),
                     V(self.dram["st_ssd"][l, s_].rearrange("h p n -> (h p) n").rearrange("(k q) n -> q k n", q=128)))
        load_state(0)
        sbf = sb("s_sbf", [128, 512], BF16)
        sdd = sb("s_sdd", [128, 512])
        sout = sb("s_sout", [128, 512])
        xdws = sb("s_xdws", [128, 512], BF16)
        yacc = sb("s_yacc", [128, 512])
        self.tt("dve", V(dAs[:, :, :], "s_dAs"), V(self.ap(c["seqsel"], 0, [[1, NSEQ], [0, 8]]), "k_seqsel"),
                V(self.ap(sm, 8, [[0, NSEQ], [1, 8]]), "s_sm"), ALU.mult)
        self.mm(V(self.ps[:, 1, 128:256], ("ps", 1)), V(c["blk"][:, 0, :], "k_blk"), V(dAs[:, :, :].rearrange("p s h -> p (s h)"), "s_dAs"))
        self.act(V(etS[:, :, :].rearrange("p s h -> p (s h)"), "s_etS"), V(self.ps[:, 1, 128:256], ("ps", 1)), AF.Exp)
        for s in range(NSEQ):
            sel = V(c["seqsel"][:, s:s + 1], "k_seqsel")
            if s + 1 < NSEQ:
                load_state(s + 1)
            for k in range(4):
                self.tr(V(self.ps[:, 3, k * 128:(k + 1) * 128], ("ps", 3)),
                        V(sin2[:, s % 2, k * 128:(k + 1) * 128], "s_sin%d" % (s % 2)), idf)
            self.cp("act", V(sbf[:, :], "s_sbf"), self.psb(3))
            self.tt("dve", V(sdd[:, :].rearrange("p (h q) -> p h q", h=8), "s_sdd"),
                    V(self.ps[:, 3, :].rearrange("p (h q) -> p h q", h=8), ("ps", 3)),
                    V(self.ap(etS, s * 8, [[1, 8], [0, 64]]), "s_etS"), ALU.mult)
            for g in range(2):
                self.mm(V(self.ps[:, 7, g * 256:(g + 1) * 256], ("ps", 7)), V(bcb[:, 2 + g, :], "s_bcb"),
                        V(sbf[:, g * 256:(g + 1) * 256], "s_sbf"))
            if s == 0:
                self.ts("dve", V(yacc[:, :], "s_yacc"), self.psb(7), sel, ALU.mult)
            else:
                self.stt(V(yacc[:, :], "s_yacc"), self.psb(7), sel, V(yacc[:, :], "s_yacc"), ALU.mult, ALU.add)
            self.ts("dve", V(xdws[:, :], "s_xdws"), V(xdw[:, :], "s_xdw"), sel, ALU.mult)
            for g in range(2):
                self.mm(V(self.ps[:, 2, g * 256:(g + 1) * 256], ("ps", 2)), V(bmT[:, g * 128:(g + 1) * 128], "s_bmT"),
                        V(xdws[:, g * 256:(g + 1) * 256], "s_xdws"))
            self.tt("dve", V(sdd[:, :], "s_sdd"), self.psb(2), V(sdd[:, :], "s_sdd"), ALU.add)
            for k in range(4):
                self.tr(V(self.ps[:, 4, k * 128:(k + 1) * 128], ("ps", 4)), V(sdd[:, k * 128:(k + 1) * 128], "s_sdd"), idf)
            self.cp("act", V(sout[:, :], "s_sout"), self.psb(4))
            self.dma("sp", V(self.dram["ssd_s"][l, s].rearrange("h p n -> (h p) n").rearrange("(k q) n -> q k n", q=128)),
                     V(sout[:, :].rearrange("q (k n) -> q k n", k=4), "s_sout"), out_dma=True)
        return [V(yacc[:, :].rearrange("p (h q) -> p h q", h=8), "s_yacc")]


    def sincos_tab(self, dst, ang, tmp, tmpi, shift):
        inv2pi = 1.0 / (2.0 * math.pi)
        self.ts("dve", tmp, ang, inv2pi, ALU.mult, shift, ALU.add)
        self.cp("dve", tmpi, tmp)
        self.cp("dve", dst, tmpi)
        self.tt("dve", tmp, tmp, dst, ALU.subtract)
        self.act(dst, tmp, AF.Sin, scale=2.0 * math.pi)

    def s5_prep_dma(self, l):
        keep = lambda nm, shp, dt=F32: self.sb_top(nm, shp, dt)
        pv = keep("z_pv", [128, 20, 16])
        BW = keep("z_BW", [128, 2, 16, 128], BF16)
        Cch = keep("z_Cch", [128, 2, 16, 32], BF16)
        self.hi_keep = self.hi
        sb = lambda nm, shp, dt=F32: self.sb_top(nm, shp, dt)
        prow = sb("z_prow", [16, 3, 128])
        ldt16 = sb("z_ldt16", [16, 2])
        pvi = sb("z_pvi", [128, 16], mybir.dt.int32)
        braw = sb("z_braw", [128, 2, 16, 16])
        bbt = sb("z_bbt", [128, 3, 16, 16])
        pad = sb("z_pad", [128, 16, 96], BF16)
        crow = sb("z_crow", [128, 4, 128])
        self.dma("sp", V(prow[:, 0, :], "z_prow"), V(self.dram["s5_a_re"][l].rearrange("(ct q) -> ct q", q=128)))
        self.dma("sp", V(prow[:, 1, :], "z_prow"), V(self.dram["s5_a_im"][l].rearrange("(ct q) -> ct q", q=128)))
        self.dma("sp", V(ldt16[:, :], "z_ldt16"), V(self.dram["s5_log_dt"][l].rearrange("(ct g) -> ct g", g=2)))
        self.dma("sp", V(braw[:, 0], "z_braw"), V(self.dram["s5_b_re"][l].rearrange("(ct q) m -> q ct m", q=128)))
        self.dma("sp", V(braw[:, 1], "z_braw"), V(self.dram["s5_b_im"][l].rearrange("(ct q) m -> q ct m", q=128)))
        for part, nm in enumerate(("s5_c_re", "s5_c_im")):
            for blk in range(2):
                for j in range(8):
                    g0 = (blk * 8 + j) * 2
                    self.dma("sp",
                             V(crow[j * 16:(j + 1) * 16, part * 2 + blk, :].rearrange("m (gl p) -> m gl p", gl=2), "z_crow"),
                             V(self.dram[nm][l, g0:g0 + 2].rearrange("gl m p -> m gl p")))
        self.s5p = dict(l=l, pv=pv, BW=BW, Cch=Cch, prow=prow, ldt16=ldt16, pvi=pvi, braw=braw, bbt=bbt, pad=pad, crow=crow, done=False)

    def s5_prep_compute(self):
        c = self.c
        sp = self.s5p
        pv, BW, Cch = sp["pv"], sp["BW"], sp["Cch"]
        prow, ldt16, pvi, braw, bbt, pad, crow = (sp[n] for n in ("prow", "ldt16", "pvi", "braw", "bbt", "pad", "crow"))
        idf = V(c["ident_f"][:, :], "k_ident_f")
        idb = V(c["ident_bf"][:, :], "k_ident_bf")
        P = lambda i: V(pv[:, i, :], "z_pv")
        A_RE, A_IM, LDT, DT, DRE, DIM, MAG, COS, SIN, ABR, ABI, DEN, NRE, FRE, FIM, T0, T1, T2 = range(18)
        self.cp("dve", V(prow[:, 2, :].rearrange("c (g p) -> c g p", g=2), "z_prow"),
                V(bass.AP(ldt16, 0, [[2, 16], [1, 2], [0, 64]]), "z_ldt16"))
        for i in range(3):
            self.tr(V(self.ps[:, 0, i * 16:(i + 1) * 16], ("ps", 0)), V(prow[:, i, :], "z_prow"), V(c["ident_f"][0:16, 0:16], "k_ident_f"))
        self.cp("dve", V(pv[:, 0:3, :], "z_pv"), V(self.ps[:, 0, 0:48].rearrange("p (a b) -> p a b", a=3), ("ps", 0)))
        self.act(P(DT), P(LDT), AF.Exp)
        self.tt("dve", P(DRE), P(DT), P(A_RE), ALU.mult)
        self.tt("dve", P(DIM), P(DT), P(A_IM), ALU.mult)
        self.act(P(MAG), P(DRE), AF.Exp)
        self.sincos_tab(P(SIN), P(DIM), P(T0), V(pvi[:, :], "z_pvi"), 0.0)
        self.sincos_tab(P(COS), P(DIM), P(T0), V(pvi[:, :], "z_pvi"), 0.25)
        self.tt("dve", P(ABR), P(MAG), P(COS), ALU.mult)
        self.tt("dve", P(ABI), P(MAG), P(SIN), ALU.mult)
        self.tt("dve", P(DEN), P(A_RE), P(A_RE), ALU.mult)
        self.tt("dve", P(T0), P(A_IM), P(A_IM), ALU.mult)
        self.tt("dve", P(DEN), P(DEN), P(T0), ALU.add)
        self.recip(P(DEN), P(DEN))
        self.ts("dve", P(NRE), P(ABR), -1.0, ALU.add)
        self.tt("dve", P(T0), P(NRE), P(A_RE), ALU.mult)
        self.tt("dve", P(T1), P(ABI), P(A_IM), ALU.mult)
        self.tt("dve", P(T0), P(T0), P(T1), ALU.add)
        self.tt("dve", P(FRE), P(T0), P(DEN), ALU.mult)
        self.tt("dve", P(T0), P(ABI), P(A_RE), ALU.mult)
        self.tt("dve", P(T1), P(NRE), P(A_IM), ALU.mult)
        self.tt("dve", P(T0), P(T0), P(T1), ALU.subtract)
        self.tt("dve", P(FIM), P(T0), P(DEN), ALU.mult)
        fb = lambda i: V(self.ap(pv, i * 16, [[1, 16], [0, 16]]), "z_pv")
        for part in range(2):
            x0, x1 = (0, 1) if part == 0 else (1, 0)
            self.tt("dve", V(bbt[:, 0], "z_bbt"), V(braw[:, x0], "z_braw"), fb(FRE), ALU.mult)
            self.tt("dve", V(bbt[:, 1], "z_bbt"), V(braw[:, x1], "z_braw"), fb(FIM), ALU.mult)
            self.tt("dve", V(bbt[:, 2], "z_bbt"), V(bbt[:, 0], "z_bbt"), V(bbt[:, 1], "z_bbt"),
                    ALU.subtract if part == 0 else ALU.add)
            self.memset("pool", V(pad[:, :, :], "z_pad"), 0.0)
            for r in range(3):
                for gl in range(2):
                    dst = bass.AP(pad, gl * 64 * (16 * 96) + r * 96 + r * 32 + gl * 16, [[16 * 96, 64], [288, 5], [1, 16]])
                    src = bass.AP(bbt, gl * 64 * (3 * 256) + 2 * 256 + r * 16, [[3 * 256, 64], [48, 5], [1, 16]])
                    self.cp("dve", V(dst, "z_pad"), V(src, "z_bbt"))
            for gl in range(2):
                self.cp("dve", V(pad[gl * 64:(gl + 1) * 64, 15, 64 + gl * 16:64 + gl * 16 + 16], "z_pad"),
                        V(bbt[gl * 64:(gl + 1) * 64, 2, 15, :], "z_bbt"))
            for half in range(2):
                pT = self.ps[:, 1 + half, :].bitcast(BF16)
                for kk in range(8):
                    ct = half * 8 + kk
                    self.tr(V(pT[0:96, kk * 128:(kk + 1) * 128], ("ps", 1 + half)), V(pad[:, ct, :], "z_pad"), idb)
                self.cp("act", V(BW[0:96, part, half * 8:(half + 1) * 8, :].rearrange("p a b -> p (a b)"), "z_BW"),
                        V(pT[0:96, :], ("ps", 1 + half)))
        self.memset("pool", V(Cch[:, :, :, :], "z_Cch"), 0.0)
        for part in range(2):
            for blk in range(2):
                self.tr(V(self.ps[:, 2, blk * 128:(blk + 1) * 128], ("ps", 2)), V(crow[:, part * 2 + blk, :], "z_crow"), idf)
            for gl in range(2):
                self.act(V(Cch[gl * 64:(gl + 1) * 64, part, :, gl * 16:(gl + 1) * 16], "z_Cch"),
                         V(self.ps[gl * 64:(gl + 1) * 64, 2, 0:256].rearrange("p (a b) -> p a b", a=16), ("ps", 2)),
                         AF.Copy, scale=(1.0 if part == 0 else -1.0))
        self.hi = self.hi_keep
        sp["done"] = True

    def phase_s5(self, l, tiles):
        c = self.c
        n = len(tiles)
        ntok = n * 128
        has_samp = self.ntp in tiles
        m0 = self.mark()
        sb = lambda nm, shp, dt=F32: self.sb(nm, shp, dt, scratch=True)
        idf = V(c["ident_f"][:, :], "k_ident_f")
        idb = V(c["ident_bf"][:, :], "k_ident_bf")
        I32 = mybir.dt.int32
        if getattr(self, "s5p", None) is None or self.s5p["l"] != l:
            self.s5_prep_dma(l)
        if not self.s5p["done"]:
            self.s5_prep_compute()
        pv, BW, Cch = self.s5p["pv"], self.s5p["BW"], self.s5p["Cch"]
        A_RE, A_IM, LDT, DT, DRE, DIM, MAG, COS, SIN, ABR, ABI, DEN, NRE, FRE, FIM, T0, T1, T2 = range(18)
        dtab = sb("z_dtab", [128, 512])
        self.bcast_load(dtab[:, :], "z_dtab", self.dram["s5_d"][l:l + 1, :])
        self.norm_scratch()
        su_slot = self.win_chunk(l, C_SU)
        glu_slot = self.wload(("glu", l))
        uT8 = sb("z_uT8", [128, 6, ntok], BF16)
        self.memset("pool", V(uT8[:, :, :], "z_uT8"), 0.0)
        for li in range(n):
            for hb_ in range(2):
                bk = 2 + hb_
                for pl in range(3):
                    w = hb_ * 3 + pl
                    ncol = 96
                    c0w = min(w * 96, 512 - 96)
                    for kt in range(8):
                        lhs = V(self.ap(self.wr, su_slot * 4096 + kt * 512 + c0w, [[1, ncol]]), ("wr", su_slot))
                        self.mm(V(self.ps[0:ncol, bk, pl * 128:(pl + 1) * 128], ("ps", bk)), lhs,
                                V(self.hT[:, kt, li * 128:(li + 1) * 128], ("hT", li)), start=(kt == 0), stop=(kt == 7))
                if hb_ == 0:
                    self.cp("act", V(uT8[0:96, 0:3, li * 128:(li + 1) * 128], "z_uT8"),
                            V(self.ps[0:96, bk, 0:384].rearrange("p (a b) -> p a b", a=3), ("ps", bk)))
                else:
                    self.cp("dve", V(uT8[0:96, 3:6, li * 128:(li + 1) * 128], "z_uT8"),
                            V(self.ps[0:96, bk, 0:384].rearrange("p (a b) -> p a b", a=3), ("ps", bk)))
        ytok = sb("z_ytok", [128, n, 512])
        if has_samp:
            hin = sb("z_hin", [128, 2, 16, NSEQ])
            hfin = sb("z_hfin", [128, 2, 16, NSEQ])
            mp = self.mark()
            srow = sb("z_srow", [NSEQ, 2048])
            for part, nm in enumerate(("st_s5re", "st_s5im")):
                self.dma("sp", V(srow[:, :], "z_srow"), V(self.dram[nm][l]))
                for ct in range(16):
                    self.tr(V(self.ps[:, 4, ct * 16:(ct + 1) * 16], ("ps", 4)), V(srow[:, ct * 128:(ct + 1) * 128], "z_srow"),
                            V(c["ident_f"][0:16, 0:16], "k_ident_f"))
                self.cp("dve", V(hin[:, part].rearrange("p a b -> p (a b)"), "z_hin"), V(self.ps[:, 4, 0:256], ("ps", 4)))
            self.release(mp)
        hst = self.h_s5
        tabs = sb("z_tabs", [128, 3, 512])
        rhos = sb("z_rhos", [128, 512]) if has_samp else None
        cin = sb("z_cin", [128, 4 * NSEQ])
        mscan = self.mark()
        ang = sb("z_ang", [128, 512])
        tmpf = sb("z_tmpf", [128, 512])
        tmpi = sb("z_tmpi", [128, 512], I32)
        self.release(mscan)
        bpre = sb("z_bpre", [128, 2, 512])
        gsc = sb("z_gsc", [128, 2, 512])
        hh = sb("z_hh", [128, 2, 512])
        hbf = sb("z_hbf", [128, 2, 512], BF16)
        t4 = sb("z_t4", [128, 2, 512])
        t5 = sb("z_t5", [128, 2, 512])
        for nm_ in ("z_t4", "z_t5", "z_bpre", "z_gsc"):
            self.split_alias(nm_)
        for part in range(2):
            g0 = self.alias["z_hh"]
            hlf = len(g0) // 2
            self.alias["z_hh%d" % part] = g0[part * hlf:(part + 1) * hlf]
            g1 = self.alias["z_hbf"]
            hl2 = len(g1) // 2
            self.alias["z_hbf%d" % part] = g1[part * hl2:(part + 1) * hl2]
        for quad in range(4):
            for ctl in range(4):
                ct = quad * 4 + ctl
                self.ts("dve", V(ang[:, ctl * 128:(ctl + 1) * 128], "z_ang"), V(c["t1"][:, :], "k_t1"), V(pv[:, DIM, ct:ct + 1], "z_pv"), ALU.mult)
                self.ts("dve", V(tabs[:, 2, ctl * 128:(ctl + 1) * 128], "z_tabs"), V(c["zmask"][:, 0, :], "k_zmask"),
                        V(pv[:, MAG, ct:ct + 1], "z_pv"), ALU.mult)
                if has_samp:
                    self.ts("dve", V(rhos[:, ctl * 128:(ctl + 1) * 128], "z_rhos"), V(c["zmask"][:, 1, :], "k_zmask"),
                            V(pv[:, MAG, ct:ct + 1], "z_pv"), ALU.mult)
            self.sincos_tab(V(tabs[:, 1, :], "z_tabs"), V(ang[:, :], "z_ang"), V(tmpf[:, :], "z_tmpf"), V(tmpi[:, :], "z_tmpi"), 0.0)
            self.sincos_tab(V(tabs[:, 0, :], "z_tabs"), V(ang[:, :], "z_ang"), V(tmpf[:, :], "z_tmpf"), V(tmpi[:, :], "z_tmpi"), 0.25)
            for li, t in enumerate(tiles):
                samp = (t == self.ntp)
                first = (t == 0)
                for part in range(2):
                    bk = 5 + part
                    for ctl in range(4):
                        ct = quad * 4 + ctl
                        w = min(ct // 3, 5)
                        self.mm(V(self.ps[:, bk, ctl * 128:(ctl + 1) * 128], ("ps", bk)),
                                V(BW[0:96, part, ct, :], "z_BW"), V(uT8[0:96, w, li * 128:(li + 1) * 128], "z_uT8"))
                if samp:
                    Ct = V(self.ap(tabs, 0, [[128, 4], [0, NSEQ], [1, DSEQ]]), "z_tabs")
                    St = V(self.ap(tabs, 512, [[128, 4], [0, NSEQ], [1, DSEQ]]), "z_tabs")
                    rho = V(rhos[:, :], "z_rhos")
                    shp = lambda ap: ap.rearrange("p (a s t) -> p a s t", a=4, s=NSEQ)
                else:
                    Ct = V(tabs[:, 0, :], "z_tabs")
                    St = V(tabs[:, 1, :], "z_tabs")
                    rho = V(tabs[:, 2, :], "z_tabs")
                    shp = lambda ap: ap
                Bre = V(shp(self.ps[:, 5, :]), ("ps", 5))
                Bim = V(shp(self.ps[:, 6, :]), ("ps", 6))
                T0v, T1v = V(shp(t4[:, 0, :]), "z_t40"), V(shp(t4[:, 1, :]), "z_t41")
                self.tt("dve", T0v, Bre, Ct, ALU.mult)
                self.tt("dve", T1v, Bim, St, ALU.mult)
                self.tt("dve", V(bpre[:, 0, :], "z_bpre0"), V(t4[:, 0, :], "z_t40"), V(t4[:, 1, :], "z_t41"), ALU.add)
                self.tt("dve", T0v, Bim, Ct, ALU.mult)
                self.tt("dve", T1v, Bre, St, ALU.mult)
                self.tt("dve", V(bpre[:, 1, :], "z_bpre1"), V(t4[:, 0, :], "z_t40"), V(t4[:, 1, :], "z_t41"), ALU.subtract)
                for part in range(2):
                    if samp:
                        cv = V(cin[:, :].rearrange("p (a s) -> p a s", a=4), "z_cin")
                        self.tt("dve", cv, V(hin[:, part, quad * 4:quad * 4 + 4, :], "z_hin"),
                                V(self.ap(pv, MAG * 16 + quad * 4, [[1, 4], [0, NSEQ]]), "z_pv"), ALU.mult)
                        b0 = V(self.ap(bpre, part * 512, [[128, 4], [DSEQ, NSEQ]]), "z_bpre%d" % part)
                        self.tt("dve", b0, b0, cv, ALU.add)
                    elif not first:
                        cv = V(cin[:, 0:4], "z_cin")
                        self.tt("dve", cv, V(hst[:, l, part, quad * 4:quad * 4 + 4], ("h_s5", (l, part))),
                                V(pv[:, MAG, quad * 4:quad * 4 + 4], "z_pv"), ALU.mult)
                        b0 = V(self.ap(bpre, part * 512, [[128, 4]]), "z_bpre%d" % part)
                        self.tt("dve", b0, b0, cv, ALU.add)
                    self.add("dve", lambda e, o=gsc[:, part, :], d0=rho.ap, d1=bpre[:, part, :]:
                             e.tensor_tensor_scan(o, d0, d1, 0.0, ALU.mult, ALU.add),
                             r=["z_bpre%d" % part] + rho.keys, w=["z_gsc%d" % part])
                Gre, Gim = V(shp(gsc[:, 0, :]), "z_gsc0"), V(shp(gsc[:, 1, :]), "z_gsc1")
                U0v, U1v = V(shp(t5[:, 0, :]), "z_t50"), V(shp(t5[:, 1, :]), "z_t51")
                self.tt("dve", U0v, Gre, St, ALU.mult)
                self.tt("dve", T0v, Gre, Ct, ALU.mult)
                self.tt("dve", U1v, Gim, Ct, ALU.mult)
                self.tt("dve", T1v, Gim, St, ALU.mult)
                self.tt("dve", V(hh[:, 1, :], "z_hh1"), V(t5[:, 0, :], "z_t50"), V(t5[:, 1, :], "z_t51"), ALU.add)
                self.tt("dve", V(hh[:, 0, :], "z_hh0"), V(t4[:, 0, :], "z_t40"), V(t4[:, 1, :], "z_t41"), ALU.subtract)
                self.cp("act", V(hbf[:, 0, :], "z_hbf0"), V(hh[:, 0, :], "z_hh0"))
                self.cp("act", V(hbf[:, 1, :], "z_hbf1"), V(hh[:, 1, :], "z_hh1"))
                for part in range(2):
                    if samp:
                        self.cp("dve", V(hfin[:, part, quad * 4:quad * 4 + 4, :], "z_hfin"),
                                V(self.ap(hh, part * 512 + DSEQ - 1, [[128, 4], [DSEQ, NSEQ]]), "z_hh%d" % part))
                    else:
                        self.cp("dve", V(hst[:, l, part, quad * 4:quad * 4 + 4], ("h_s5", (l, part))),
                                V(self.ap(hh, part * 512 + 127, [[128, 4]]), "z_hh%d" % part))
                for ctl in range(4):
                    ct = quad * 4 + ctl
                    for part in range(2):
                        self.mm(V(self.ps[:, 7, ctl * 32:(ctl + 1) * 32], ("ps", 7)),
                                V(hbf[:, part, ctl * 128:(ctl + 1) * 128], "z_hbf%d" % part),
                                V(Cch[:, part, ct, :], "z_Cch"), start=(part == 0), stop=(part == 1))
                self.cp("act", V(ytok[:, li, quad * 128:(quad + 1) * 128], "z_ytok"), V(self.ps[:, 7, 0:128], ("ps", 7)))
        self.release(mscan)
        mp = self.mark()
        if self.ntp - 1 in tiles:
            po = sb("z_po", [16, 2, 128])
            for part, nm in enumerate(("s5re_p", "s5im_p")):
                self.tr(V(self.ps[0:16, 4, part * 128:(part + 1) * 128], ("ps", 4)), V(hst[:, l, part, :], ("h_s5", (l, part))), idf)
            self.cp("dve", V(po[:, :, :].rearrange("p a b -> p (a b)"), "z_po"), V(self.ps[0:16, 4, 0:256], ("ps", 4)))
            for part, nm in enumerate(("s5re_p", "s5im_p")):
                self.dma("sp", V(self.dram[nm][l].rearrange("(ct q) -> ct q", q=128)), V(po[:, part, :], "z_po"), out_dma=True)
        if has_samp:
            so = sb("z_so", [NSEQ, 2048])
            for part, nm in enumerate(("s5re_s", "s5im_s")):
                for ct in range(16):
                    bk = 2 + ct // 4
                    self.tr(V(self.ps[0:NSEQ, bk, (ct % 4) * 128:(ct % 4 + 1) * 128], ("ps", bk)), V(hfin[:, part, ct, :], "z_hfin"), idf)
                self.cp("act", V(so[:, :].rearrange("p (a b) -> p a b", a=4), "z_so"), V(self.ps[0:NSEQ, 2:6, :], *[("ps", b) for b in (2, 3, 4, 5)]))
                self.dma("sp", V(self.dram[nm][l]), V(so[:, :], "z_so"), out_dma=True)
        self.release(mp)
        ya = sb("z_ya", [128, 512])
        yb = sb("z_yb", [128, 512])
        zbf = sb("z_zbf", [128, 512], BF16)
        zT = sb("z_zT", [128, 4, 128], BF16)
        sgl = sb("z_sgl", [128, 512])
        for li, t in enumerate(tiles):
            self.proj_tok(li, [su_slot], [0])
            yv = V(ya[:, :], "z_ya")
            self.tt("dve", yv, self.psb(0), V(dtab[:, :], "z_dtab"), ALU.mult)
            self.tt("dve", yv, yv, V(ytok[:, li, :], "z_ytok"), ALU.add)
            wv_ = V(yb[:, :], "z_yb")
            self.tt("dve", wv_, yv, yv, ALU.mult)
            self.ts("dve", wv_, wv_, 0.044715, ALU.mult, 1.0, ALU.add)
            self.tt("dve", wv_, wv_, yv, ALU.mult)
            self.act(wv_, wv_, AF.Sigmoid, scale=2.0 * math.sqrt(2.0 / math.pi))
            self.tt("dve", V(ya[:, :], "z_ya"), yv, wv_, ALU.mult)
            self.cp("act", V(zbf[:, :], "z_zbf"), V(ya[:, :], "z_ya"))
            pT = self.ps[:, 1, :].bitcast(BF16)
            for k in range(4):
                self.tr(V(pT[:, k * 128:(k + 1) * 128], ("ps", 1)), V(zbf[:, k * 128:(k + 1) * 128], "z_zbf"), idb)
            self.cp("dve", V(zT[:, :, :].rearrange("p a b -> p (a b)"), "z_zT"), V(pT[:, 0:512], ("ps", 1)))
            for k in range(4):
                self.mm(self.psb(2), V(zT[:, k, :], "z_zT"), self.wv(glu_slot, 4, 512, k), start=(k == 0), stop=(k == 3))
            self.act(V(sgl[:, :], "z_sgl"), self.psb(2), AF.Sigmoid)
            self.tt("dve", V(self.obf[:, :], "obf"), V(ya[:, :], "z_ya"), V(sgl[:, :], "z_sgl"), ALU.mult)
            self.o_to_oT(li)
        self.release(m0)
        self.hi = SB_HI
        self.s5p = None

    def phase_gate(self, l, b, n, first):
        m0 = self.mark()
        sig = self.sb("sig", [128, 2, 512], scratch=True)
        gtmp = self.sb("gtmp", [128, 2, 512], scratch=True)
        self.split_alias("sig"); self.split_alias("gtmp")
        ntok = n * 128
        gs = [self.win_chunk(l, C_GL + b * 1024 + i * 512) for i in range(2)]
        bsl = self.wload(("wb", l, b))
        blocks = [(t0, min(512, ntok - t0)) for t0 in range(0, ntok, 512)]
        it = 0
        for cb in range(8):
            for (t0, nn) in blocks:
                lis = list(range(t0 // 128, (t0 + nn) // 128))
                bA, bB = (0, 1) if it % 2 == 0 else (2, 3)
                i2 = it % 2
                it += 1
                hk = [("hT", li) for li in lis]
                ok = [("oT", li) for li in lis]
                mk = [("mrg", li) for li in lis]
                for kt in range(8):
                    self.mm(V(self.ps[:, bA, 0:nn], ("ps", bA)), self.wv(gs[cb // 4], 8, 512, kt, (cb % 4) * 128, 128),
                            V(self.hT[:, kt, t0:t0 + nn], *hk), start=(kt == 0), stop=(kt == 7))
                for kt in range(4):
                    self.mm(V(self.ps[:, bB, 0:nn], ("ps", bB)), self.wv(bsl, 4, 1024, kt, cb * 128, 128),
                            V(self.oT[:, kt, t0:t0 + nn], *ok), start=(kt == 0), stop=(kt == 3))
                sg = V(sig[:, i2, 0:nn], "sig%d" % i2)
                self.act(sg, V(self.ps[:, bA, 0:nn], ("ps", bA)), AF.Sigmoid)
                dst = V(self.mrg[:, cb, t0:t0 + nn], *mk)
                if first:
                    self.tt("dve", dst, V(self.ps[:, bB, 0:nn], ("ps", bB)), sg, ALU.mult)
                else:
                    tmp = V(gtmp[:, i2, 0:nn], "gtmp%d" % i2)
                    self.tt("dve", tmp, V(self.ps[:, bB, 0:nn], ("ps", bB)), sg, ALU.mult)
                    self.tt("dve", dst, dst, tmp, ALU.add)
        self.release(m0)

    def resid_add(self, li, src, g, otmp, n=D):
        rstd = self.rms_rstd(src, n)
        tv = otmp[:, :]
        if len(src.ap.shape) == 3:
            tv = tv.rearrange("p (a b) -> p a b", a=src.ap.shape[1])
            gv = V(g.ap.rearrange("p (a b) -> p a b", a=src.ap.shape[1]), *g.keys)
        else:
            gv = g
        self.stt(V(tv, "otmp"), src, rstd, gv, ALU.mult, ALU.mult)
        xs = V(self.x[:, li, :], ("x", li))
        self.tt("dve", xs, xs, V(otmp[:, :], "otmp"), ALU.add)

    def phase_out(self, l, n):
        m0 = self.mark()
        mrgb = self.sb("mrgb", [128, 2, 1024], BF16, scratch=True)
        self.split_alias("mrgb")
        otmp = self.sb("otmp", [128, 1024], scratch=True)
        sl = [self.wload(("wout", l, i)) for i in range(2)]
        g = self.load_gain("g_post_mix", l)
        for li in range(n):
            i2 = li % 2
            mb = mrgb[:, i2, :].rearrange("p (k t) -> p k t", k=8)
            self.cp("act", V(mb, "mrgb%d" % i2), V(self.mrg[:, :, li * 128:(li + 1) * 128], ("mrg", li)))
            banks = (0, 1) if li % 2 == 0 else (2, 3)
            for ch in range(2):
                for kt in range(8):
                    self.mm(self.psb(banks[ch]), V(mb[:, kt, :], "mrgb%d" % i2), self.wv(sl[ch], 8, 512, kt),
                            start=(kt == 0), stop=(kt == 7))
            src = V(self.ps[:, banks[0]:banks[0] + 2, :], ("ps", banks[0]), ("ps", banks[1]))
            self.resid_add(li, src, g, otmp)
        self.release(m0)

    def phase_ffn(self, l, n):
        m0 = self.mark()
        uT = self.sb("uT", [128, 8, 512], BF16, scratch=True)
        urel = self.sb("urel", [128, 2, 512], scratch=True)
        self.split_alias("urel")
        otmp = self.sb("otmp", [128, 1024], scratch=True)
        ntok = n * 128
        g = self.load_gain("g_pre_ffn", l)
        for li in range(n):
            self.norm_to_hT(li, g)
        blocks = [(t0, min(512, ntok - t0)) for t0 in range(0, ntok, 512)]

        def yv(li, three=False):
            if three:
                return V(self.ap(self.mrg, li * 1024, [[512, 2], [1, 512]]), "mrg")
            return V(self.ap(self.mrg, li * 1024, [[1, 1024]]), "mrg")
        it = 0
        for q in range(4):
            if q > 0:
                self.phase_end()
            w1 = [self.wload(("ff1", l, q * 1024 + i * 512)) for i in range(2)]
            w2 = [self.wload(("ff2", l, q, i)) for i in range(2)]
            for (t0, nn) in blocks:
                lis = list(range(t0 // 128, (t0 + nn) // 128))
                hk = [("hT", li) for li in lis]
                for fb in range(8):
                    bU = 4 + (it % 2)
                    i2 = it % 2
                    it += 1
                    for kt in range(8):
                        self.mm(V(self.ps[:, bU, 0:nn], ("ps", bU)), self.wv(w1[fb // 4], 8, 512, kt, (fb % 4) * 128, 128),
                                V(self.hT[:, kt, t0:t0 + nn], *hk), start=(kt == 0), stop=(kt == 7))
                    ur = V(urel[:, i2, 0:nn], "urel%d" % i2)
                    self.ts("dve", ur, V(self.ps[:, bU, 0:nn], ("ps", bU)), 0.0, ALU.max)
                    self.act(V(uT[:, fb, 0:nn], "uT"), ur, AF.Square)
                for li in lis:
                    banks = (0, 1) if li % 2 == 0 else (2, 3)
                    tl = (li * 128 - t0)
                    for ch in range(2):
                        for fb in range(8):
                            self.mm(self.psb(banks[ch]), V(uT[:, fb, tl:tl + 128], "uT"),
                                    self.wv(w2[ch], 8, 512, fb), start=(fb == 0), stop=(fb == 7))
                    src = V(self.ps[:, banks[0]:banks[0] + 2, :], ("ps", banks[0]), ("ps", banks[1]))
                    if q == 0:
                        self.cp("act", yv(li, True), src)
                    else:
                        self.tt("dve", yv(li, True), src, yv(li, True), ALU.add)
        g2 = self.load_gain("g_post_ffn", l)
        for li in range(n):
            self.resid_add(li, yv(li), g2, otmp)
        self.release(m0)

    def build(self, emit=True):
        self.setup()
        for gi, tiles in enumerate(self.groups):
            n = len(tiles)
            for li, t in enumerate(tiles):
                self.dma("sp", V(self.x[:, li, :], ("x", li)), V(self.dram["xin"][t * 128:(t + 1) * 128, :]))
                self.dma("sp", V(self.c["rope"][:, li], "k_rope"), V(self.dram["c_rope"][:, t]))
            for l in range(self.depth):
                g = self.load_gain("g_pre_mix", l)
                for li in range(n):
                    self.norm_to_hT(li, g)
                first = True
                for b, name in enumerate(("ret", "ssd", "hg", "s5")):
                    if name not in self.branches:
                        continue
                    getattr(self, "phase_" + name)(l, tiles)
                    self.phase_end()
                    self.phase_gate(l, b, n, first)
                    self.phase_end()
                    first = False
                self.phase_out(l, n)
                self.phase_end()
                if self.ffn:
                    self.phase_ffn(l, n)
                    self.phase_end()
            for li, t in enumerate(tiles):
                self.dma("sp", V(self.dram["y"][t * 128:(t + 1) * 128, :]), V(self.x[:, li, :], ("x", li)), out_dma=True)
        if not emit:
            return None
        stats = self.s.emit(self.nc)
        stats["sbuf_peak"] = self.peak - SB_LO
        return stats


def make_prog(ntp, groups, depth=2, branches=("ret", "ssd", "hg", "s5"), ffn=True):
    p1 = Prog(ntp, groups, depth=depth, branches=branches, ffn=ffn)
    p1.build(emit=False)
    p2 = Prog(ntp, groups, depth=depth, branches=branches, ffn=ffn, plan=list(p1.rec))
    stats = p2.build()
    return p2, stats


NTP = 16
GROUPS = [[0, 1, 2, 3], [4, 5, 6, 7], [8, 9, 10, 11], [12, 13, 14, 15], [16]]
NCORES = 8
DEPTH = 2


def kernel(**inp):
    inp = {k: np.asarray(v) for k, v in inp.items()}
    prog, _ = make_prog(NTP, GROUPS, depth=DEPTH, branches=("ret", "ssd", "hg", "s5"), ffn=True)
    L = DEPTH
    shared = {}
    for k in ("g_pre_mix", "g_post_mix", "g_pre_ffn", "g_post_ffn", "w_in", "ssd_conv_w", "ssd_conv_b", "ssd_a_log",
              "ssd_dt_bias", "ssd_d", "ssd_norm", "hg_lb_logits", "hg_norm", "s5_log_dt", "s5_w_glu", "w_branch",
              "w_out", "w_ff1", "w_ff2", "s5_c_re", "s5_c_im"):
        shared[k] = np.ascontiguousarray(inp[k], dtype=np.float32)
    shared["ret_gn"] = np.ascontiguousarray(inp["ret_gn"].reshape(L, 512))
    shared["s5_a_re"] = np.ascontiguousarray(inp["s5_a_re"].reshape(L, 2048))
    shared["s5_a_im"] = np.ascontiguousarray(inp["s5_a_im"].reshape(L, 2048))
    shared["s5_b_re"] = np.ascontiguousarray(inp["s5_b_re"].reshape(L, 2048, 16))
    shared["s5_b_im"] = np.ascontiguousarray(inp["s5_b_im"].reshape(L, 2048, 16))
    shared["s5_d"] = np.ascontiguousarray(inp["s5_d"].reshape(L, 512))
    for k, v in prog.consts.items():
        shared["c_" + k] = np.ascontiguousarray(v)
    maps = []
    for c in range(NCORES):
        sl = slice(c * NSEQ, (c + 1) * NSEQ)
        m = dict(shared)
        m["xin"] = np.ascontiguousarray(np.concatenate(
            [inp["x_prompt"][c], inp["x_sample"][sl].reshape(NSEQ * DSEQ, D)], 0), dtype=np.float32)
        m["st_ret"] = np.ascontiguousarray(inp["state_ret"][:, sl])
        m["st_hg"] = np.ascontiguousarray(inp["state_hgrn"][:, sl])
        m["st_ssd"] = np.ascontiguousarray(inp["state_ssd"][:, sl])
        m["st_conv"] = np.ascontiguousarray(inp["state_conv"][:, sl])
        m["st_s5re"] = np.ascontiguousarray(inp["state_s5_re"][:, sl].reshape(L, NSEQ, 2048))
        m["st_s5im"] = np.ascontiguousarray(inp["state_s5_im"][:, sl].reshape(L, NSEQ, 2048))
        maps.append(m)
    res = run_bass_kernel_spmd(prog.nc, maps, core_ids=list(range(NCORES)))
    R = res.results
    f32 = np.float32
    y_prompt = np.stack([R[c]["y"][:NTP * 128] for c in range(NCORES)], 0).astype(f32)
    y_sample = np.concatenate([R[c]["y"][NTP * 128:].reshape(NSEQ, DSEQ, D) for c in range(NCORES)], 0).astype(f32)

    def pstack(name, shape=None):
        a = np.stack([R[c][name] for c in range(NCORES)], 1)
        return a.reshape(shape).astype(f32) if shape is not None else a.astype(f32)

    def scat(name, shape=None):
        a = np.concatenate([R[c][name] for c in range(NCORES)], 1)
        return a.reshape(shape).astype(f32) if shape is not None else a.astype(f32)

    B, BS = NCORES, NCORES * NSEQ
    return (y_prompt, y_sample,
            pstack("ret_p"), scat("ret_s"),
            pstack("ssd_p"), scat("ssd_s"),
            pstack("conv_p"), scat("conv_s"),
            pstack("hg_p"), scat("hg_s"),
            pstack("s5re_p", (L, B, 32, 64)), scat("s5re_s", (L, BS, 32, 64)),
            pstack("s5im_p", (L, B, 32, 64)), scat("s5im_s", (L, BS, 32, 64)))
```
